# Optimizing a Trainium2 kernel written in Bass

```python
import math
import jax, jax.numpy as jnp
from jax import lax
import numpy as np

D_MODEL = 1024
BATCH = 1
SEQ = 16384
DEPTH = 1

CHUNK = 64
MEM_LEN = 256
HEAD_DIM = 64
POOL_WINDOWS = (2, 4, 8, 16)
POOL_GROUPS = len(POOL_WINDOWS)
POOL_GROUP_DIM = 64
POOL_WIDTH = POOL_GROUPS * POOL_GROUP_DIM
ATTN_HEADS = 8
ATTN_WIDTH = ATTN_HEADS * HEAD_DIM
BAND_CHUNKS = 9
BAND_KEYS = BAND_CHUNKS * CHUNK
MAX_REL = 128
MEM_HEADS = 4
MEM_WIDTH = MEM_HEADS * HEAD_DIM
N_BRANCH = 3
IN_COLS = POOL_WIDTH + 3 * ATTN_WIDTH + MEM_WIDTH + N_BRANCH * D_MODEL
D_FF = 2816
CONV_WIDTH = 3
RMS_EPS = 1e-6
NEG_INF = -1e30

kernel_name = "hybrid_pool_chunkattn_mem_convffn"


def rms_norm(x, g):
    xf = x.astype(jnp.float32)
    y = xf * lax.rsqrt(jnp.mean(xf * xf, axis=-1, keepdims=True) + RMS_EPS)
    return (y * g.astype(jnp.float32)).astype(x.dtype)


def pool_mixer(u, w_pool, pool_scale):
    B, S, _ = u.shape
    ug = u.reshape(B, S, POOL_GROUPS, POOL_GROUP_DIM)
    pos1 = jnp.arange(1, S + 1, dtype=jnp.int32)
    outs = []
    for g, w in enumerate(POOL_WINDOWS):
        ui = ug[:, :, g].astype(jnp.float32)
        c = jnp.cumsum(ui, axis=1)
        c_prev = jnp.pad(c, ((0, 0), (w, 0), (0, 0)))[:, :S]
        cnt = jnp.minimum(pos1, w).astype(jnp.float32)[None, :, None]
        outs.append((c - c_prev) / cnt - ui)
    p = jnp.stack(outs, axis=2).astype(u.dtype)
    y = jnp.einsum('bsgc,gcd->bsgd', p, w_pool)
    return y.reshape(B, S, POOL_WIDTH) * pool_scale


def chunk_band_attention(q, k, v, rel_bias):
    B, S, _ = q.shape
    NC = S // CHUNK
    shp = (B, NC, CHUNK, ATTN_HEADS, HEAD_DIM)
    q, k, v = q.reshape(shp), k.reshape(shp), v.reshape(shp)
    pad = ((0, 0), (BAND_CHUNKS - 1, 0), (0, 0), (0, 0), (0, 0))
    kp, vp = jnp.pad(k, pad), jnp.pad(v, pad)
    kb = jnp.stack([kp[:, j:j + NC] for j in range(BAND_CHUNKS)], axis=2).reshape(B, NC, BAND_KEYS, ATTN_HEADS, HEAD_DIM)
    vb = jnp.stack([vp[:, j:j + NC] for j in range(BAND_CHUNKS)], axis=2).reshape(B, NC, BAND_KEYS, ATTN_HEADS, HEAD_DIM)
    s = jnp.einsum('bnqhd,bnkhd->bnhqk', q, kb, preferred_element_type=jnp.float32) * (HEAD_DIM ** -0.5)
    qi = jnp.arange(CHUNK)
    kj = jnp.arange(BAND_KEYS)
    rel = qi[:, None] + (BAND_CHUNKS - 1) * CHUNK - kj[None, :]
    idx = jnp.clip(rel, -MAX_REL, MAX_REL) + MAX_REL
    bias = rel_bias[:, idx].astype(jnp.float32)
    s = s + bias[None, None]
    valid = (jnp.arange(NC)[:, None] - (BAND_CHUNKS - 1) + kj[None, :] // CHUNK) >= 0
    s = jnp.where(valid[None, :, None, None, :], s, NEG_INF)
    p = jax.nn.softmax(s, axis=-1).astype(v.dtype)
    o = jnp.einsum('bnhqk,bnkhd->bnqhd', p, vb)
    return o.reshape(B, S, ATTN_WIDTH)


def memory_attention(q, mem_n, w_mem_kv):
    B, S, _ = q.shape
    M = mem_n.shape[1]
    kv = (mem_n @ w_mem_kv).reshape(B, M, 2, MEM_HEADS, HEAD_DIM)
    km, vm = kv[:, :, 0], kv[:, :, 1]
    qh = q.reshape(B, S, MEM_HEADS, HEAD_DIM)
    s = jnp.einsum('bshd,bmhd->bhsm', qh, km, preferred_element_type=jnp.float32) * (HEAD_DIM ** -0.5)
    p = jax.nn.softmax(s, axis=-1).astype(vm.dtype)
    o = jnp.einsum('bhsm,bmhd->bshd', p, vm)
    return o.reshape(B, S, MEM_WIDTH)


def conv_gated_mlp(h, w_up, conv_w, conv_b, w_down):
    S = h.shape[1]
    a = h @ w_up
    ap = jnp.pad(a, ((0, 0), (CONV_WIDTH - 1, 0), (0, 0)))
    c = conv_b + sum(ap[:, t:t + S] * conv_w[t] for t in range(CONV_WIDTH))
    gate, val = c[..., :D_FF], c[..., D_FF:]
    return (jax.nn.gelu(gate, approximate=False) * val) @ w_down


def setup_inputs(seed: int = 0) -> dict:
    key = jax.random.key(seed)
    ks = jax.random.split(key, 24)
    f32 = jnp.float32
    nrm = lambda k, shape, fan_in: jax.random.normal(k, shape, f32) * (fan_in ** -0.5)
    gain = lambda k, shape: 1.0 + 0.05 * jax.random.normal(k, shape, f32)
    L = DEPTH
    return {
        "x": jax.random.normal(ks[0], (BATCH, SEQ, D_MODEL), f32),
        "mem": jax.random.normal(ks[1], (BATCH, MEM_LEN, D_MODEL), f32),
        "norm_mix_g": gain(ks[2], (L, D_MODEL)),
        "norm_mem_g": gain(ks[3], (L, D_MODEL)),
        "w_in": nrm(ks[4], (L, D_MODEL, IN_COLS), D_MODEL),
        "b_gate": 0.02 * jax.random.normal(ks[5], (L, N_BRANCH * D_MODEL), f32),
        "w_pool": nrm(ks[6], (L, POOL_GROUPS, POOL_GROUP_DIM, POOL_GROUP_DIM), POOL_GROUP_DIM),
        "pool_scale": gain(ks[7], (L, POOL_WIDTH)),
        "rel_bias": 0.5 * jax.random.normal(ks[8], (L, ATTN_HEADS, 2 * MAX_REL + 1), f32),
        "w_mem_kv": nrm(ks[9], (L, D_MODEL, 2 * MEM_WIDTH), D_MODEL),
        "w_up_pool": nrm(ks[10], (L, POOL_WIDTH, D_MODEL), POOL_WIDTH),
        "w_up_attn": nrm(ks[11], (L, ATTN_WIDTH, D_MODEL), ATTN_WIDTH),
        "w_up_mem": nrm(ks[12], (L, MEM_WIDTH, D_MODEL), MEM_WIDTH),
        "w_out": nrm(ks[13], (L, D_MODEL, D_MODEL), D_MODEL),
        "norm_ffn_g": gain(ks[14], (L, D_MODEL)),
        "w_ffn_up": nrm(ks[15], (L, D_MODEL, 2 * D_FF), D_MODEL),
        "conv_w": nrm(ks[16], (L, CONV_WIDTH, 2 * D_FF), CONV_WIDTH),
        "conv_b": 0.02 * jax.random.normal(ks[17], (L, 2 * D_FF), f32),
        "w_ffn_down": nrm(ks[18], (L, D_FF, D_MODEL), D_FF),
        "norm_final_g": gain(ks[19], (D_MODEL,)),
    }


def reference(x, mem, norm_mix_g, norm_mem_g, w_in, b_gate, w_pool, pool_scale, rel_bias,
              w_mem_kv, w_up_pool, w_up_attn, w_up_mem, w_out, norm_ffn_g, w_ffn_up,
              conv_w, conv_b, w_ffn_down, norm_final_g):
    B, S, D = x.shape
    o1 = POOL_WIDTH
    o2 = o1 + ATTN_WIDTH
    o3 = o2 + ATTN_WIDTH
    o4 = o3 + ATTN_WIDTH
    o5 = o4 + MEM_WIDTH
    for l in range(DEPTH):
        h = rms_norm(x, norm_mix_g[l])
        z = h @ w_in[l]
        u_pool = z[..., :o1]
        q_a, k_a, v_a = z[..., o1:o2], z[..., o2:o3], z[..., o3:o4]
        q_m = z[..., o4:o5]
        gates = jax.nn.sigmoid(z[..., o5:] + b_gate[l]).reshape(B, S, N_BRANCH, D)

        y_pool = pool_mixer(u_pool, w_pool[l], pool_scale[l]) @ w_up_pool[l]
        y_attn = chunk_band_attention(q_a, k_a, v_a, rel_bias[l]) @ w_up_attn[l]
        mem_n = rms_norm(mem, norm_mem_g[l])
        y_mem = memory_attention(q_m, mem_n, w_mem_kv[l]) @ w_up_mem[l]

        merged = gates[:, :, 0] * y_pool + gates[:, :, 1] * y_attn + gates[:, :, 2] * y_mem
        x = x + merged @ w_out[l]

        h2 = rms_norm(x, norm_ffn_g[l])
        x = x + conv_gated_mlp(h2, w_ffn_up[l], conv_w[l], conv_b[l], w_ffn_down[l])
    return rms_norm(x, norm_final_g)
```

```python
import numpy as np
import concourse.bass as bass
import concourse.mybir as mybir
import concourse.bass_utils as bu

F32 = mybir.dt.float32
BF16 = mybir.dt.bfloat16
AF = mybir.ActivationFunctionType
ALU = mybir.AluOpType

NCORES = 8
SEQ = 16384
D = 1024
OWN = SEQ // NCORES
HALO = 640
WIN = OWN + HALO
NT = WIN // 128
Q0 = 4
NQ = NT - Q0
NQTOK = NQ * 128
DFF = 2816
EPS = 1e-6
NEG = -30000.0
NDMA_SEM = {"sp": 36, "pool": 20}


class Op:
    __slots__ = ("eng", "fn", "deps", "idx", "dma", "signal", "sig", "sem", "tag")


class Sched:
    ENG = ("pe", "act", "dve", "pool", "sp")

    def __init__(self, nc):
        self.nc = nc
        self.q = {e: [] for e in self.ENG}
        self.lw = {}
        self.rd = {}
        self.base = {}
        self.names = {}
        self.dma_list = []
        self.dma_q = {}

    def add(self, eng, fn, reads=(), writes=(), deps=(), dma=False, tag=None):
        o = Op()
        o.eng = eng; o.fn = fn; o.dma = dma; o.signal = False; o.sig = 0; o.sem = None; o.tag = tag
        o.idx = len(self.q[eng])
        d = {}
        for x in deps:
            d[x] = True
        for k in reads:
            self.names.setdefault(k[0], set()).add(k)
            w = self.lw.get(k)
            if w is not None:
                d[w] = True
            for x in self.base.get(k[0], ()):
                d.setdefault(x, False)
        for k in writes:
            self.names.setdefault(k[0], set()).add(k)
            w = self.lw.get(k)
            if w is not None:
                d.setdefault(w, False)
            for r in self.rd.get(k, {}).values():
                d.setdefault(r, False)
            for x in self.base.get(k[0], ()):
                d.setdefault(x, False)
        for k in reads:
            slot = self.rd.setdefault(k, {})
            slot[("dma", len(self.dma_list)) if dma else eng] = o
        for k in writes:
            self.lw[k] = o
            self.rd[k] = {}
        if dma:
            lst = self.dma_q.setdefault(eng, [])
            n = len(lst)
            R = NDMA_SEM[eng]
            if n >= R:
                d[lst[n - R]] = True
            o.sem = (eng, n % R)
            o.sig = 16 * (n // R + 1)
            lst.append(o)
            self.dma_list.append(o)
        d.pop(o, None)
        o.deps = d
        self.q[eng].append(o)
        return o

    def frontier(self, name):
        best = {}
        out = []
        for k in self.names.get(name, ()):
            cands = []
            w = self.lw.get(k)
            if w is not None:
                cands.append(w)
            cands.extend(self.rd.get(k, {}).values())
            for o in cands:
                if o.dma:
                    out.append(o)
                else:
                    b = best.get(o.eng)
                    if b is None or o.idx > b.idx:
                        best[o.eng] = o
        out.extend(best.values())
        out.extend(self.base.get(name, ()))
        return out

    STRICT = True

    @staticmethod
    def _needs_wait(o, d, raw):
        if d.dma or o.dma:
            return True
        if d.eng != o.eng:
            return True
        if o.eng == "pe":
            return False
        return raw or Sched.STRICT

    def emit(self):
        nc = self.nc
        for e in self.ENG:
            for o in self.q[e]:
                for d, raw in o.deps.items():
                    if self._needs_wait(o, d, raw):
                        d.signal = True
        for e in self.ENG:
            c = 0
            for o in self.q[e]:
                if not o.dma and o.signal:
                    c += 1
                    o.sig = c
        from contextlib import ExitStack
        with ExitStack() as st:
            esem = {e: st.enter_context(nc.semaphore("s_" + e)) for e in self.ENG}
            dsem = {(q_, i): st.enter_context(nc.semaphore("d%s%d" % (q_, i))) for q_, r_ in NDMA_SEM.items() for i in range(r_)}
            block = st.enter_context(nc.Block())

            def run(e, eng):
                waited = {}
                for o in self.q[e]:
                    for d, raw in o.deps.items():
                        if not self._needs_wait(o, d, raw):
                            continue
                        sem = dsem[d.sem] if d.dma else esem[d.eng]
                        key = ("d", d.sem) if d.dma else d.eng
                        if waited.get(key, 0) >= d.sig:
                            continue
                        eng.wait_ge(sem, d.sig)
                        waited[key] = d.sig
                    ins = o.fn(eng)
                    if o.dma:
                        ins.then_inc(dsem[o.sem], 16)
                    elif o.signal:
                        ins.then_inc(esem[e], 1)

            @block.tensor
            def _(eng):
                run("pe", eng)

            @block.scalar
            def _(eng):
                run("act", eng)

            @block.vector
            def _(eng):
                run("dve", eng)

            @block.gpsimd
            def _(eng):
                run("pool", eng)

            @block.sync
            def _(eng):
                run("sp", eng)


class Mem:
    def __init__(self, nc, sched, lo=16512, hi=229376 - 128):
        self.nc = nc; self.S = sched
        self.free = [[lo, hi, ()]]
        self.live = {}
        self.uid = 0

    def alloc(self, name, shape, dtype, top=False):
        nbytes = int(np.prod(shape[1:])) * mybir.dt.size(dtype)
        nbytes = (nbytes + 63) // 64 * 64
        order = range(len(self.free) - 1, -1, -1) if top else range(len(self.free))
        for i in order:
            s, e, fr = self.free[i]
            if e - s >= nbytes:
                if top:
                    a = e - nbytes
                    self.free[i] = [s, a, fr]
                else:
                    a = s
                    self.free[i] = [s + nbytes, e, fr]
                if self.free[i][0] == self.free[i][1]:
                    del self.free[i]
                self.uid += 1
                t = self.nc.alloc_sbuf_tensor_at("%s_%d" % (name, self.uid), list(shape), dtype, offset=a)
                self.live[name] = (a, a + nbytes)
                if fr:
                    self.S.base[name] = tuple(fr)
                return t
        raise RuntimeError("SBUF OOM for %s (%d B); free=%s" % (name, nbytes, [(s, e) for s, e, _ in self.free]))

    def release(self, name):
        a, b = self.live.pop(name)
        fr = tuple(self.S.frontier(name))
        self.free.append([a, b, fr])
        self.free.sort(key=lambda r: r[0])
        merged = []
        for r in self.free:
            if merged and merged[-1][1] == r[0]:
                merged[-1] = [merged[-1][0], r[1], tuple(set(merged[-1][2]) | set(r[2]))]
            else:
                merged.append(r)
        self.free = merged
        for k in self.S.names.pop(name, ()):
            self.S.lw.pop(k, None)
            self.S.rd.pop(k, None)
        self.S.base.pop(name, None)


def build(debug=None):
    nc = bass.Bass("TRN2", target_bir_lowering=False)
    S = Sched(nc)
    M = Mem(nc, S)

    def din(name, shape):
        return nc.dram_tensor(name, list(shape), F32, kind="ExternalInput").ap()

    xw = din("xw", [WIN, D])
    memd = din("mem", [256, D])
    w_in = din("w_in", [D, 5120])
    bgate_d = din("bgate", [128, 24])
    wpool_d = din("w_pool", [4, 64, 64])
    pscale_d = din("pscale", [128, 2])
    biasT_d = din("biasT", [128, 8 * 385])
    w_mkv = din("w_mem_kv", [D, 512])
    w_upp_d = din("w_up_pool", [256, D])
    w_upa_d = din("w_up_attn", [512, D])
    w_upm_d = din("w_up_mem", [256, D])
    w_out_d = din("w_out", [D, D])
    w_fu_d = din("w_ffn_up", [D, 2 * DFF])
    convw_d = din("convw", [128, 44 * 3])
    convb_d = din("convb", [128, 44])
    w_fd_d = din("w_ffn_down", [DFF, D])
    gmix_d = din("g_mix_b", [128, D])
    gmem_d = din("g_mem_b", [128, D])
    gffn_d = din("g_ffn_b", [128, D])
    gfin_d = din("g_fin_b", [128, D])
    valid_d = din("valid", [128, NT])
    flag_d = din("flag", [128, 1])
    invcnt_d = din("invcnt", [128, 32])
    ident_d = din("ident", [128, 128])
    out_d = nc.dram_tensor("out", [OWN, D], F32, kind="ExternalOutput").ap()
    xmid_d = nc.dram_tensor("xmid_scr", [OWN, D], F32, kind="Internal").ap()
    wfu_bf = nc.dram_tensor("wfu_bf16", [D, 2 * DFF], BF16, kind="Internal").ap()
    wfd_bf = nc.dram_tensor("wfd_bf16", [DFF, D], BF16, kind="Internal").ap()
    dbg_d = {}
    if debug:
        for nm, shp in debug.items():
            if nm.startswith("_"):
                continue
            dbg_d[nm] = nc.dram_tensor("dbg_" + nm, list(shp), F32, kind="ExternalOutput").ap()

    from contextlib import ExitStack
    ps_state = {"stack": None, "names": [], "frontier": ()}

    def ps_open():
        ps_state["stack"] = ExitStack()
        ps_state["names"] = []

    def ps_alloc(name, shape, dtype=F32):
        t = ps_state["stack"].enter_context(nc.psum_tensor(name, list(shape), dtype))
        ps_state["names"].append(name)
        if ps_state["frontier"]:
            S.base[name] = tuple(ps_state["frontier"])
        return t

    def ps_close():
        fr = []
        for nm in ps_state["names"]:
            fr.extend(S.frontier(nm))
        ps_state["frontier"] = tuple(set(fr))
        ps_state["stack"].close()

    def dma(eng, out_ap, in_ap, reads=(), writes=(), tag=None):
        return S.add(eng, lambda e: e.dma_start(out=out_ap, in_=in_ap), reads=reads, writes=writes, dma=True, tag=tag)

    rr = {"evac": 0}

    def evac_copy(out_ap, in_ap, reads, writes, eng=None):
        if eng is None:
            eng = ("act", "dve")[rr["evac"] % 2]
            rr["evac"] += 1
        if eng == "act":
            return S.add("act", lambda e: e.activation(out=out_ap, in_=in_ap, func=AF.Copy), reads=reads, writes=writes)
        return S.add(eng, lambda e: e.tensor_copy(out=out_ap, in_=in_ap), reads=reads, writes=writes)

    ident = M.alloc("ident", [128, 128], BF16)
    id32 = M.alloc("id32", [128, 128], F32)
    def _ident_load():
        dma("sp", id32[:], ident_d, writes=[("id32",)])
        S.add("dve", lambda e: e.tensor_copy(out=ident[:], in_=id32[:]), reads=[("id32",)], writes=[("ident",)])
    const_dmas = [_ident_load]
    g_a = M.alloc("g_a", [128, D], F32)
    g_b = M.alloc("g_b", [128, D], F32)
    const_dmas.insert(0, lambda: dma("sp", g_a[:], gmix_d, writes=[("g_a",)]))
    const_dmas.append(lambda: dma("sp", g_b[:], gmem_d, writes=[("g_b",)]))
    valid = M.alloc("valid", [128, NT], F32)
    const_dmas.append(lambda: dma("sp", valid[:], valid_d, writes=[("valid",)]))
    flag = M.alloc("flag", [128, 1], F32)
    const_dmas.append(lambda: dma("sp", flag[:], flag_d, writes=[("flag",)]))
    ss = M.alloc("ss", [128, 8], F32)
    rstd_x = M.alloc("rstd_x", [128, NT], F32)
    rstd_m = M.alloc("rstd_m", [128, 16], F32)
    mhalf = M.alloc("mhalf", [128, 1], F32)
    S.add("pool", lambda e: e.memset(mhalf[:], -0.5), writes=[("mhalf",)])
    S.add("act", lambda e: e.activation(out=junk[:, 0:1], in_=mhalf[:, 0:1], func=AF.Square), reads=[("mhalf",)], writes=[("junk",)])
    junk = M.alloc("junk", [128, D], BF16)
    xs = M.alloc("xs", [128, 5, D], F32)
    xs_glob = xs
    HB_N = 3
    hb = M.alloc("hb", [128, HB_N, D], BF16)
    h2halo = M.alloc("h2halo", [128, 8, 2], BF16)
    convw = M.alloc("convw", [128, 44, 3], F32)
    convb = M.alloc("convb", [128, 44], F32)
    bias8 = M.alloc("bias8", [128, 8, 385], F32)
    XS_N = 5
    ctr = {"xs": 0, "hb": 0, "ss": 0, "psT": 0, "pj": 0}

    w_a = M.alloc("w_a", [128, 8, 2048], BF16)
    kT = M.alloc("kT", [128, 4, WIN], BF16, top=True)
    Vb = M.alloc("V", [128, NT, 8, 65], BF16, top=True)
    qT = M.alloc("qT", [128, 4, NQTOK], BF16, top=True)
    qmT = M.alloc("qmT", [128, 2, NQTOK], BF16, top=True)
    ypT = M.alloc("ypT", [128, 2, NQTOK], BF16, top=True)
    kmT = M.alloc("kmT", [128, 2, 256], BF16, top=True)
    vm = M.alloc("vm", [128, 2, 4, 65], BF16, top=True)
    hT = M.alloc("hT", [128, 2, 8, 512], BF16)
    w_mk = M.alloc("w_mk", [128, 8, 512], BF16)
    wpbd = M.alloc("wpbd", [128, 2, 128], BF16)
    pscale = M.alloc("pscale", [128, 2], F32)
    invcnt = M.alloc("invcnt", [128, 2, 16], F32)
    ub = M.alloc("ub", [128, 2, 16 + 512], F32)
    sA = M.alloc("sA", [128, 2, 16 + 512], F32)
    sB = M.alloc("sB", [128, 2, 16 + 512], F32)
    pT = M.alloc("pT", [128, 2, 512], BF16)
    ptmp = M.alloc("ptmp", [128, 16], F32)

    w_in_v = w_in.rearrange("(kc p) c -> p kc c", p=128)
    for c0 in (0, 512, 1024, 1536):
        dma("pool", w_a[:, :, c0:c0 + 512], w_in_v[:, :, c0:c0 + 512],
            writes=[("w_a", c0 // 256), ("w_a", c0 // 256 + 1)])
    wp32 = M.alloc("wp32", [128, 2, 128], F32)
    wpbd_keys = [("wpbd",)]
    S.add("pool", lambda e: e.memset(wp32[:], 0.0), writes=[("wp32",)])

    def late_setup():
        dma("sp", bias8[:], biasT_d.rearrange("p (h c) -> p h c", h=8), writes=[("bias8",)])
        for h in range(8):
            S.add("dve", lambda e, h=h: e.tensor_scalar(
                out=bias8[:, h, 0:384], in0=bias8[:, h, 0:384], scalar1=bias8[:, h, 384:385], scalar2=8.0,
                op0=ALU.subtract, op1=ALU.mult),
                reads=[("bias8",)], writes=[("bias8",)])
        S.add("pool", lambda e: e.memset(bias8[0:64, :, 64:128], 8.0 * NEG), reads=[("bias8",)], writes=[("bias8",)])
        S.add("pool", lambda e: e.memset(bias8[64:128, :, 256:320], 8.0 * NEG), reads=[("bias8",)], writes=[("bias8",)])
        dma("sp", convw[:], convw_d.rearrange("p (j t) -> p j t", t=3), writes=[("convw",)])
        dma("sp", convb[:], convb_d, writes=[("convb",)])
        dma("pool", w_mk[:], w_mkv.rearrange("(kc p) c -> p kc c", p=128), writes=[("w_mk",)])
        dma("sp", pscale[:], pscale_d, writes=[("pscale",)])
        dma("sp", invcnt[:], invcnt_d.rearrange("p (t c) -> p t c", t=2), writes=[("invcnt",)])
        for g in range(4):
            t, hlf = g // 2, g % 2
            dma("sp", wp32[hlf * 64:(hlf + 1) * 64, t, hlf * 64:(hlf + 1) * 64], wpool_d[g], reads=[("wp32",)],
                writes=[("wp32", g)])
        S.add("dve", lambda e: e.tensor_copy(out=wpbd[:], in_=wp32[:]),
              reads=[("wp32", g) for g in range(4)], writes=[("wpbd",)])
        for h in range(8):
            S.add("pool", lambda e, h=h: e.tensor_copy(out=Vb[:, :, h, 64:65], in_=valid[:, :].unsqueeze(2)),
                  reads=[("valid",)], writes=[("Vones", h)])
        S.add("pool", lambda e: e.memset(vm[:, :, :, 64:65], 1.0), writes=[("vmones",)])
        S.add("pool", lambda e: e.memset(ub[:], 0.0), writes=[("ub",)])

    ps_open()
    psT = [ps_alloc("psT0", [128, 8, 128], BF16), ps_alloc("psT1", [128, 8, 128], BF16)]
    pj = [ps_alloc("pj%d" % i, [128, 512], F32) for i in range(4)]

    def pskey(kind, i):
        return ("%s%d" % (kind, i),)

    def norm_stages(load_fn, gname, gtile, xbuf, slot_fn, ps_fn, dst_fn, dst_key, save=None, reuse=None):
        xs, xname = xbuf
        st = {}

        def n0():
            st["s"] = slot_fn()
            if load_fn is not None:
                load_fn(st["s"])

        def n1():
            if reuse is not None:
                return
            s = st["s"]
            c = ctr["ss"] % 8; ctr["ss"] += 1
            st["c"] = c
            S.add("act", lambda e: e.activation(out=junk[:], in_=xs[:, s, :], func=AF.Square, scale=1.0 / 32.0,
                                                accum_out=ss[:, c:c + 1]),
                  reads=[(xname, s)], writes=[("junk",), ("ss", c)])
            S.add("pool", lambda e: e.tensor_scalar(out=ss[:, c:c + 1], in0=ss[:, c:c + 1], scalar1=EPS, scalar2=None,
                                                    op0=ALU.add),
                  reads=[("ss", c)], writes=[("ss", c)])
            if save is not None:
                sv_ap, sv_key = save
                S.add("pool", lambda e: e.tensor_tensor(out=sv_ap, in0=ss[:, c:c + 1], in1=mhalf[:, 0:1], op=ALU.pow),
                      reads=[("ss", c), ("mhalf",)], writes=[sv_key])
            else:
                S.add("pool", lambda e: e.tensor_tensor(out=ss[:, c:c + 1], in0=ss[:, c:c + 1], in1=mhalf[:, 0:1], op=ALU.pow),
                      reads=[("ss", c), ("mhalf",)], writes=[("ss", c)])

        def n2():
            s = st["s"]
            if reuse is not None:
                sc_ap, sc_key = reuse
            elif save is not None:
                sc_ap, sc_key = save
            else:
                c = st["c"]
                sc_ap, sc_key = ss[:, c:c + 1], ("ss", c)
            hs = ctr["hb"] % HB_N; ctr["hb"] += 1
            st["hs"] = hs
            S.add("dve", lambda e: e.scalar_tensor_tensor(out=hb[:, hs, :], in0=xs[:, s, :], scalar=sc_ap,
                                                          in1=gtile[:], op0=ALU.mult, op1=ALU.mult),
                  reads=[(xname, s), sc_key, (gname,)], writes=[("hb", hs)])

        def n3():
            hs = st["hs"]
            ps_ap, ps_key = ps_fn()

            def tr(e):
                ins = None
                for kc in range(8):
                    ins = e.transpose(ps_ap[:, kc, :], hb[:, hs, kc * 128:(kc + 1) * 128], ident[:])
                return ins
            S.add("pe", tr, reads=[("hb", hs), ("ident",)], writes=[ps_key])
            dst_fn(ps_ap, ps_key, dst_key)
        return [n0, n1, n2, n3], st

    def wavefront(stage_lists):
        out = []
        nt_ = len(stage_lists)
        ns_ = max(len(s_) for s_ in stage_lists) if stage_lists else 0
        for d_ in range(nt_ + ns_ - 1):
            for k_ in range(ns_):
                t_ = d_ - k_
                if 0 <= t_ < nt_ and k_ < len(stage_lists[t_]):
                    out.append(stage_lists[t_][k_])
        return out

    def norm_s1(src_rows_ap, gname, gtile, loaded=None, xbuf=None):
        xs, xname = xbuf if xbuf is not None else (xs_glob, "xs")
        if loaded is None:
            s = ctr["xs"] % XS_N; ctr["xs"] += 1
            dma("sp", xs[:, s, :], src_rows_ap, writes=[(xname, s)])
        else:
            s = loaded
        c = ctr["ss"] % 8; ctr["ss"] += 1
        S.add("act", lambda e: e.activation(out=junk[:], in_=xs[:, s, :], func=AF.Square, scale=1.0 / 32.0,
                                            accum_out=ss[:, c:c + 1]),
              reads=[(xname, s)], writes=[("junk",), ("ss", c)])
        S.add("pool", lambda e: e.tensor_scalar(out=ss[:, c:c + 1], in0=ss[:, c:c + 1], scalar1=EPS, scalar2=None,
                                                op0=ALU.add),
              reads=[("ss", c)], writes=[("ss", c)])
        S.add("pool", lambda e: e.tensor_tensor(out=ss[:, c:c + 1], in0=ss[:, c:c + 1], in1=mhalf[:, 0:1], op=ALU.pow),
              reads=[("ss", c), ("mhalf",)], writes=[("ss", c)])
        hs = ctr["hb"] % HB_N; ctr["hb"] += 1
        S.add("dve", lambda e: e.scalar_tensor_tensor(out=hb[:, hs, :], in0=xs[:, s, :], scalar=ss[:, c:c + 1],
                                                      in1=gtile[:], op0=ALU.mult, op1=ALU.mult),
              reads=[(xname, s), ("ss", c), (gname,)], writes=[("hb", hs)])
        return (s, hs)

    def norm_s2(ctx, ps_ap, ps_key, dst_fn, dst_key):
        s, hs = ctx

        def tr(e):
            ins = None
            for kc in range(8):
                ins = e.transpose(ps_ap[:, kc, :], hb[:, hs, kc * 128:(kc + 1) * 128], ident[:])
            return ins
        S.add("pe", tr, reads=[("hb", hs), ("ident",)], writes=[ps_key])
        dst_fn(ps_ap, ps_key, dst_key)

    def norm_tile(src_rows_ap, gname, gtile, psT_, dst_fn, dst_key, src_is_loaded=None, pname="psT", xbuf=None):
        ctx = norm_s1(src_rows_ap, gname, gtile, loaded=src_is_loaded, xbuf=xbuf)
        pt = ctr["psT"] % 2; ctr["psT"] += 1
        norm_s2(ctx, psT_[pt], pskey(pname, pt), dst_fn, dst_key)
        return ctx[0]

    def mem_path():
        memT = M.alloc("memT", [128, 8, 256], BF16)
        for mt in range(2):
            def dst(ps, pskey_, dkey, mt=mt):
                evac_copy(memT[:, :, mt * 128:(mt + 1) * 128], ps[:], [pskey_], [dkey])
            norm_tile(memd[mt * 128:(mt + 1) * 128, :], "g_b", g_b, psT, dst, ("memT", mt))
        for pr in range(2):
            pi = ctr["pj"] % 4; ctr["pj"] += 1

            def mm(e, pr=pr, pi=pi):
                ins = None
                for kc in range(8):
                    ins = e.matmul(pj[pi][:, 0:256], lhsT=w_mk[:, kc, pr * 128:(pr + 1) * 128], rhs=memT[:, kc, :],
                                   start=(kc == 0), stop=(kc == 7))
                return ins
            S.add("pe", mm, reads=[("w_mk",), ("memT", 0), ("memT", 1)], writes=[pskey("pj", pi)])
            evac_copy(kmT[:, pr, :], pj[pi][:, 0:256], [pskey("pj", pi)], [("kmT", pr)])
        for mt in range(2):
            pi = ctr["pj"] % 4; ctr["pj"] += 1

            def mm(e, mt=mt, pi=pi):
                ins = None
                for kc in range(8):
                    ins = e.matmul(pj[pi][:, 0:256], lhsT=memT[:, kc, mt * 128:(mt + 1) * 128], rhs=w_mk[:, kc, 256:512],
                                   start=(kc == 0), stop=(kc == 7))
                return ins
            S.add("pe", mm, reads=[("w_mk",), ("memT", mt)], writes=[pskey("pj", pi)])
            evac_copy(vm[:, mt, :, 0:64], pj[pi][:, 0:256].rearrange("p (h d) -> p h d", d=64),
                      [pskey("pj", pi)], [("vm", mt)])
        dma("sp", g_b[:], gffn_d, reads=[], writes=[("g_b",)])


    groups = [[0, 1, 2, 3], [4, 5, 6, 7], [8, 9, 10, 11], [12, 13, 14, 15], [16, 17, 18, 19], [20]]
    groups_C = [[5, 6, 7, 8], [4], [9, 10, 11, 12], [13, 14, 15, 16], [17, 18, 19, 20]]

    def tile_T_stages(i, gb, j):
        def slot_fn():
            s = ctr["xs"] % XS_N; ctr["xs"] += 1
            return s

        def load_fn(s):
            dma("sp", xs[:, s, :], xw[i * 128:(i + 1) * 128, :], writes=[("xs", s)])

        def ps_fn():
            pt = ctr["psT"] % 2; ctr["psT"] += 1
            return psT[pt], pskey("psT", pt)

        def dst(ps, pskey_, dkey):
            evac_copy(hT[:, gb, :, j * 128:(j + 1) * 128], ps[:], [pskey_], [dkey], eng="dve")
        stages, _ = norm_stages(load_fn, "g_a", g_a, (xs, "xs"), slot_fn, ps_fn, dst, ("hT", gb, j),
                                save=(rstd_x[:, i:i + 1], ("rstd_x", i)))
        return stages

    def staged_order(stage_pairs, ahead=2):
        out = []
        n_ = len(stage_pairs)
        for k_ in range(n_ + ahead):
            if k_ < n_:
                out.append(stage_pairs[k_][0])
            if k_ - ahead >= 0:
                out.append(stage_pairs[k_ - ahead][1])
        return out

    def proj_fm(col0, n, gb, ntiles, wkey, evac):
        pi = ctr["pj"] % 4; ctr["pj"] += 1

        def mm(e):
            ins = None
            for kc in range(8):
                ins = e.matmul(pj[pi][:, 0:n], lhsT=w_a[:, kc, col0:col0 + 128], rhs=hT[:, gb, kc, 0:n],
                               start=(kc == 0), stop=(kc == 7))
            return ins
        S.add("pe", mm, reads=[wkey] + [("hT", gb, j) for j in range(ntiles)], writes=[pskey("pj", pi)])
        evac(pj[pi], pskey("pj", pi))

    deferred = []
    pre_B = {}

    def early_B():
        M.release("w_a")

    def group_items(gi, tiles):
        gb = gi % 2
        n = 128 * len(tiles)
        ntl = len(tiles)
        tok0 = tiles[0] * 128
        isq = gi >= 1
        q0 = (tiles[0] - Q0) * 128
        L = 16 + n
        items = []

        def u_item(pr):
            if isq:
                proj_fm(pr * 128, n, gb, ntl, ("w_a", 0),
                        lambda ps, pk: evac_copy(ub[:, pr, 16:16 + n], ps[:, 0:n], [pk], [("ub",)], eng="act"))
            else:
                proj_fm(pr * 128, n, gb, ntl, ("w_a", 0),
                        lambda ps, pk: evac_copy(ub[:, pr, 0:16], ps[:, n - 16:n], [pk], [("ub",)], eng="act"))

        def k_item(pr):
            c0 = 768 + pr * 128
            proj_fm(c0, n, gb, ntl, ("w_a", c0 // 256),
                    lambda ps, pk: evac_copy(kT[:, pr, tok0:tok0 + n], ps[:, 0:n], [pk], [("kT", i) for i in tiles], eng="act"))

        def v_item(j, i):
            pi = ctr["pj"] % 4; ctr["pj"] += 1

            def mm(e):
                ins = None
                for kc in range(8):
                    ins = e.matmul(pj[pi][:, 0:512], lhsT=hT[:, gb, kc, j * 128:(j + 1) * 128],
                                   rhs=w_a[:, kc, 1280:1792], start=(kc == 0), stop=(kc == 7))
                return ins
            S.add("pe", mm, reads=[("w_a", 5), ("w_a", 6), ("hT", gb, j)], writes=[pskey("pj", pi)])
            evac_copy(Vb[:, i, :, 0:64], pj[pi][:, 0:512].rearrange("p (h d) -> p h d", d=64),
                      [pskey("pj", pi)], [("V", i)], eng="act")

        def q_item(pr):
            c0 = 256 + pr * 128
            proj_fm(c0, n, gb, ntl, ("w_a", c0 // 256),
                    lambda ps, pk: evac_copy(qT[:, pr, q0:q0 + n], ps[:, 0:n], [pk], [("qT", i) for i in tiles], eng="act"))

        def qm_item(pr):
            c0 = 1792 + pr * 128
            proj_fm(c0, n, gb, ntl, ("w_a", 7),
                    lambda ps, pk: evac_copy(qmT[:, pr, q0:q0 + n], ps[:, 0:n], [pk], [("qmT", i) for i in tiles], eng="act"))

        def pool_mixer_ops():
            ops = []
            ops.append(lambda: S.add("dve", lambda e: e.tensor_tensor(out=sA[:, :, 1:L], in0=ub[:, :, 1:L], in1=ub[:, :, 0:L - 1], op=ALU.add),
                                     reads=[("ub",)], writes=[("sA",)]))
            ops.append(lambda: S.add("dve", lambda e: e.tensor_tensor(out=sB[:, :, 3:L], in0=sA[:, :, 3:L], in1=sA[:, :, 1:L - 2], op=ALU.add),
                                     reads=[("sA",)], writes=[("sB",)]))
            ops.append(lambda: S.add("dve", lambda e: e.tensor_tensor(out=sA[:, 1, 7:L], in0=sB[:, 1, 7:L], in1=sB[:, 1, 3:L - 4], op=ALU.add),
                                     reads=[("sB",)], writes=[("sA",)]))
            ops.append(lambda: S.add("dve", lambda e: e.tensor_tensor(out=sB[:, 1, 15:L], in0=sA[:, 1, 15:L], in1=sA[:, 1, 7:L - 8], op=ALU.add),
                                     reads=[("sA",)], writes=[("sB",)]))
            srcs = [(sA, 0, 0, 2.0), (sB, 0, 1, 4.0), (sA, 1, 0, 8.0), (sB, 1, 1, 16.0)]
            for (sbuf_, t, hlf, w) in srcs:
                p0, p1 = hlf * 64, hlf * 64 + 64
                kk = ("sA",) if sbuf_ is sA else ("sB",)
                ops.append(lambda sbuf_=sbuf_, t=t, p0=p0, p1=p1, w=w, kk=kk: S.add("dve", lambda e: e.scalar_tensor_tensor(
                    out=pT[p0:p1, t, 0:n], in0=sbuf_[p0:p1, t, 16:L], scalar=1.0 / w, in1=ub[p0:p1, t, 16:L],
                    op0=ALU.mult, op1=ALU.subtract),
                    reads=[kk, ("ub",)], writes=[("pT",)]))
            if 5 in tiles:
                fo = (5 - tiles[0]) * 128

                def fix():
                    for (sbuf_, t, hlf, w) in srcs:
                        p0, p1 = hlf * 64, hlf * 64 + 64
                        kk = ("sA",) if sbuf_ is sA else ("sB",)
                        S.add("dve", lambda e, sbuf_=sbuf_, t=t, p0=p0, p1=p1: e.tensor_tensor(
                            out=ptmp[p0:p1, :], in0=sbuf_[p0:p1, t, 16 + fo:32 + fo], in1=invcnt[p0:p1, t, :], op=ALU.mult),
                            reads=[kk, ("invcnt",)], writes=[("ptmp",)])
                        S.add("dve", lambda e, t=t, p0=p0, p1=p1: e.tensor_tensor(
                            out=pT[p0:p1, t, fo:fo + 16], in0=ptmp[p0:p1, :], in1=ub[p0:p1, t, 16 + fo:32 + fo], op=ALU.subtract),
                            reads=[("ptmp",), ("ub",)], writes=[("pT",)])
                ops.append(fix)
            ops.append(lambda: S.add("dve", lambda e: e.tensor_copy(out=ub[:, :, 0:16], in_=ub[:, :, n:n + 16]),
                                     reads=[("ub",)], writes=[("ub",)]))
            return ops

        def pool_mm(t):
            pi = ctr["pj"] % 4; ctr["pj"] += 1
            S.add("pe", lambda e: e.matmul(pj[pi][:, 0:n], lhsT=wpbd[:, t, :], rhs=pT[:, t, 0:n], start=True, stop=True),
                  reads=[("pT",)] + wpbd_keys, writes=[pskey("pj", pi)])
            S.add("act", lambda e: e.activation(out=ypT[:, t, q0:q0 + n], in_=pj[pi][:, 0:n], func=AF.Copy,
                                                scale=pscale[:, t:t + 1]),
                  reads=[pskey("pj", pi), ("pscale",)], writes=[("ypT", t, q0)])

        for pr in range(2):
            items.append(lambda pr=pr: u_item(pr))
        for pr in range(4):
            items.append(lambda pr=pr: k_item(pr))
        mix = pool_mixer_ops() if isq else []
        for j, i in enumerate(tiles):
            items.append(lambda j=j, i=i: v_item(j, i))
            if mix:
                items.append(mix.pop(0))
        if isq:
            for pr in range(4):
                items.append(lambda pr=pr: q_item(pr))
                if mix:
                    items.append(mix.pop(0))
            for pr in range(2):
                items.append(lambda pr=pr: qm_item(pr))
                if mix:
                    items.append(mix.pop(0))
            items.extend(mix)
            if gi == len(groups) - 1:
                items.append(early_B)
            for t in range(2):
                deferred.append(lambda t=t: pool_mm(t))
        return items

    T_items = [wavefront([tile_T_stages(i, gi % 2, j) for j, i in enumerate(tiles)])
               for gi, tiles in enumerate(groups)]
    T_items[0][0]()
    const_dmas.pop(0)()
    T_items[0][1]()
    for fn_ in const_dmas:
        fn_()
    for th in T_items[0][2:]:
        th()
    late_setup()
    for gi, tiles in enumerate(groups):
        if gi == 1:
            mem_path()
        carry = list(deferred)
        del deferred[:]
        P = group_items(gi, tiles)
        P[6:6] = carry
        T = T_items[gi + 1] if gi + 1 < len(groups) else []
        ti = 0
        for k_, p in enumerate(P):
            p()
            want = min(len(T), (len(T) * (k_ + 1) * 10) // (len(P) * 6))
            while ti < want:
                T[ti](); ti += 1
        while ti < len(T):
            T[ti](); ti += 1
    for th in deferred:
        th()

    ps_close()
    for nm in ("hT", "w_mk", "wpbd", "wp32", "id32", "pscale", "invcnt", "ub", "sA", "sB", "pT", "ptmp", "memT"):
        M.release(nm)

    stop_after = (debug or {}).get("_stop", "D")
    if debug and stop_after == "A":
        stg = M.alloc("dbgstg", [128, NT * 8 * 65], F32)
        if "kT" in debug:
            S.add("dve", lambda e: e.tensor_copy(out=stg[:, 0:4 * WIN], in_=kT[:].rearrange("p a b -> p (a b)")),
                  reads=[("kT", i) for i in range(NT)], writes=[("dbgstg",)])
            dma("sp", dbg_d["kT"], stg[:, 0:4 * WIN], reads=[("dbgstg",)], tag="out")
        if "ypT" in debug:
            S.add("dve", lambda e: e.tensor_copy(out=stg[:, 0:2 * NQTOK], in_=ypT[:].rearrange("p a b -> p (a b)")),
                  reads=[("ypT", t, (g[0] - Q0) * 128) for t in range(2) for g in groups[1:]], writes=[("dbgstg",)])
            dma("sp", dbg_d["ypT"], stg[:, 0:2 * NQTOK], reads=[("dbgstg",)], tag="out")
        if "V" in debug:
            S.add("dve", lambda e: e.tensor_copy(out=stg[:], in_=Vb[:].rearrange("p a b c -> p (a b c)")),
                  reads=[("V", i) for i in range(NT)] + [("Vones", h) for h in range(8)], writes=[("dbgstg",)])
            dma("sp", dbg_d["V"], stg[:], reads=[("dbgstg",)], tag="out")

    def phase_B():
        BPOS = {0: 0, 3: 1, 4: 2, 1: 3, 2: 4}
        OT = M.alloc("OT", [128, 6, NQTOK], BF16)
        PT = M.alloc("PT", [128, 4, 640], BF16)
        PmT = M.alloc("PmT", [128, 4, 256], BF16)
        On = M.alloc("On", [128, 2, 12, 64], BF16)
        den = M.alloc("den", [128, 2, 12, 1], F32)
        w_upp = M.alloc("w_upp", [128, 2, D], BF16)
        w_upa = M.alloc("w_upa", [128, 4, D], BF16)
        w_upm = M.alloc("w_upm", [128, 2, D], BF16)
        dma("pool", w_upp[:], w_upp_d.rearrange("(kc p) c -> p kc c", p=128), writes=[("w_upp",)])
        dma("pool", w_upa[:], w_upa_d.rearrange("(kc p) c -> p kc c", p=128), writes=[("w_upa",)])
        dma("pool", w_upm[:], w_upm_d.rearrange("(kc p) c -> p kc c", p=128), writes=[("w_upm",)])
        M.release("xs")
        w_out = M.alloc("w_out", [128, 8, D], BF16)
        dma("pool", w_out[:], w_out_d.rearrange("(kc p) c -> p kc c", p=128), writes=[("w_out",)])
        pre_B["mg"] = M.alloc("mg", [128, 1, 8, 512], BF16)
        pre_B["tb"] = M.alloc("tb", [128, 4, 512], F32)
        w_g = {}
        for br in range(2):
            try:
                t_ = M.alloc("w_g%d_0" % br, [128, 8, 512], BF16)
            except RuntimeError:
                break
            c0 = 2048 + br * 1024
            dma("pool", t_[:], w_in_v[:, :, c0:c0 + 512], writes=[("w_g%d_0" % br,)])
            for nt in range(4):
                w_g[(br, nt)] = (t_, "w_g%d_0" % br, nt * 128)

        for kc in range(8):
            dma("pool", wfu_bf[kc * 128:(kc + 1) * 128, :], w_fu_d[kc * 128:(kc + 1) * 128, :], writes=[("wfu_bf", kc)])
        for rb in range(8):
            dma("pool", wfd_bf[rb * 352:(rb + 1) * 352, :], w_fd_d[rb * 352:(rb + 1) * 352, :], writes=[("wfd_bf", rb)])

        ps_open()
        NS = 3
        psS = [ps_alloc("psS%d" % i, [128, 1024], F32) for i in range(NS)]
        psST = [t_[:, 0:512].bitcast(BF16).rearrange("p (k c) -> p k c", k=8) for t_ in psS]
        psO = ps_alloc("psO", [128, 2, 512], F32)
        c = {"S": 0, "tmp": 0, "PT": 0, "Pm": 0, "On": 0, "T": 0}

        def unit(un):
            return un // 6, (un % 6) * 65

        def band_head(uq, h):
            u = uq + Q0
            pr, base = h // 2, (h % 2) * 64
            sb = c["S"] % NS; c["S"] += 1
            pb = c["PT"] % 4; c["PT"] += 1
            bank, col = unit(h)

            def mmS(e):
                ins = None
                for b in range(5):
                    kt = u - 4 + b
                    pos = BPOS[b]
                    ins = e.matmul(psS[sb][:, pos * 128:(pos + 1) * 128], lhsT=kT[base:base + 64, pr, kt * 128:(kt + 1) * 128],
                                   rhs=qT[base:base + 64, pr, uq * 128:(uq + 1) * 128], start=True, stop=True)
                return ins
            S.add("pe", mmS, reads=[("kT", u - 4 + b) for b in range(5)] + [("qT", u)], writes=[("psS%d" % sb,)])
            S.add("dve", lambda e: e.tensor_tensor(out=psS[sb][:, 0:384], in0=psS[sb][:, 0:384], in1=bias8[:, h, 0:384], op=ALU.add),
                  reads=[("psS%d" % sb,), ("bias8",)], writes=[("psS%d" % sb,)])
            S.add("act", lambda e: e.activation(out=PT[:, pb, :], in_=psS[sb][:, 0:640], func=AF.Exp, scale=0.125),
                  reads=[("psS%d" % sb,)], writes=[("PT", pb)])

            def mmO(e):
                ins = None
                for b in range(5):
                    kt = u - 4 + b
                    pos = BPOS[b]
                    ins = e.matmul(psO[:, bank, col:col + 65], lhsT=PT[:, pb, pos * 128:(pos + 1) * 128], rhs=Vb[:, kt, h, :],
                                   start=(b == 0), stop=(b == 4))
                return ins
            return lambda: S.add("pe", mmO, reads=[("PT", pb), ("Vones", h)] + [("V", u - 4 + b) for b in range(5)],
                                 writes=[("psO", bank)])

        def mem_head(uq, hm):
            u = uq + Q0
            pr, base = hm // 2, (hm % 2) * 64
            sb = c["S"] % NS; c["S"] += 1
            pm = c["Pm"] % 4; c["Pm"] += 1
            bank, col = unit(8 + hm)

            def mmS(e):
                ins = None
                for mt in range(2):
                    ins = e.matmul(psS[sb][:, mt * 128:(mt + 1) * 128], lhsT=kmT[base:base + 64, pr, mt * 128:(mt + 1) * 128],
                                   rhs=qmT[base:base + 64, pr, uq * 128:(uq + 1) * 128], start=True, stop=True)
                return ins
            S.add("pe", mmS, reads=[("kmT", pr), ("qmT", u)], writes=[("psS%d" % sb,)])
            S.add("act", lambda e: e.activation(out=PmT[:, pm, :], in_=psS[sb][:, 0:256], func=AF.Exp, scale=0.125),
                  reads=[("psS%d" % sb,)], writes=[("PmT", pm)])

            def mmO(e):
                ins = None
                for mt in range(2):
                    ins = e.matmul(psO[:, bank, col:col + 65], lhsT=PmT[:, pm, mt * 128:(mt + 1) * 128], rhs=vm[:, mt, hm, :],
                                   start=(mt == 0), stop=(mt == 1))
                return ins
            return lambda: S.add("pe", mmO, reads=[("PmT", pm), ("vm", 0), ("vm", 1), ("vmones",)], writes=[("psO", bank)])

        obs = {}

        def finish_bank(uq, bank):
            if bank == 0:
                obs[uq] = c["On"] % 2; c["On"] += 1
            ob = obs[uq]
            v3 = psO[:, bank, 0:390].rearrange("p (u c) -> p u c", c=65)
            dsl = den[:, ob, bank * 6:(bank + 1) * 6, :]
            S.add("dve", lambda e: e.tensor_scalar_max(out=dsl, in0=v3[:, :, 64:65], scalar1=1e-30),
                  reads=[("psO", bank)], writes=[("den", ob, bank)])
            S.add("dve", lambda e: e.reciprocal(out=dsl, in_=dsl),
                  reads=[("den", ob, bank)], writes=[("den", ob, bank)])
            S.add("dve", lambda e: e.tensor_tensor(
                out=On[:, ob, bank * 6:(bank + 1) * 6, :], in0=v3[:, :, 0:64], in1=dsl.to_broadcast([128, 6, 64]),
                op=ALU.mult),
                reads=[("psO", bank), ("den", ob, bank)], writes=[("On", ob, bank)])

        def finish_tile_T(uq):
            ob = obs[uq]
            pt = c["S"] % NS; c["S"] += 1
            Onf = On[:, ob, :, :].rearrange("p u d -> p (u d)")

            def tr(e):
                ins = None
                for blk in range(6):
                    ins = e.transpose(psST[pt][:, blk, :], Onf[:, blk * 128:(blk + 1) * 128], ident[:])
                return ins
            S.add("pe", tr, reads=[("On", ob, 0), ("On", ob, 1), ("ident",)], writes=[("psS%d" % pt,)])
            evac_copy(OT[:, :, uq * 128:(uq + 1) * 128], psST[pt][:, 0:6, :], [("psS%d" % pt,)], [("OT", uq)], eng="act")

        ORDER = (0, 1, 2, 3, 4, 5, 8, 9, 10, 11, 6, 7)
        work = [(uq, un) for uq in range(NQ) for un in ORDER]
        pend = []
        later = []
        DEPTH = 2

        def step_later():
            for ent in list(later):
                ent[0] -= 1
                if ent[0] <= 0:
                    later.remove(ent)
                    ent[1]()

        def after_consume(uq0, un0):
            if un0 == 5:
                finish_bank(uq0, 0)
            if un0 == ORDER[-1]:
                finish_bank(uq0, 1)
                later.append([3, lambda uq0=uq0: finish_tile_T(uq0)])

        for (uq, un) in work:
            cons = band_head(uq, un) if un < 8 else mem_head(uq, un - 8)
            pend.append((cons, uq, un))
            if len(pend) > DEPTH:
                c0_, uq0, un0 = pend.pop(0)
                c0_()
                step_later()
                after_consume(uq0, un0)
        while pend:
            c0_, uq0, un0 = pend.pop(0)
            c0_()
            step_later()
            after_consume(uq0, un0)
        while later:
            step_later()
        ps_close()
        for nm in ("bias8", "PT", "PmT", "On", "den", "kT", "V", "qT", "qmT", "kmT", "vm"):
            M.release(nm)
        return OT, w_upp, w_upa, w_upm, w_out, w_g

    if stop_after in ("B", "C", "D"):
        OT, w_upp, w_upa, w_upm, w_out, w_g = phase_B()

    if debug and stop_after == "B":
        stg = M.alloc("dbgstg", [128, 6 * NQTOK], F32)
        S.add("dve", lambda e: e.tensor_copy(out=stg[:], in_=OT[:].rearrange("p a b -> p (a b)")),
              reads=[("OT", uq) for uq in range(NQ)], writes=[("dbgstg",)])
        dma("sp", dbg_d["OT"], stg[:], reads=[("dbgstg",)], tag="out")

    ffw = {}
    ffd = {}
    w_fu_v = wfu_bf.rearrange("(kc p) c -> p kc c", p=128)
    w_fd_v = wfd_bf.rearrange("(j p) c -> p j c", p=128)
    fu_keys = [("wfu_bf", kc) for kc in range(8)]
    fd_keys = [("wfd_bf", rb) for rb in range(8)]
    ffn_order = []
    for ch in range(6):
        ffn_order += [("g", ch), ("v", ch), ("d", ch)]

    def ffn_prefetch(limit=None, strict=False, defer=None):
        n_new = 0
        for key in ffn_order:
            if key in ffw:
                continue
            if limit is not None and n_new >= limit:
                return
            kind, ch = key
            c0 = ch * 512
            cw = min(512, DFF - c0)
            nm = "ff%s%d" % (kind, ch)
            try:
                if kind == "d":
                    j0, j1 = ch * 4, min(22, ch * 4 + 4)
                    t_ = M.alloc(nm, [128, j1 - j0, D], BF16)
                    for j_ in range(j0, j1):
                        ffd[j_] = (t_, nm, j_ - j0)
                else:
                    t_ = M.alloc(nm, [128, 8, cw], BF16)
            except RuntimeError:
                if not strict:
                    return
                if kind != "d":
                    raise
                ffw[key] = None
                for j_ in range(j0, j1):
                    nmj = "ffdj%d" % j_
                    tj = M.alloc(nmj, [128, 1, D], BF16)
                    ffd[j_] = (tj, nmj, 0)
                    issue = (lambda tj=tj, j_=j_, nmj=nmj: dma("sp", tj[:], w_fd_v[:, j_:j_ + 1, :], reads=fd_keys, writes=[(nmj,)]))
                    if defer is None:
                        issue()
                    else:
                        defer.append(issue)
                n_new += 1
                continue
            ffw[key] = t_
            if kind == "d":
                issue = (lambda t_=t_, j0=j0, j1=j1, nm=nm: dma("sp", t_[:], w_fd_v[:, j0:j1, :], reads=fd_keys, writes=[(nm,)]))
            else:
                off = c0 if kind == "g" else DFF + c0
                issue = (lambda t_=t_, off=off, cw=cw, nm=nm: dma("sp", t_[:], w_fu_v[:, :, off:off + cw], reads=fu_keys, writes=[(nm,)]))
            if defer is None:
                issue()
            else:
                defer.append(issue)
            n_new += 1

    ffn_state = {}

    def ffn_tile0_prologue(ps_fn):
        XD_ = 6
        g_c_ = M.alloc("g_c", [128, D], F32)
        ffn_state["g_c"] = g_c_
        dma("sp", g_c_[:], gfin_d, writes=[("g_c",)])
        xsd_ = M.alloc("xsd", [128, XD_, D], F32)
        h2T_ = M.alloc("h2T", [128, 2, 8, 258], BF16)
        ffn_state["xsd"] = xsd_; ffn_state["h2T"] = h2T_
        lists = []
        for sub in range(2):
            def slot_fn(sub=sub):
                return sub

            def load_fn(s, sub=sub):
                dma("sp", xsd_[:, s, :], xmid_d[sub * 128:(sub + 1) * 128, :], reads=[("xmid", sub)], writes=[("xsd", s)])

            def dst(ps, pskey_, dkey, sub=sub):
                evac_copy(h2T_[:, 0, :, 2 + sub * 128:2 + (sub + 1) * 128], ps[:], [pskey_], [dkey], eng="act")
            stages, _ = norm_stages(load_fn, "g_b", g_b, (xsd_, "xsd"), slot_fn, ps_fn, dst, ("h2T", 0, 1 + sub),
                                    reuse=(rstd_m[:, sub:sub + 1], ("rstd_m", sub)))
            lists.append(stages)
        def halo():
            S.add("pool", lambda e: e.tensor_copy(out=h2T_[:, 0, :, 0:2], in_=h2halo[:]),
                  reads=[("h2halo",)], writes=[("h2T", 0, 0)])
        return lists, halo

    def phase_C():
        XC = 8
        xsc = M.alloc("xsc", [128, XC, D], F32)
        hTc = M.alloc("hT", [128, 2, 8, 512], BF16)
        tg = M.alloc("tg", [128, 2, 512], F32)
        bgate = M.alloc("bgate", [128, 24], F32)
        wg_names = sorted(set(v[1] for v in w_g.values()))
        wg_dmas = []
        for q4 in range(4):
            for br in range(3):
                if (br, 2 * q4) in w_g:
                    continue
                nm = "w_g%d_q%d" % (br, q4)
                t_ = M.alloc(nm, [128, 8, 256], BF16)
                c0 = 2048 + br * 1024 + q4 * 256
                wg_dmas.append(lambda t_=t_, c0=c0, nm=nm: dma("pool", t_[:], w_in_v[:, :, c0:c0 + 256], writes=[(nm,)]))
                wg_names.append(nm)
                for k_ in range(2):
                    w_g[(br, 2 * q4 + k_)] = (t_, nm, k_ * 128)
        tb = pre_B["tb"]
        a01 = M.alloc("a01", [128, 1, 512], F32)
        mg = pre_B["mg"]
        dma("sp", bgate[:], bgate_d, writes=[("bgate",)])
        ps_open()
        psTc = [ps_alloc("psTc0", [128, 8, 128], BF16), ps_alloc("psTc1", [128, 8, 128], BF16)]
        psG = [ps_alloc("psG0", [128, 512], F32), ps_alloc("psG1", [128, 512], F32)]
        psY = [ps_alloc("psY0", [128, 512], F32), ps_alloc("psY1", [128, 512], F32)]
        psX = ps_alloc("psX", [128, 2, 512], F32)
        c = {"G": 0, "Y": 0, "tg": 0, "tb": 0, "xs": 0, "T": 0}
        ysrc = [(w_upp, "w_upp", 2, ypT, lambda q0, n: [("ypT", t, (g_[0] - Q0) * 128) for t in range(2) for g_ in groups[1:]]),
                (w_upa, "w_upa", 4, OT, None), (w_upm, "w_upm", 2, OT, None)]
        qgroups = groups_C
        slots_of = {}

        def prologue_pairs(gi):
            tiles = qgroups[gi]
            gb = gi % 2
            slots_of[gi] = [None] * len(tiles)
            lists = []
            for j, i in enumerate(tiles):
                def slot_fn(j=j):
                    s = c["xs"] % XC; c["xs"] += 1
                    slots_of[gi][j] = s
                    return s

                def load_fn(s, i=i):
                    dma("sp", xsc[:, s, :], xw[i * 128:(i + 1) * 128, :], writes=[("xsc", s)])

                def ps_fn():
                    pt = c["T"] % 2; c["T"] += 1
                    return psTc[pt], ("psTc%d" % pt,)

                def dst(ps, pskey_, dkey, j=j):
                    evac_copy(hTc[:, gb, :, j * 128:(j + 1) * 128], ps[:], [pskey_], [dkey])
                stages, _ = norm_stages(load_fn, "g_a", g_a, (xsc, "xsc"), slot_fn, ps_fn, dst, ("hT", gb, j),
                                        reuse=(rstd_x[:, i:i + 1], ("rstd_x", i)))
                lists.append(stages)
            return wavefront(lists)

        def ntile(gi, nt):
            tiles = qgroups[gi]
            gb = gi % 2
            n = 128 * len(tiles)
            q0 = (tiles[0] - Q0) * 128
            hkeys = [("hT", gb, j) for j in range(len(tiles))]
            otkeys = [("OT", i - Q0) for i in tiles]
            terms = []
            for br in range(3):
                gs = c["G"] % 2; c["G"] += 1
                ys = c["Y"] % 2; c["Y"] += 1
                ti = c["tg"] % 2; c["tg"] += 1
                bi = c["tb"] % 4; c["tb"] += 1
                wgt, wgname, gc0 = w_g[(br, nt)]

                def mmG(e, gs=gs, wgt=wgt, gc0=gc0):
                    ins = None
                    for kc in range(8):
                        ins = e.matmul(psG[gs][:, 0:n], lhsT=wgt[:, kc, gc0:gc0 + 128], rhs=hTc[:, gb, kc, 0:n],
                                       start=(kc == 0), stop=(kc == 7))
                    return ins
                S.add("pe", mmG, reads=[(wgname,)] + hkeys, writes=[("psG%d" % gs,)])
                wt, wname, nk, src_, keyfn = ysrc[br]
                koff = 4 if br == 2 else 0

                def mmY(e, ys=ys, wt=wt, nk=nk, src_=src_, koff=koff):
                    ins = None
                    for kc in range(nk):
                        ins = e.matmul(psY[ys][:, 0:n], lhsT=wt[:, kc, nt * 128:(nt + 1) * 128],
                                       rhs=src_[:, koff + kc, q0:q0 + n], start=(kc == 0), stop=(kc == nk - 1))
                    return ins
                rk = keyfn(q0, n) if keyfn else otkeys
                S.add("pe", mmY, reads=[(wname,)] + rk, writes=[("psY%d" % ys,)])
                S.add("act", lambda e, gs=gs, ti=ti, br=br: e.activation(
                    out=tg[:, ti, 0:n], in_=psG[gs][:, 0:n], func=AF.Sigmoid, bias=bgate[:, br * 8 + nt:br * 8 + nt + 1]),
                    reads=[("psG%d" % gs,), ("bgate",)], writes=[("tg", ti)])
                S.add("dve", lambda e, ys=ys, ti=ti, bi=bi: e.tensor_tensor(
                    out=tb[:, bi, 0:n], in0=psY[ys][:, 0:n], in1=tg[:, ti, 0:n], op=ALU.mult),
                    reads=[("psY%d" % ys,), ("tg", ti)], writes=[("tb", bi)])
                terms.append(bi)
            S.add("pool", lambda e: e.tensor_tensor(out=a01[:, 0, 0:n], in0=tb[:, terms[0], 0:n], in1=tb[:, terms[1], 0:n],
                                                    op=ALU.add),
                  reads=[("tb", terms[0]), ("tb", terms[1])], writes=[("a01", 0)])
            S.add("pool", lambda e: e.tensor_tensor(out=mg[:, 0, nt, 0:n], in0=a01[:, 0, 0:n], in1=tb[:, terms[2], 0:n],
                                                    op=ALU.add),
                  reads=[("a01", 0), ("tb", terms[2])], writes=[("mg", 0, nt)])

        def out_stage(gi, hooks=None):
            tiles = qgroups[gi]
            mkeys = [("mg", 0, nt) for nt in range(8)]
            for j, i in enumerate(tiles):
                s = slots_of[gi][j]
                for fn_ in (hooks or {}).get(j, []):
                    fn_()

                for hf in range(2):
                    def mmX(e, j=j, hf=hf):
                        ins = None
                        for kc in range(8):
                            ins = e.matmul(psX[:, hf, :], lhsT=mg[:, 0, kc, j * 128:(j + 1) * 128],
                                           rhs=w_out[:, kc, hf * 512:(hf + 1) * 512], start=(kc == 0), stop=(kc == 7))
                        return ins
                    S.add("pe", mmX, reads=mkeys + [("w_out",)], writes=[("psX", hf)])
                    S.add("dve", lambda e, s=s, hf=hf: e.tensor_tensor(
                        out=xsc[:, s, hf * 512:(hf + 1) * 512], in0=psX[:, hf, :], in1=xsc[:, s, hf * 512:(hf + 1) * 512],
                        op=ALU.add),
                        reads=[("psX", hf), ("xsc", s)], writes=[("xsc", s)])
                if i >= 5:
                    r = i - 5
                    dma("sp", xmid_d[r * 128:(r + 1) * 128, :], xsc[:, s, :], reads=[("xsc", s)], writes=[("xmid", r)])
                    k = ctr["ss"] % 8; ctr["ss"] += 1
                    S.add("act", lambda e, s=s, k=k: e.activation(out=junk[:], in_=xsc[:, s, :], func=AF.Square, scale=1.0 / 32.0,
                                                                 accum_out=ss[:, k:k + 1]),
                          reads=[("xsc", s)], writes=[("junk",), ("ss", k)])
                    S.add("pool", lambda e, k=k: e.tensor_scalar(out=ss[:, k:k + 1], in0=ss[:, k:k + 1], scalar1=EPS, scalar2=None,
                                                                op0=ALU.add),
                          reads=[("ss", k)], writes=[("ss", k)])
                    S.add("pool", lambda e, k=k, r=r: e.tensor_tensor(out=rstd_m[:, r:r + 1], in0=ss[:, k:k + 1], in1=mhalf[:, 0:1],
                                                                     op=ALU.pow),
                          reads=[("ss", k), ("mhalf",)], writes=[("rstd_m", r)])
                else:
                    def dst(ps, pskey_, dkey):
                        S.add("act", lambda e: e.activation(out=h2halo[:], in_=ps[:, :, 126:128], func=AF.Copy,
                                                            scale=flag[:, 0:1]),
                              reads=[pskey_, ("flag",)], writes=[dkey])
                    ctx = norm_s1(None, "g_b", g_b, loaded=s, xbuf=(xsc, "xsc"))
                    pt = c["T"] % 2; c["T"] += 1
                    norm_s2(ctx, psTc[pt], ("psTc%d" % pt,), dst, ("h2halo",))

        pro0 = prologue_pairs(0)
        if wg_dmas:
            wg_dmas.pop(0)()
        for th in pro0[:11]:
            th()
        for fn_ in wg_dmas:
            fn_()
        for th in pro0[11:]:
            th()
        for gi in range(len(qgroups)):
            nxt = prologue_pairs(gi + 1) if gi + 1 < len(qgroups) else []
            ti_ = 0
            last = gi == len(qgroups) - 1
            for nt in range(8):
                ntile(gi, nt)
                want = min(len(nxt), (len(nxt) * (nt + 1)) // 5)
                while ti_ < want:
                    nxt[ti_](); ti_ += 1
                if last and nt == 3:
                    early = [nm for nm in wg_names if all(v[1] != nm or k[1] < 4 for k, v in w_g.items())]
                    for nm in early:
                        M.release(nm)
                        wg_names.remove(nm)
                    ffn_prefetch()
            if last:
                for nm in ["bgate", "hT", "tg", "tb", "a01", "w_upp", "w_upa", "w_upm", "OT", "ypT", "g_a"] + list(wg_names):
                    M.release(nm)
                def ps_fn_c():
                    pt = c["T"] % 2; c["T"] += 1
                    return psTc[pt], ("psTc%d" % pt,)
                lists0, halo0 = ffn_tile0_prologue(ps_fn_c)
                for sub in range(2):
                    lists0[sub][0]()
                    lists0[sub][1]()
                ffn_prefetch(limit=6)
                hooks = {2: [lists0[0][2], lists0[1][2]]}
                out_stage(gi, hooks)
                for fn_ in (lists0[0][3], lists0[1][3], halo0, ffn_prefetch):
                    fn_()
            else:
                out_stage(gi)
        ps_close()
        for nm in ("w_out", "mg", "xsc"):
            M.release(nm)

    if stop_after in ("C", "D"):
        phase_C()

    if debug and stop_after == "C":
        xdb = M.alloc("xdb", [128, 4, D], F32)
        for r in range(16):
            s = r % 4
            dma("sp", xdb[:, s, :], xmid_d[r * 128:(r + 1) * 128, :], reads=[("xmid", r)], writes=[("xdb", s)])
            dma("sp", dbg_d["xmid"][r * 128:(r + 1) * 128, :], xdb[:, s, :], reads=[("xdb", s)], tag="out")

    def phase_D():
        XD = 6
        xsd = ffn_state["xsd"]
        g_c = ffn_state["g_c"]
        h2T_pre = None
        h2T = ffn_state["h2T"]
        NR = 3
        cg = M.alloc("cg", [128, NR, 256], F32)
        cv = M.alloc("cv", [128, NR, 256], F32)
        gl = M.alloc("gl", [128, 2, 256], F32)
        mb = M.alloc("mb", [128, 4, 256], BF16)
        ob = M.alloc("ob", [128, 2, D], F32)
        late_dmas = []
        ffn_prefetch(strict=True, defer=late_dmas)
        ps_open()
        NPA = 4
        psA = [ps_alloc("psA%d" % i, [128, 512], F32) for i in range(NPA)]
        psAT = [t_[:].bitcast(BF16).rearrange("p (k c) -> p k c", k=8) for t_ in psA]
        psXD = [ps_alloc("psXD%d" % i, [128, 2, 512], F32) for i in range(2)]
        c = {"r": 0, "gl": 0, "mb": 0, "ob": 0, "xs": 2, "pa": 0}

        sched_at = {}

        def at(it, fn):
            sched_at.setdefault(it, []).append(fn)

        slots = {}

        def plan_prologue(tt, it0):
            tbuf = tt % 2
            slots[tt] = [None, None]
            per_sub = []
            for sub in range(2):
                r = tt * 2 + sub

                def slot_fn(sub=sub):
                    s = c["xs"] % XD; c["xs"] += 1
                    slots[tt][sub] = s
                    return s

                def load_fn(s, r=r):
                    dma("sp", xsd[:, s, :], xmid_d[r * 128:(r + 1) * 128, :], reads=[("xmid", r)], writes=[("xsd", s)])

                def ps_fn():
                    pa = c["pa"] % NPA; c["pa"] += 1
                    return psAT[pa], ("psA%d" % pa,)

                def dst(ps, pskey_, dkey, sub=sub):
                    evac_copy(h2T[:, tbuf, :, 2 + sub * 128:2 + (sub + 1) * 128], ps[:], [pskey_], [dkey], eng="act")
                stages, _ = norm_stages(load_fn, "g_b", g_b, (xsd, "xsd"), slot_fn, ps_fn, dst, ("h2T", tbuf, 1 + sub),
                                        reuse=(rstd_m[:, r:r + 1], ("rstd_m", r)))
                per_sub.append(stages)

            def halo():
                if tt == 0:
                    S.add("pool", lambda e: e.tensor_copy(out=h2T[:, tbuf, :, 0:2], in_=h2halo[:]),
                          reads=[("h2halo",)], writes=[("h2T", tbuf, 0)])
                else:
                    S.add("pool", lambda e: e.tensor_copy(out=h2T[:, tbuf, :, 0:2], in_=h2T[:, 1 - tbuf, :, 256:258]),
                          reads=[("h2T", 1 - tbuf, 2)], writes=[("h2T", tbuf, 0)])
            offs = (0, 9, 12, 15)
            for k_ in range(4):
                for sub in range(2):
                    at(it0 + offs[k_] + (sub if k_ else 0), per_sub[sub][k_])
            at(it0 + 17, halo)

        def produce(tt, j):
            tbuf = tt % 2
            hk = [("h2T", tbuf, q) for q in range(3)]
            ri = c["r"] % NR; c["r"] += 1
            for gv in range(2):
                pa = c["pa"] % NPA; c["pa"] += 1
                col0 = gv * DFF + j * 128
                jj = gv * 22 + j
                cbuf, cname = (cg, "cg") if gv == 0 else (cv, "cv")

                wch = ffw[("g" if gv == 0 else "v", (j * 128) // 512)]
                wc0 = (j * 128) % 512
                wnm = "ff%s%d" % ("g" if gv == 0 else "v", (j * 128) // 512)

                def mmA(e, pa=pa, wch=wch, wc0=wc0):
                    ins = None
                    for kc in range(8):
                        ins = e.matmul(psA[pa][:, 0:258], lhsT=wch[:, kc, wc0:wc0 + 128], rhs=h2T[:, tbuf, kc, :],
                                       start=(kc == 0), stop=(kc == 7))
                    return ins
                S.add("pe", mmA, reads=[(wnm,)] + hk, writes=[("psA%d" % pa,)])
                S.add("act", lambda e, pa=pa, cbuf=cbuf, jj=jj: e.activation(
                    out=cbuf[:, ri, :], in_=psA[pa][:, 2:258], func=AF.Identity, bias=convb[:, jj:jj + 1],
                    scale=convw[:, jj, 2:3]),
                    reads=[("psA%d" % pa,), ("convw",), ("convb",)], writes=[(cname, ri)])
                for tap in (1, 0):
                    S.add("dve", lambda e, pa=pa, cbuf=cbuf, jj=jj, tap=tap: e.scalar_tensor_tensor(
                        out=cbuf[:, ri, :], in0=psA[pa][:, tap:tap + 256], scalar=convw[:, jj, tap:tap + 1],
                        in1=cbuf[:, ri, :], op0=ALU.mult, op1=ALU.add),
                        reads=[("psA%d" % pa,), ("convw",), (cname, ri)], writes=[(cname, ri)])
            return (tt, j, ri)

        def mid(ctx):
            tt, j, ri = ctx
            gi_ = c["gl"] % 2; c["gl"] += 1
            S.add("act", lambda e: e.activation(out=gl[:, gi_, :], in_=cg[:, ri, :], func=AF.Gelu),
                  reads=[("cg", ri)], writes=[("gl", gi_)])
            mi = c["mb"] % 4; c["mb"] += 1
            S.add("pool", lambda e: e.tensor_tensor(out=mb[:, mi, :], in0=gl[:, gi_, :], in1=cv[:, ri, :], op=ALU.mult),
                  reads=[("gl", gi_), ("cv", ri)], writes=[("mb", mi)])
            return (tt, j, mi)

        def consume(ctx, it):
            tt, j, mi = ctx
            wd, wdname, wdi = ffd[j]

            def mmD(e):
                ins = None
                for sub in range(2):
                    for hf in range(2):
                        ins = e.matmul(psXD[sub][:, hf, :], lhsT=mb[:, mi, sub * 128:(sub + 1) * 128],
                                       rhs=wd[:, wdi, hf * 512:(hf + 1) * 512], start=(j == 0), stop=(j == 21))
                return ins
            S.add("pe", mmD, reads=[("mb", mi), (wdname,)], writes=[("psXD0",), ("psXD1",)])
            if j == 21:
                plan_epilogue(tt, it)

        def plan_epilogue(tt, it):
            sl = slots[tt]
            ks = [None, None]
            ois = [None, None]

            def e1(sub):
                s = sl[sub]
                S.add("dve", lambda e: e.tensor_tensor(
                    out=xsd[:, s, :], in0=psXD[sub][:].rearrange("p a b -> p (a b)"), in1=xsd[:, s, :], op=ALU.add),
                    reads=[("psXD%d" % sub,), ("xsd", s)], writes=[("xsd", s)])

            def e1b(sub):
                s = sl[sub]
                k = ctr["ss"] % 8; ctr["ss"] += 1
                ks[sub] = k
                S.add("act", lambda e: e.activation(out=junk[:], in_=xsd[:, s, :], func=AF.Square,
                                                    scale=1.0 / 32.0, accum_out=ss[:, k:k + 1]),
                      reads=[("xsd", s)], writes=[("junk",), ("ss", k)])
                S.add("pool", lambda e: e.tensor_scalar(out=ss[:, k:k + 1], in0=ss[:, k:k + 1], scalar1=EPS,
                                                        scalar2=None, op0=ALU.add),
                      reads=[("ss", k)], writes=[("ss", k)])
                S.add("pool", lambda e: e.tensor_tensor(out=ss[:, k:k + 1], in0=ss[:, k:k + 1], in1=mhalf[:, 0:1],
                                                        op=ALU.pow),
                      reads=[("ss", k), ("mhalf",)], writes=[("ss", k)])

            def e2(sub):
                s, k = sl[sub], ks[sub]
                oi = c["ob"] % 2; c["ob"] += 1
                ois[sub] = oi
                S.add("act", lambda e: e.activation(out=ob[:, oi, :], in_=xsd[:, s, :], func=AF.Copy, scale=ss[:, k:k + 1]),
                      reads=[("xsd", s), ("ss", k)], writes=[("ob", oi)])

            def e3(sub):
                r = tt * 2 + sub
                oi = ois[sub]
                S.add("pool", lambda e: e.tensor_tensor(out=ob[:, oi, :], in0=ob[:, oi, :], in1=g_c[:], op=ALU.mult),
                      reads=[("ob", oi), ("g_c",)], writes=[("ob", oi)])
                dma("sp", out_d[r * 128:(r + 1) * 128, :], ob[:, oi, :], reads=[("ob", oi)], tag="out")
            e1(0)
            at(it + 1, lambda: e1(1))
            at(it + 1, lambda: e1b(0))
            at(it + 2, lambda: e1b(1))
            at(it + 3, lambda: e2(0))
            at(it + 4, lambda: e2(1))
            at(it + 5, lambda: e3(0))
            at(it + 6, lambda: e3(1))

        slots[0] = [0, 1]
        for k_, fn_ in enumerate(late_dmas):
            at(k_, fn_)
        q1, q2 = [], []
        it = 0
        for tt in range(8):
            if tt + 1 < 8:
                plan_prologue(tt + 1, it)
            for j in range(22):
                for fn in sched_at.pop(it, []):
                    fn()
                q1.append(produce(tt, j))
                if len(q1) > 1:
                    q2.append(mid(q1.pop(0)))
                if len(q2) > 1:
                    consume(q2.pop(0), it)
                it += 1
        while q1 or q2:
            for fn in sched_at.pop(it, []):
                fn()
            if q1:
                q2.append(mid(q1.pop(0)))
            if q2:
                consume(q2.pop(0), it)
            it += 1
        for it_ in sorted(sched_at):
            for fn in sched_at.pop(it_):
                fn()
        ps_close()

    if stop_after == "D":
        phase_D()

    outs = [o for o in S.dma_list if o.tag == "out"]
    S.add("sp", lambda e: e.nop(), deps=outs)
    S.emit()
    return nc


def _bias_layout(rel_bias):
    k = np.arange(128)[:, None, None]
    b = np.arange(5)[None, :, None]
    qi = np.arange(128)[None, None, :]
    rel = (4 - b) * 128 + qi - k
    idx = np.clip(rel, -128, 128) + 128
    g = rel_bias[:, idx]
    g = g.transpose(1, 0, 2, 3)
    near = g[:, :, [0, 3, 4], :].reshape(128, 8, 384)
    far = g[:, :, 1, 0:1]
    return np.ascontiguousarray(np.concatenate([near, far], axis=2).reshape(128, 8 * 385)).astype(np.float32)


def make_in_maps(inputs):
    f = lambda a: np.ascontiguousarray(np.asarray(a, dtype=np.float32))
    x = f(inputs["x"])[0]
    shared = {
        "mem": f(inputs["mem"])[0],
        "w_in": f(inputs["w_in"])[0],
        "bgate": f(f(inputs["b_gate"])[0].reshape(24, 128).T),
        "w_pool": f(inputs["w_pool"])[0],
        "pscale": f(f(inputs["pool_scale"])[0].reshape(2, 128).T),
        "biasT": _bias_layout(f(inputs["rel_bias"])[0]),
        "w_mem_kv": f(inputs["w_mem_kv"])[0],
        "w_up_pool": f(inputs["w_up_pool"])[0],
        "w_up_attn": f(inputs["w_up_attn"])[0],
        "w_up_mem": f(inputs["w_up_mem"])[0],
        "w_out": f(inputs["w_out"])[0],
        "w_ffn_up": f(inputs["w_ffn_up"])[0],
        "convw": f(f(inputs["conv_w"])[0].reshape(3, 44, 128).transpose(2, 1, 0).reshape(128, 132)),
        "convb": f(f(inputs["conv_b"])[0].reshape(44, 128).T),
        "w_ffn_down": f(inputs["w_ffn_down"])[0],
        "g_mix_b": f(np.broadcast_to(f(inputs["norm_mix_g"])[0], (128, D))),
        "g_mem_b": f(np.broadcast_to(f(inputs["norm_mem_g"])[0], (128, D))),
        "g_ffn_b": f(np.broadcast_to(f(inputs["norm_ffn_g"])[0], (128, D))),
        "g_fin_b": f(np.broadcast_to(f(inputs["norm_final_g"]), (128, D))),
        "ident": np.eye(128, dtype=np.float32),
    }
    maps = []
    wins = (2.0, 4.0, 8.0, 16.0)
    for c in range(NCORES):
        t0 = c * OWN
        xw = np.zeros((WIN, D), np.float32)
        lo = t0 - HALO
        src_lo = max(lo, 0)
        xw[src_lo - lo:] = x[src_lo:t0 + OWN]
        pos = lo + np.arange(WIN)
        valid = (pos >= 0).astype(np.float32).reshape(NT, 128).T
        inv = np.zeros((128, 2, 16), np.float32)
        for p in range(128):
            for t in range(2):
                w = wins[2 * t + (1 if p >= 64 else 0)]
                inv[p, t, :] = 1.0 / np.minimum(t0 + np.arange(16) + 1.0, w)
        m = dict(shared)
        m["xw"] = xw
        m["valid"] = f(valid)
        m["flag"] = np.full((128, 1), 0.0 if c == 0 else 1.0, np.float32)
        m["invcnt"] = f(inv.reshape(128, 32))
        maps.append(m)
    return maps


_NC_CACHE = {}


def kernel(**inputs):
    if "nc" not in _NC_CACHE:
        _NC_CACHE["nc"] = build()
    nc = _NC_CACHE["nc"]
    maps = make_in_maps(inputs)
    res = bu.run_bass_kernel_spmd(nc, maps, core_ids=list(range(NCORES)))
    out = np.concatenate([np.asarray(r["out"], dtype=np.float32) for r in res.results], axis=0)
    return out.reshape(1, SEQ, D)
```

```python
import numpy as np
import concourse.bass as bass
import concourse.mybir as mybir
import concourse.bass_utils as bu

F32 = mybir.dt.float32
BF16 = mybir.dt.bfloat16
AF = mybir.ActivationFunctionType
ALU = mybir.AluOpType

NCORES = 8
SEQ = 16384
D = 1024
OWN = SEQ // NCORES
HALO = 640
WIN = OWN + HALO
NT = WIN // 128
Q0 = 4
NQ = NT - Q0
NQTOK = NQ * 128
DFF = 2816
EPS = 1e-6
NEG = -30000.0
NDMA_SEM = {"sp": 36, "pool": 20}


class Op:
    __slots__ = ("eng", "fn", "deps", "idx", "dma", "signal", "sig", "sem", "tag")


class Sched:
    ENG = ("pe", "act", "dve", "pool", "sp")

    def __init__(self, nc):
        self.nc = nc
        self.q = {e: [] for e in self.ENG}
        self.lw = {}
        self.rd = {}
        self.base = {}
        self.names = {}
        self.dma_list = []
        self.dma_q = {}

    def add(self, eng, fn, reads=(), writes=(), deps=(), dma=False, tag=None):
        o = Op()
        o.eng = eng; o.fn = fn; o.dma = dma; o.signal = False; o.sig = 0; o.sem = None; o.tag = tag
        o.idx = len(self.q[eng])
        d = {}
        for x in deps:
            d[x] = True
        for k in reads:
            self.names.setdefault(k[0], set()).add(k)
            w = self.lw.get(k)
            if w is not None:
                d[w] = True
            for x in self.base.get(k[0], ()):
                d.setdefault(x, False)
        for k in writes:
            self.names.setdefault(k[0], set()).add(k)
            w = self.lw.get(k)
            if w is not None:
                d.setdefault(w, False)
            for r in self.rd.get(k, {}).values():
                d.setdefault(r, False)
            for x in self.base.get(k[0], ()):
                d.setdefault(x, False)
        for k in reads:
            slot = self.rd.setdefault(k, {})
            slot[("dma", len(self.dma_list)) if dma else eng] = o
        for k in writes:
            self.lw[k] = o
            self.rd[k] = {}
        if dma:
            lst = self.dma_q.setdefault(eng, [])
            n = len(lst)
            R = NDMA_SEM[eng]
            if n >= R:
                d[lst[n - R]] = True
            o.sem = (eng, n % R)
            o.sig = 16 * (n // R + 1)
            lst.append(o)
            self.dma_list.append(o)
        d.pop(o, None)
        o.deps = d
        self.q[eng].append(o)
        return o

    def frontier(self, name):
        best = {}
        out = []
        for k in self.names.get(name, ()):
            cands = []
            w = self.lw.get(k)
            if w is not None:
                cands.append(w)
            cands.extend(self.rd.get(k, {}).values())
            for o in cands:
                if o.dma:
                    out.append(o)
                else:
                    b = best.get(o.eng)
                    if b is None or o.idx > b.idx:
                        best[o.eng] = o
        out.extend(best.values())
        out.extend(self.base.get(name, ()))
        return out

    STRICT = True

    @staticmethod
    def _needs_wait(o, d, raw):
        if d.dma or o.dma:
            return True
        if d.eng != o.eng:
            return True
        if o.eng == "pe":
            return False
        return raw or Sched.STRICT

    def emit(self):
        nc = self.nc
        for e in self.ENG:
            for o in self.q[e]:
                for d, raw in o.deps.items():
                    if self._needs_wait(o, d, raw):
                        d.signal = True
        for e in self.ENG:
            c = 0
            for o in self.q[e]:
                if not o.dma and o.signal:
                    c += 1
                    o.sig = c
        from contextlib import ExitStack
        with ExitStack() as st:
            esem = {e: st.enter_context(nc.semaphore("s_" + e)) for e in self.ENG}
            dsem = {(q_, i): st.enter_context(nc.semaphore("d%s%d" % (q_, i))) for q_, r_ in NDMA_SEM.items() for i in range(r_)}
            block = st.enter_context(nc.Block())

            def run(e, eng):
                waited = {}
                for o in self.q[e]:
                    for d, raw in o.deps.items():
                        if not self._needs_wait(o, d, raw):
                            continue
                        sem = dsem[d.sem] if d.dma else esem[d.eng]
                        key = ("d", d.sem) if d.dma else d.eng
                        if waited.get(key, 0) >= d.sig:
                            continue
                        eng.wait_ge(sem, d.sig)
                        waited[key] = d.sig
                    ins = o.fn(eng)
                    if o.dma:
                        ins.then_inc(dsem[o.sem], 16)
                    elif o.signal:
                        ins.then_inc(esem[e], 1)

            @block.tensor
            def _(eng):
                run("pe", eng)

            @block.scalar
            def _(eng):
                run("act", eng)

            @block.vector
            def _(eng):
                run("dve", eng)

            @block.gpsimd
            def _(eng):
                run("pool", eng)

            @block.sync
            def _(eng):
                run("sp", eng)


class Mem:
    def __init__(self, nc, sched, lo=16512, hi=229376 - 128):
        self.nc = nc; self.S = sched
        self.free = [[lo, hi, ()]]
        self.live = {}
        self.uid = 0

    def alloc(self, name, shape, dtype, top=False):
        nbytes = int(np.prod(shape[1:])) * mybir.dt.size(dtype)
        nbytes = (nbytes + 63) // 64 * 64
        order = range(len(self.free) - 1, -1, -1) if top else range(len(self.free))
        for i in order:
            s, e, fr = self.free[i]
            if e - s >= nbytes:
                if top:
                    a = e - nbytes
                    self.free[i] = [s, a, fr]
                else:
                    a = s
                    self.free[i] = [s + nbytes, e, fr]
                if self.free[i][0] == self.free[i][1]:
                    del self.free[i]
                self.uid += 1
                t = self.nc.alloc_sbuf_tensor_at("%s_%d" % (name, self.uid), list(shape), dtype, offset=a)
                self.live[name] = (a, a + nbytes)
                if fr:
                    self.S.base[name] = tuple(fr)
                return t
        raise RuntimeError("SBUF OOM for %s (%d B); free=%s" % (name, nbytes, [(s, e) for s, e, _ in self.free]))

    def release(self, name):
        a, b = self.live.pop(name)
        fr = tuple(self.S.frontier(name))
        self.free.append([a, b, fr])
        self.free.sort(key=lambda r: r[0])
        merged = []
        for r in self.free:
            if merged and merged[-1][1] == r[0]:
                merged[-1] = [merged[-1][0], r[1], tuple(set(merged[-1][2]) | set(r[2]))]
            else:
                merged.append(r)
        self.free = merged
        for k in self.S.names.pop(name, ()):
            self.S.lw.pop(k, None)
            self.S.rd.pop(k, None)
        self.S.base.pop(name, None)


def build(debug=None):
    nc = bass.Bass("TRN2", target_bir_lowering=False)
    S = Sched(nc)
    M = Mem(nc, S)

    def din(name, shape):
        return nc.dram_tensor(name, list(shape), F32, kind="ExternalInput").ap()

    xw = din("xw", [WIN, D])
    memd = din("mem", [256, D])
    w_in = din("w_in", [D, 5120])
    bgate_d = din("bgate", [128, 24])
    wpool_d = din("w_pool", [4, 64, 64])
    pscale_d = din("pscale", [128, 2])
    biasT_d = din("biasT", [128, 8 * 385])
    w_mkv = din("w_mem_kv", [D, 512])
    w_upp_d = din("w_up_pool", [256, D])
    w_upa_d = din("w_up_attn", [512, D])
    w_upm_d = din("w_up_mem", [256, D])
    w_out_d = din("w_out", [D, D])
    w_fu_d = din("w_ffn_up", [D, 2 * DFF])
    convw_d = din("convw", [128, 44 * 3])
    convb_d = din("convb", [128, 44])
    w_fd_d = din("w_ffn_down", [DFF, D])
    gmix_d = din("g_mix_b", [128, D])
    gmem_d = din("g_mem_b", [128, D])
    gffn_d = din("g_ffn_b", [128, D])
    gfin_d = din("g_fin_b", [128, D])
    valid_d = din("valid", [128, NT])
    flag_d = din("flag", [128, 1])
    invcnt_d = din("invcnt", [128, 32])
    ident_d = din("ident", [128, 128])
    out_d = nc.dram_tensor("out", [OWN, D], F32, kind="ExternalOutput").ap()
    xmid_d = nc.dram_tensor("xmid_scr", [OWN, D], F32, kind="Internal").ap()
    wfu_bf = nc.dram_tensor("wfu_bf16", [D, 2 * DFF], BF16, kind="Internal").ap()
    wfd_bf = nc.dram_tensor("wfd_bf16", [DFF, D], BF16, kind="Internal").ap()
    dbg_d = {}
    if debug:
        for nm, shp in debug.items():
            if nm.startswith("_"):
                continue
            dbg_d[nm] = nc.dram_tensor("dbg_" + nm, list(shp), F32, kind="ExternalOutput").ap()

    from contextlib import ExitStack
    ps_state = {"stack": None, "names": [], "frontier": ()}

    def ps_open():
        ps_state["stack"] = ExitStack()
        ps_state["names"] = []

    def ps_alloc(name, shape, dtype=F32):
        t = ps_state["stack"].enter_context(nc.psum_tensor(name, list(shape), dtype))
        ps_state["names"].append(name)
        if ps_state["frontier"]:
            S.base[name] = tuple(ps_state["frontier"])
        return t

    def ps_close():
        fr = []
        for nm in ps_state["names"]:
            fr.extend(S.frontier(nm))
        ps_state["frontier"] = tuple(set(fr))
        ps_state["stack"].close()

    def dma(eng, out_ap, in_ap, reads=(), writes=(), tag=None):
        return S.add(eng, lambda e: e.dma_start(out=out_ap, in_=in_ap), reads=reads, writes=writes, dma=True, tag=tag)

    rr = {"evac": 0}

    def evac_copy(out_ap, in_ap, reads, writes, eng=None):
        if eng is None:
            eng = ("act", "dve")[rr["evac"] % 2]
            rr["evac"] += 1
        if eng == "act":
            return S.add("act", lambda e: e.activation(out=out_ap, in_=in_ap, func=AF.Copy), reads=reads, writes=writes)
        return S.add(eng, lambda e: e.tensor_copy(out=out_ap, in_=in_ap), reads=reads, writes=writes)

    ident = M.alloc("ident", [128, 128], BF16)
    id32 = M.alloc("id32", [128, 128], F32)
    def _ident_load():
        dma("sp", id32[:], ident_d, writes=[("id32",)])
        S.add("dve", lambda e: e.tensor_copy(out=ident[:], in_=id32[:]), reads=[("id32",)], writes=[("ident",)])
    const_dmas = [_ident_load]
    g_a = M.alloc("g_a", [128, D], F32)
    g_b = M.alloc("g_b", [128, D], F32)
    const_dmas.insert(0, lambda: dma("sp", g_a[:], gmix_d, writes=[("g_a",)]))
    const_dmas.append(lambda: dma("sp", g_b[:], gmem_d, writes=[("g_b",)]))
    valid = M.alloc("valid", [128, NT], F32)
    const_dmas.append(lambda: dma("sp", valid[:], valid_d, writes=[("valid",)]))
    flag = M.alloc("flag", [128, 1], F32)
    const_dmas.append(lambda: dma("sp", flag[:], flag_d, writes=[("flag",)]))
    ss = M.alloc("ss", [128, 8], F32)
    rstd_x = M.alloc("rstd_x", [128, NT], F32)
    mhalf = M.alloc("mhalf", [128, 1], F32)
    S.add("pool", lambda e: e.memset(mhalf[:], -0.5), writes=[("mhalf",)])
    S.add("act", lambda e: e.activation(out=junk[:, 0:1], in_=mhalf[:, 0:1], func=AF.Square), reads=[("mhalf",)], writes=[("junk",)])
    junk = M.alloc("junk", [128, D], BF16)
    xs = M.alloc("xs", [128, 5, D], F32)
    xs_glob = xs
    HB_N = 3
    hb = M.alloc("hb", [128, HB_N, D], BF16)
    h2halo = M.alloc("h2halo", [128, 8, 2], BF16)
    convw = M.alloc("convw", [128, 44, 3], F32)
    convb = M.alloc("convb", [128, 44], F32)
    bias8 = M.alloc("bias8", [128, 8, 385], F32)
    XS_N = 5
    ctr = {"xs": 0, "hb": 0, "ss": 0, "psT": 0, "pj": 0}

    w_a = M.alloc("w_a", [128, 8, 2048], BF16)
    kT = M.alloc("kT", [128, 4, WIN], BF16, top=True)
    Vb = M.alloc("V", [128, NT, 8, 65], BF16, top=True)
    qT = M.alloc("qT", [128, 4, NQTOK], BF16, top=True)
    qmT = M.alloc("qmT", [128, 2, NQTOK], BF16, top=True)
    ypT = M.alloc("ypT", [128, 2, NQTOK], BF16, top=True)
    kmT = M.alloc("kmT", [128, 2, 256], BF16, top=True)
    vm = M.alloc("vm", [128, 2, 4, 65], BF16, top=True)
    hT = M.alloc("hT", [128, 2, 8, 512], BF16)
    w_mk = M.alloc("w_mk", [128, 8, 512], BF16)
    wpbd = M.alloc("wpbd", [128, 2, 128], BF16)
    pscale = M.alloc("pscale", [128, 2], F32)
    invcnt = M.alloc("invcnt", [128, 2, 16], F32)
    ub = M.alloc("ub", [128, 2, 16 + 512], F32)
    sA = M.alloc("sA", [128, 2, 16 + 512], F32)
    sB = M.alloc("sB", [128, 2, 16 + 512], F32)
    pT = M.alloc("pT", [128, 2, 512], BF16)
    ptmp = M.alloc("ptmp", [128, 16], F32)

    w_in_v = w_in.rearrange("(kc p) c -> p kc c", p=128)
    for c0 in (0, 512, 1024, 1536):
        dma("pool", w_a[:, :, c0:c0 + 512], w_in_v[:, :, c0:c0 + 512],
            writes=[("w_a", c0 // 256), ("w_a", c0 // 256 + 1)])
    wp32 = M.alloc("wp32", [128, 2, 128], F32)
    wpbd_keys = [("wpbd",)]
    S.add("pool", lambda e: e.memset(wp32[:], 0.0), writes=[("wp32",)])

    def late_setup():
        dma("sp", bias8[:], biasT_d.rearrange("p (h c) -> p h c", h=8), writes=[("bias8",)])
        for h in range(8):
            S.add("dve", lambda e, h=h: e.tensor_scalar(
                out=bias8[:, h, 0:384], in0=bias8[:, h, 0:384], scalar1=bias8[:, h, 384:385], scalar2=8.0,
                op0=ALU.subtract, op1=ALU.mult),
                reads=[("bias8",)], writes=[("bias8",)])
        S.add("pool", lambda e: e.memset(bias8[0:64, :, 64:128], 8.0 * NEG), reads=[("bias8",)], writes=[("bias8",)])
        S.add("pool", lambda e: e.memset(bias8[64:128, :, 256:320], 8.0 * NEG), reads=[("bias8",)], writes=[("bias8",)])
        dma("sp", convw[:], convw_d.rearrange("p (j t) -> p j t", t=3), writes=[("convw",)])
        dma("sp", convb[:], convb_d, writes=[("convb",)])
        dma("pool", w_mk[:], w_mkv.rearrange("(kc p) c -> p kc c", p=128), writes=[("w_mk",)])
        dma("sp", pscale[:], pscale_d, writes=[("pscale",)])
        dma("sp", invcnt[:], invcnt_d.rearrange("p (t c) -> p t c", t=2), writes=[("invcnt",)])
        for g in range(4):
            t, hlf = g // 2, g % 2
            dma("sp", wp32[hlf * 64:(hlf + 1) * 64, t, hlf * 64:(hlf + 1) * 64], wpool_d[g], reads=[("wp32",)],
                writes=[("wp32", g)])
        S.add("dve", lambda e: e.tensor_copy(out=wpbd[:], in_=wp32[:]),
              reads=[("wp32", g) for g in range(4)], writes=[("wpbd",)])
        for h in range(8):
            S.add("pool", lambda e, h=h: e.tensor_copy(out=Vb[:, :, h, 64:65], in_=valid[:, :].unsqueeze(2)),
                  reads=[("valid",)], writes=[("Vones", h)])
        S.add("pool", lambda e: e.memset(vm[:, :, :, 64:65], 1.0), writes=[("vmones",)])
        S.add("pool", lambda e: e.memset(ub[:], 0.0), writes=[("ub",)])

    ps_open()
    psT = [ps_alloc("psT0", [128, 8, 128], BF16), ps_alloc("psT1", [128, 8, 128], BF16)]
    pj = [ps_alloc("pj%d" % i, [128, 512], F32) for i in range(4)]

    def pskey(kind, i):
        return ("%s%d" % (kind, i),)

    def norm_stages(load_fn, gname, gtile, xbuf, slot_fn, ps_fn, dst_fn, dst_key, save=None, reuse=None):
        xs, xname = xbuf
        st = {}

        def n0():
            st["s"] = slot_fn()
            if load_fn is not None:
                load_fn(st["s"])

        def n1():
            if reuse is not None:
                return
            s = st["s"]
            c = ctr["ss"] % 8; ctr["ss"] += 1
            st["c"] = c
            S.add("act", lambda e: e.activation(out=junk[:], in_=xs[:, s, :], func=AF.Square, scale=1.0 / 32.0,
                                                accum_out=ss[:, c:c + 1]),
                  reads=[(xname, s)], writes=[("junk",), ("ss", c)])
            S.add("pool", lambda e: e.tensor_scalar(out=ss[:, c:c + 1], in0=ss[:, c:c + 1], scalar1=EPS, scalar2=None,
                                                    op0=ALU.add),
                  reads=[("ss", c)], writes=[("ss", c)])
            if save is not None:
                sv_ap, sv_key = save
                S.add("pool", lambda e: e.tensor_tensor(out=sv_ap, in0=ss[:, c:c + 1], in1=mhalf[:, 0:1], op=ALU.pow),
                      reads=[("ss", c), ("mhalf",)], writes=[sv_key])
            else:
                S.add("pool", lambda e: e.tensor_tensor(out=ss[:, c:c + 1], in0=ss[:, c:c + 1], in1=mhalf[:, 0:1], op=ALU.pow),
                      reads=[("ss", c), ("mhalf",)], writes=[("ss", c)])

        def n2():
            s = st["s"]
            if reuse is not None:
                sc_ap, sc_key = reuse
            elif save is not None:
                sc_ap, sc_key = save
            else:
                c = st["c"]
                sc_ap, sc_key = ss[:, c:c + 1], ("ss", c)
            hs = ctr["hb"] % HB_N; ctr["hb"] += 1
            st["hs"] = hs
            S.add("dve", lambda e: e.scalar_tensor_tensor(out=hb[:, hs, :], in0=xs[:, s, :], scalar=sc_ap,
                                                          in1=gtile[:], op0=ALU.mult, op1=ALU.mult),
                  reads=[(xname, s), sc_key, (gname,)], writes=[("hb", hs)])

        def n3():
            hs = st["hs"]
            ps_ap, ps_key = ps_fn()

            def tr(e):
                ins = None
                for kc in range(8):
                    ins = e.transpose(ps_ap[:, kc, :], hb[:, hs, kc * 128:(kc + 1) * 128], ident[:])
                return ins
            S.add("pe", tr, reads=[("hb", hs), ("ident",)], writes=[ps_key])
            dst_fn(ps_ap, ps_key, dst_key)
        return [n0, n1, n2, n3], st

    def wavefront(stage_lists):
        out = []
        nt_ = len(stage_lists)
        ns_ = max(len(s_) for s_ in stage_lists) if stage_lists else 0
        for d_ in range(nt_ + ns_ - 1):
            for k_ in range(ns_):
                t_ = d_ - k_
                if 0 <= t_ < nt_ and k_ < len(stage_lists[t_]):
                    out.append(stage_lists[t_][k_])
        return out

    def norm_s1(src_rows_ap, gname, gtile, loaded=None, xbuf=None):
        xs, xname = xbuf if xbuf is not None else (xs_glob, "xs")
        if loaded is None:
            s = ctr["xs"] % XS_N; ctr["xs"] += 1
            dma("sp", xs[:, s, :], src_rows_ap, writes=[(xname, s)])
        else:
            s = loaded
        c = ctr["ss"] % 8; ctr["ss"] += 1
        S.add("act", lambda e: e.activation(out=junk[:], in_=xs[:, s, :], func=AF.Square, scale=1.0 / 32.0,
                                            accum_out=ss[:, c:c + 1]),
              reads=[(xname, s)], writes=[("junk",), ("ss", c)])
        S.add("pool", lambda e: e.tensor_scalar(out=ss[:, c:c + 1], in0=ss[:, c:c + 1], scalar1=EPS, scalar2=None,
                                                op0=ALU.add),
              reads=[("ss", c)], writes=[("ss", c)])
        S.add("pool", lambda e: e.tensor_tensor(out=ss[:, c:c + 1], in0=ss[:, c:c + 1], in1=mhalf[:, 0:1], op=ALU.pow),
              reads=[("ss", c), ("mhalf",)], writes=[("ss", c)])
        hs = ctr["hb"] % HB_N; ctr["hb"] += 1
        S.add("dve", lambda e: e.scalar_tensor_tensor(out=hb[:, hs, :], in0=xs[:, s, :], scalar=ss[:, c:c + 1],
                                                      in1=gtile[:], op0=ALU.mult, op1=ALU.mult),
              reads=[(xname, s), ("ss", c), (gname,)], writes=[("hb", hs)])
        return (s, hs)

    def norm_s2(ctx, ps_ap, ps_key, dst_fn, dst_key):
        s, hs = ctx

        def tr(e):
            ins = None
            for kc in range(8):
                ins = e.transpose(ps_ap[:, kc, :], hb[:, hs, kc * 128:(kc + 1) * 128], ident[:])
            return ins
        S.add("pe", tr, reads=[("hb", hs), ("ident",)], writes=[ps_key])
        dst_fn(ps_ap, ps_key, dst_key)

    def norm_tile(src_rows_ap, gname, gtile, psT_, dst_fn, dst_key, src_is_loaded=None, pname="psT", xbuf=None):
        ctx = norm_s1(src_rows_ap, gname, gtile, loaded=src_is_loaded, xbuf=xbuf)
        pt = ctr["psT"] % 2; ctr["psT"] += 1
        norm_s2(ctx, psT_[pt], pskey(pname, pt), dst_fn, dst_key)
        return ctx[0]

    def mem_path():
        memT = M.alloc("memT", [128, 8, 256], BF16)
        for mt in range(2):
            def dst(ps, pskey_, dkey, mt=mt):
                evac_copy(memT[:, :, mt * 128:(mt + 1) * 128], ps[:], [pskey_], [dkey])
            norm_tile(memd[mt * 128:(mt + 1) * 128, :], "g_b", g_b, psT, dst, ("memT", mt))
        for pr in range(2):
            pi = ctr["pj"] % 4; ctr["pj"] += 1

            def mm(e, pr=pr, pi=pi):
                ins = None
                for kc in range(8):
                    ins = e.matmul(pj[pi][:, 0:256], lhsT=w_mk[:, kc, pr * 128:(pr + 1) * 128], rhs=memT[:, kc, :],
                                   start=(kc == 0), stop=(kc == 7))
                return ins
            S.add("pe", mm, reads=[("w_mk",), ("memT", 0), ("memT", 1)], writes=[pskey("pj", pi)])
            evac_copy(kmT[:, pr, :], pj[pi][:, 0:256], [pskey("pj", pi)], [("kmT", pr)])
        for mt in range(2):
            pi = ctr["pj"] % 4; ctr["pj"] += 1

            def mm(e, mt=mt, pi=pi):
                ins = None
                for kc in range(8):
                    ins = e.matmul(pj[pi][:, 0:256], lhsT=memT[:, kc, mt * 128:(mt + 1) * 128], rhs=w_mk[:, kc, 256:512],
                                   start=(kc == 0), stop=(kc == 7))
                return ins
            S.add("pe", mm, reads=[("w_mk",), ("memT", mt)], writes=[pskey("pj", pi)])
            evac_copy(vm[:, mt, :, 0:64], pj[pi][:, 0:256].rearrange("p (h d) -> p h d", d=64),
                      [pskey("pj", pi)], [("vm", mt)])
        dma("sp", g_b[:], gffn_d, reads=[], writes=[("g_b",)])


    groups = [[0, 1, 2, 3], [4, 5, 6, 7], [8, 9, 10, 11], [12, 13, 14, 15], [16, 17, 18, 19], [20]]
    groups_C = [[5, 6, 7, 8], [4], [9, 10, 11, 12], [13, 14, 15, 16], [17, 18, 19, 20]]

    def tile_T_stages(i, gb, j):
        def slot_fn():
            s = ctr["xs"] % XS_N; ctr["xs"] += 1
            return s

        def load_fn(s):
            dma("sp", xs[:, s, :], xw[i * 128:(i + 1) * 128, :], writes=[("xs", s)])

        def ps_fn():
            pt = ctr["psT"] % 2; ctr["psT"] += 1
            return psT[pt], pskey("psT", pt)

        def dst(ps, pskey_, dkey):
            evac_copy(hT[:, gb, :, j * 128:(j + 1) * 128], ps[:], [pskey_], [dkey], eng="dve")
        stages, _ = norm_stages(load_fn, "g_a", g_a, (xs, "xs"), slot_fn, ps_fn, dst, ("hT", gb, j),
                                save=(rstd_x[:, i:i + 1], ("rstd_x", i)))
        return stages

    def staged_order(stage_pairs, ahead=2):
        out = []
        n_ = len(stage_pairs)
        for k_ in range(n_ + ahead):
            if k_ < n_:
                out.append(stage_pairs[k_][0])
            if k_ - ahead >= 0:
                out.append(stage_pairs[k_ - ahead][1])
        return out

    def proj_fm(col0, n, gb, ntiles, wkey, evac):
        pi = ctr["pj"] % 4; ctr["pj"] += 1

        def mm(e):
            ins = None
            for kc in range(8):
                ins = e.matmul(pj[pi][:, 0:n], lhsT=w_a[:, kc, col0:col0 + 128], rhs=hT[:, gb, kc, 0:n],
                               start=(kc == 0), stop=(kc == 7))
            return ins
        S.add("pe", mm, reads=[wkey] + [("hT", gb, j) for j in range(ntiles)], writes=[pskey("pj", pi)])
        evac(pj[pi], pskey("pj", pi))

    deferred = []
    pre_B = {}

    def early_B():
        M.release("w_a")

    def group_items(gi, tiles):
        gb = gi % 2
        n = 128 * len(tiles)
        ntl = len(tiles)
        tok0 = tiles[0] * 128
        isq = gi >= 1
        q0 = (tiles[0] - Q0) * 128
        L = 16 + n
        items = []

        def u_item(pr):
            if isq:
                proj_fm(pr * 128, n, gb, ntl, ("w_a", 0),
                        lambda ps, pk: evac_copy(ub[:, pr, 16:16 + n], ps[:, 0:n], [pk], [("ub",)], eng="act"))
            else:
                proj_fm(pr * 128, n, gb, ntl, ("w_a", 0),
                        lambda ps, pk: evac_copy(ub[:, pr, 0:16], ps[:, n - 16:n], [pk], [("ub",)], eng="act"))

        def k_item(pr):
            c0 = 768 + pr * 128
            proj_fm(c0, n, gb, ntl, ("w_a", c0 // 256),
                    lambda ps, pk: evac_copy(kT[:, pr, tok0:tok0 + n], ps[:, 0:n], [pk], [("kT", i) for i in tiles], eng="act"))

        def v_item(j, i):
            pi = ctr["pj"] % 4; ctr["pj"] += 1

            def mm(e):
                ins = None
                for kc in range(8):
                    ins = e.matmul(pj[pi][:, 0:512], lhsT=hT[:, gb, kc, j * 128:(j + 1) * 128],
                                   rhs=w_a[:, kc, 1280:1792], start=(kc == 0), stop=(kc == 7))
                return ins
            S.add("pe", mm, reads=[("w_a", 5), ("w_a", 6), ("hT", gb, j)], writes=[pskey("pj", pi)])
            evac_copy(Vb[:, i, :, 0:64], pj[pi][:, 0:512].rearrange("p (h d) -> p h d", d=64),
                      [pskey("pj", pi)], [("V", i)], eng="act")

        def q_item(pr):
            c0 = 256 + pr * 128
            proj_fm(c0, n, gb, ntl, ("w_a", c0 // 256),
                    lambda ps, pk: evac_copy(qT[:, pr, q0:q0 + n], ps[:, 0:n], [pk], [("qT", i) for i in tiles], eng="act"))

        def qm_item(pr):
            c0 = 1792 + pr * 128
            proj_fm(c0, n, gb, ntl, ("w_a", 7),
                    lambda ps, pk: evac_copy(qmT[:, pr, q0:q0 + n], ps[:, 0:n], [pk], [("qmT", i) for i in tiles], eng="act"))

        def pool_mixer_ops():
            ops = []
            ops.append(lambda: S.add("dve", lambda e: e.tensor_tensor(out=sA[:, :, 1:L], in0=ub[:, :, 1:L], in1=ub[:, :, 0:L - 1], op=ALU.add),
                                     reads=[("ub",)], writes=[("sA",)]))
            ops.append(lambda: S.add("dve", lambda e: e.tensor_tensor(out=sB[:, :, 3:L], in0=sA[:, :, 3:L], in1=sA[:, :, 1:L - 2], op=ALU.add),
                                     reads=[("sA",)], writes=[("sB",)]))
            ops.append(lambda: S.add("dve", lambda e: e.tensor_tensor(out=sA[:, 1, 7:L], in0=sB[:, 1, 7:L], in1=sB[:, 1, 3:L - 4], op=ALU.add),
                                     reads=[("sB",)], writes=[("sA",)]))
            ops.append(lambda: S.add("dve", lambda e: e.tensor_tensor(out=sB[:, 1, 15:L], in0=sA[:, 1, 15:L], in1=sA[:, 1, 7:L - 8], op=ALU.add),
                                     reads=[("sA",)], writes=[("sB",)]))
            srcs = [(sA, 0, 0, 2.0), (sB, 0, 1, 4.0), (sA, 1, 0, 8.0), (sB, 1, 1, 16.0)]
            for (sbuf_, t, hlf, w) in srcs:
                p0, p1 = hlf * 64, hlf * 64 + 64
                kk = ("sA",) if sbuf_ is sA else ("sB",)
                ops.append(lambda sbuf_=sbuf_, t=t, p0=p0, p1=p1, w=w, kk=kk: S.add("dve", lambda e: e.scalar_tensor_tensor(
                    out=pT[p0:p1, t, 0:n], in0=sbuf_[p0:p1, t, 16:L], scalar=1.0 / w, in1=ub[p0:p1, t, 16:L],
                    op0=ALU.mult, op1=ALU.subtract),
                    reads=[kk, ("ub",)], writes=[("pT",)]))
            if 5 in tiles:
                fo = (5 - tiles[0]) * 128

                def fix():
                    for (sbuf_, t, hlf, w) in srcs:
                        p0, p1 = hlf * 64, hlf * 64 + 64
                        kk = ("sA",) if sbuf_ is sA else ("sB",)
                        S.add("dve", lambda e, sbuf_=sbuf_, t=t, p0=p0, p1=p1: e.tensor_tensor(
                            out=ptmp[p0:p1, :], in0=sbuf_[p0:p1, t, 16 + fo:32 + fo], in1=invcnt[p0:p1, t, :], op=ALU.mult),
                            reads=[kk, ("invcnt",)], writes=[("ptmp",)])
                        S.add("dve", lambda e, t=t, p0=p0, p1=p1: e.tensor_tensor(
                            out=pT[p0:p1, t, fo:fo + 16], in0=ptmp[p0:p1, :], in1=ub[p0:p1, t, 16 + fo:32 + fo], op=ALU.subtract),
                            reads=[("ptmp",), ("ub",)], writes=[("pT",)])
                ops.append(fix)
            ops.append(lambda: S.add("dve", lambda e: e.tensor_copy(out=ub[:, :, 0:16], in_=ub[:, :, n:n + 16]),
                                     reads=[("ub",)], writes=[("ub",)]))
            return ops

        def pool_mm(t):
            pi = ctr["pj"] % 4; ctr["pj"] += 1
            S.add("pe", lambda e: e.matmul(pj[pi][:, 0:n], lhsT=wpbd[:, t, :], rhs=pT[:, t, 0:n], start=True, stop=True),
                  reads=[("pT",)] + wpbd_keys, writes=[pskey("pj", pi)])
            S.add("act", lambda e: e.activation(out=ypT[:, t, q0:q0 + n], in_=pj[pi][:, 0:n], func=AF.Copy,
                                                scale=pscale[:, t:t + 1]),
                  reads=[pskey("pj", pi), ("pscale",)], writes=[("ypT", t, q0)])

        for pr in range(2):
            items.append(lambda pr=pr: u_item(pr))
        for pr in range(4):
            items.append(lambda pr=pr: k_item(pr))
        mix = pool_mixer_ops() if isq else []
        for j, i in enumerate(tiles):
            items.append(lambda j=j, i=i: v_item(j, i))
            if mix:
                items.append(mix.pop(0))
        if isq:
            for pr in range(4):
                items.append(lambda pr=pr: q_item(pr))
                if mix:
                    items.append(mix.pop(0))
            for pr in range(2):
                items.append(lambda pr=pr: qm_item(pr))
                if mix:
                    items.append(mix.pop(0))
            items.extend(mix)
            if gi == len(groups) - 1:
                items.append(early_B)
            for t in range(2):
                deferred.append(lambda t=t: pool_mm(t))
        return items

    T_items = [wavefront([tile_T_stages(i, gi % 2, j) for j, i in enumerate(tiles)])
               for gi, tiles in enumerate(groups)]
    T_items[0][0]()
    const_dmas.pop(0)()
    T_items[0][1]()
    for fn_ in const_dmas:
        fn_()
    for th in T_items[0][2:]:
        th()
    late_setup()
    for gi, tiles in enumerate(groups):
        if gi == 1:
            mem_path()
        carry = list(deferred)
        del deferred[:]
        P = group_items(gi, tiles)
        P[6:6] = carry
        T = T_items[gi + 1] if gi + 1 < len(groups) else []
        ti = 0
        for k_, p in enumerate(P):
            p()
            want = min(len(T), (len(T) * (k_ + 1) * 10) // (len(P) * 6))
            while ti < want:
                T[ti](); ti += 1
        while ti < len(T):
            T[ti](); ti += 1
    for th in deferred:
        th()

    ps_close()
    for nm in ("hT", "w_mk", "wpbd", "wp32", "id32", "pscale", "invcnt", "ub", "sA", "sB", "pT", "ptmp", "memT"):
        M.release(nm)

    stop_after = (debug or {}).get("_stop", "D")
    if debug and stop_after == "A":
        stg = M.alloc("dbgstg", [128, NT * 8 * 65], F32)
        if "kT" in debug:
            S.add("dve", lambda e: e.tensor_copy(out=stg[:, 0:4 * WIN], in_=kT[:].rearrange("p a b -> p (a b)")),
                  reads=[("kT", i) for i in range(NT)], writes=[("dbgstg",)])
            dma("sp", dbg_d["kT"], stg[:, 0:4 * WIN], reads=[("dbgstg",)], tag="out")
        if "ypT" in debug:
            S.add("dve", lambda e: e.tensor_copy(out=stg[:, 0:2 * NQTOK], in_=ypT[:].rearrange("p a b -> p (a b)")),
                  reads=[("ypT", t, (g[0] - Q0) * 128) for t in range(2) for g in groups[1:]], writes=[("dbgstg",)])
            dma("sp", dbg_d["ypT"], stg[:, 0:2 * NQTOK], reads=[("dbgstg",)], tag="out")
        if "V" in debug:
            S.add("dve", lambda e: e.tensor_copy(out=stg[:], in_=Vb[:].rearrange("p a b c -> p (a b c)")),
                  reads=[("V", i) for i in range(NT)] + [("Vones", h) for h in range(8)], writes=[("dbgstg",)])
            dma("sp", dbg_d["V"], stg[:], reads=[("dbgstg",)], tag="out")

    def phase_B():
        BPOS = {0: 0, 3: 1, 4: 2, 1: 3, 2: 4}
        OT = M.alloc("OT", [128, 6, NQTOK], BF16)
        PT = M.alloc("PT", [128, 4, 640], BF16)
        PmT = M.alloc("PmT", [128, 4, 256], BF16)
        On = M.alloc("On", [128, 2, 12, 64], BF16)
        den = M.alloc("den", [128, 2, 12, 1], F32)
        w_upp = M.alloc("w_upp", [128, 2, D], BF16)
        w_upa = M.alloc("w_upa", [128, 4, D], BF16)
        w_upm = M.alloc("w_upm", [128, 2, D], BF16)
        dma("pool", w_upp[:], w_upp_d.rearrange("(kc p) c -> p kc c", p=128), writes=[("w_upp",)])
        dma("pool", w_upa[:], w_upa_d.rearrange("(kc p) c -> p kc c", p=128), writes=[("w_upa",)])
        dma("pool", w_upm[:], w_upm_d.rearrange("(kc p) c -> p kc c", p=128), writes=[("w_upm",)])
        M.release("xs")
        w_out = M.alloc("w_out", [128, 8, D], BF16)
        dma("pool", w_out[:], w_out_d.rearrange("(kc p) c -> p kc c", p=128), writes=[("w_out",)])
        pre_B["mg"] = M.alloc("mg", [128, 1, 8, 512], BF16)
        pre_B["tb"] = M.alloc("tb", [128, 4, 512], F32)
        w_g = {}
        for br in range(2):
            try:
                t_ = M.alloc("w_g%d_0" % br, [128, 8, 512], BF16)
            except RuntimeError:
                break
            c0 = 2048 + br * 1024
            dma("pool", t_[:], w_in_v[:, :, c0:c0 + 512], writes=[("w_g%d_0" % br,)])
            for nt in range(4):
                w_g[(br, nt)] = (t_, "w_g%d_0" % br, nt * 128)

        for kc in range(8):
            dma("pool", wfu_bf[kc * 128:(kc + 1) * 128, :], w_fu_d[kc * 128:(kc + 1) * 128, :], writes=[("wfu_bf", kc)])
        for rb in range(8):
            dma("pool", wfd_bf[rb * 352:(rb + 1) * 352, :], w_fd_d[rb * 352:(rb + 1) * 352, :], writes=[("wfd_bf", rb)])

        ps_open()
        NS = 3
        psS = [ps_alloc("psS%d" % i, [128, 1024], F32) for i in range(NS)]
        psST = [t_[:, 0:512].bitcast(BF16).rearrange("p (k c) -> p k c", k=8) for t_ in psS]
        psO = ps_alloc("psO", [128, 2, 512], F32)
        c = {"S": 0, "tmp": 0, "PT": 0, "Pm": 0, "On": 0, "T": 0}

        def unit(un):
            return un // 6, (un % 6) * 65

        def band_head(uq, h):
            u = uq + Q0
            pr, base = h // 2, (h % 2) * 64
            sb = c["S"] % NS; c["S"] += 1
            pb = c["PT"] % 4; c["PT"] += 1
            bank, col = unit(h)

            def mmS(e):
                ins = None
                for b in range(5):
                    kt = u - 4 + b
                    pos = BPOS[b]
                    ins = e.matmul(psS[sb][:, pos * 128:(pos + 1) * 128], lhsT=kT[base:base + 64, pr, kt * 128:(kt + 1) * 128],
                                   rhs=qT[base:base + 64, pr, uq * 128:(uq + 1) * 128], start=True, stop=True)
                return ins
            S.add("pe", mmS, reads=[("kT", u - 4 + b) for b in range(5)] + [("qT", u)], writes=[("psS%d" % sb,)])
            S.add("dve", lambda e: e.tensor_tensor(out=psS[sb][:, 0:384], in0=psS[sb][:, 0:384], in1=bias8[:, h, 0:384], op=ALU.add),
                  reads=[("psS%d" % sb,), ("bias8",)], writes=[("psS%d" % sb,)])
            S.add("act", lambda e: e.activation(out=PT[:, pb, :], in_=psS[sb][:, 0:640], func=AF.Exp, scale=0.125),
                  reads=[("psS%d" % sb,)], writes=[("PT", pb)])

            def mmO(e):
                ins = None
                for b in range(5):
                    kt = u - 4 + b
                    pos = BPOS[b]
                    ins = e.matmul(psO[:, bank, col:col + 65], lhsT=PT[:, pb, pos * 128:(pos + 1) * 128], rhs=Vb[:, kt, h, :],
                                   start=(b == 0), stop=(b == 4))
                return ins
            return lambda: S.add("pe", mmO, reads=[("PT", pb), ("Vones", h)] + [("V", u - 4 + b) for b in range(5)],
                                 writes=[("psO", bank)])

        def mem_head(uq, hm):
            u = uq + Q0
            pr, base = hm // 2, (hm % 2) * 64
            sb = c["S"] % NS; c["S"] += 1
            pm = c["Pm"] % 4; c["Pm"] += 1
            bank, col = unit(8 + hm)

            def mmS(e):
                ins = None
                for mt in range(2):
                    ins = e.matmul(psS[sb][:, mt * 128:(mt + 1) * 128], lhsT=kmT[base:base + 64, pr, mt * 128:(mt + 1) * 128],
                                   rhs=qmT[base:base + 64, pr, uq * 128:(uq + 1) * 128], start=True, stop=True)
                return ins
            S.add("pe", mmS, reads=[("kmT", pr), ("qmT", u)], writes=[("psS%d" % sb,)])
            S.add("act", lambda e: e.activation(out=PmT[:, pm, :], in_=psS[sb][:, 0:256], func=AF.Exp, scale=0.125),
                  reads=[("psS%d" % sb,)], writes=[("PmT", pm)])

            def mmO(e):
                ins = None
                for mt in range(2):
                    ins = e.matmul(psO[:, bank, col:col + 65], lhsT=PmT[:, pm, mt * 128:(mt + 1) * 128], rhs=vm[:, mt, hm, :],
                                   start=(mt == 0), stop=(mt == 1))
                return ins
            return lambda: S.add("pe", mmO, reads=[("PmT", pm), ("vm", 0), ("vm", 1), ("vmones",)], writes=[("psO", bank)])

        obs = {}

        def finish_bank(uq, bank):
            if bank == 0:
                obs[uq] = c["On"] % 2; c["On"] += 1
            ob = obs[uq]
            v3 = psO[:, bank, 0:390].rearrange("p (u c) -> p u c", c=65)
            dsl = den[:, ob, bank * 6:(bank + 1) * 6, :]
            S.add("dve", lambda e: e.tensor_scalar_max(out=dsl, in0=v3[:, :, 64:65], scalar1=1e-30),
                  reads=[("psO", bank)], writes=[("den", ob, bank)])
            S.add("dve", lambda e: e.reciprocal(out=dsl, in_=dsl),
                  reads=[("den", ob, bank)], writes=[("den", ob, bank)])
            S.add("dve", lambda e: e.tensor_tensor(
                out=On[:, ob, bank * 6:(bank + 1) * 6, :], in0=v3[:, :, 0:64], in1=dsl.to_broadcast([128, 6, 64]),
                op=ALU.mult),
                reads=[("psO", bank), ("den", ob, bank)], writes=[("On", ob, bank)])

        def finish_tile_T(uq):
            ob = obs[uq]
            pt = c["S"] % NS; c["S"] += 1
            Onf = On[:, ob, :, :].rearrange("p u d -> p (u d)")

            def tr(e):
                ins = None
                for blk in range(6):
                    ins = e.transpose(psST[pt][:, blk, :], Onf[:, blk * 128:(blk + 1) * 128], ident[:])
                return ins
            S.add("pe", tr, reads=[("On", ob, 0), ("On", ob, 1), ("ident",)], writes=[("psS%d" % pt,)])
            evac_copy(OT[:, :, uq * 128:(uq + 1) * 128], psST[pt][:, 0:6, :], [("psS%d" % pt,)], [("OT", uq)], eng="act")

        ORDER = (0, 1, 2, 3, 4, 5, 8, 9, 10, 11, 6, 7)
        work = [(uq, un) for uq in range(NQ) for un in ORDER]
        pend = []
        later = []
        DEPTH = 2

        def step_later():
            for ent in list(later):
                ent[0] -= 1
                if ent[0] <= 0:
                    later.remove(ent)
                    ent[1]()

        def after_consume(uq0, un0):
            if un0 == 5:
                finish_bank(uq0, 0)
            if un0 == ORDER[-1]:
                finish_bank(uq0, 1)
                later.append([3, lambda uq0=uq0: finish_tile_T(uq0)])

        for (uq, un) in work:
            cons = band_head(uq, un) if un < 8 else mem_head(uq, un - 8)
            pend.append((cons, uq, un))
            if len(pend) > DEPTH:
                c0_, uq0, un0 = pend.pop(0)
                c0_()
                step_later()
                after_consume(uq0, un0)
        while pend:
            c0_, uq0, un0 = pend.pop(0)
            c0_()
            step_later()
            after_consume(uq0, un0)
        while later:
            step_later()
        ps_close()
        for nm in ("bias8", "PT", "PmT", "On", "den", "kT", "V", "qT", "qmT", "kmT", "vm"):
            M.release(nm)
        return OT, w_upp, w_upa, w_upm, w_out, w_g

    if stop_after in ("B", "C", "D"):
        OT, w_upp, w_upa, w_upm, w_out, w_g = phase_B()

    if debug and stop_after == "B":
        stg = M.alloc("dbgstg", [128, 6 * NQTOK], F32)
        S.add("dve", lambda e: e.tensor_copy(out=stg[:], in_=OT[:].rearrange("p a b -> p (a b)")),
              reads=[("OT", uq) for uq in range(NQ)], writes=[("dbgstg",)])
        dma("sp", dbg_d["OT"], stg[:], reads=[("dbgstg",)], tag="out")

    ffw = {}
    ffd = {}
    w_fu_v = wfu_bf.rearrange("(kc p) c -> p kc c", p=128)
    w_fd_v = wfd_bf.rearrange("(j p) c -> p j c", p=128)
    fu_keys = [("wfu_bf", kc) for kc in range(8)]
    fd_keys = [("wfd_bf", rb) for rb in range(8)]
    ffn_order = []
    for ch in range(6):
        ffn_order += [("g", ch), ("v", ch), ("d", ch)]

    def ffn_prefetch(limit=None, strict=False, defer=None):
        n_new = 0
        for key in ffn_order:
            if key in ffw:
                continue
            if limit is not None and n_new >= limit:
                return
            kind, ch = key
            c0 = ch * 512
            cw = min(512, DFF - c0)
            nm = "ff%s%d" % (kind, ch)
            try:
                if kind == "d":
                    j0, j1 = ch * 4, min(22, ch * 4 + 4)
                    t_ = M.alloc(nm, [128, j1 - j0, D], BF16)
                    for j_ in range(j0, j1):
                        ffd[j_] = (t_, nm, j_ - j0)
                else:
                    t_ = M.alloc(nm, [128, 8, cw], BF16)
            except RuntimeError:
                if not strict:
                    return
                if kind != "d":
                    raise
                ffw[key] = None
                for j_ in range(j0, j1):
                    nmj = "ffdj%d" % j_
                    tj = M.alloc(nmj, [128, 1, D], BF16)
                    ffd[j_] = (tj, nmj, 0)
                    issue = (lambda tj=tj, j_=j_, nmj=nmj: dma("sp", tj[:], w_fd_v[:, j_:j_ + 1, :], reads=fd_keys, writes=[(nmj,)]))
                    if defer is None:
                        issue()
                    else:
                        defer.append(issue)
                n_new += 1
                continue
            ffw[key] = t_
            if kind == "d":
                issue = (lambda t_=t_, j0=j0, j1=j1, nm=nm: dma("sp", t_[:], w_fd_v[:, j0:j1, :], reads=fd_keys, writes=[(nm,)]))
            else:
                off = c0 if kind == "g" else DFF + c0
                issue = (lambda t_=t_, off=off, cw=cw, nm=nm: dma("sp", t_[:], w_fu_v[:, :, off:off + cw], reads=fu_keys, writes=[(nm,)]))
            if defer is None:
                issue()
            else:
                defer.append(issue)
            n_new += 1

    ffn_state = {}

    def ffn_tile0_prologue(ps_fn):
        XD_ = 6
        g_c_ = M.alloc("g_c", [128, D], F32)
        ffn_state["g_c"] = g_c_
        dma("sp", g_c_[:], gfin_d, writes=[("g_c",)])
        xsd_ = M.alloc("xsd", [128, XD_, D], F32)
        h2T_ = M.alloc("h2T", [128, 2, 8, 258], BF16)
        ffn_state["xsd"] = xsd_; ffn_state["h2T"] = h2T_
        lists = []
        for sub in range(2):
            def slot_fn(sub=sub):
                return sub

            def load_fn(s, sub=sub):
                dma("sp", xsd_[:, s, :], xmid_d[sub * 128:(sub + 1) * 128, :], reads=[("xmid", sub)], writes=[("xsd", s)])

            def dst(ps, pskey_, dkey, sub=sub):
                evac_copy(h2T_[:, 0, :, 2 + sub * 128:2 + (sub + 1) * 128], ps[:], [pskey_], [dkey], eng="act")
            stages, _ = norm_stages(load_fn, "g_b", g_b, (xsd_, "xsd"), slot_fn, ps_fn, dst, ("h2T", 0, 1 + sub))
            lists.append(stages)
        def halo():
            S.add("pool", lambda e: e.tensor_copy(out=h2T_[:, 0, :, 0:2], in_=h2halo[:]),
                  reads=[("h2halo",)], writes=[("h2T", 0, 0)])
        return lists, halo

    def phase_C():
        XC = 8
        xsc = M.alloc("xsc", [128, XC, D], F32)
        hTc = M.alloc("hT", [128, 2, 8, 512], BF16)
        tg = M.alloc("tg", [128, 2, 512], F32)
        bgate = M.alloc("bgate", [128, 24], F32)
        wg_names = sorted(set(v[1] for v in w_g.values()))
        wg_dmas = []
        for q4 in range(4):
            for br in range(3):
                if (br, 2 * q4) in w_g:
                    continue
                nm = "w_g%d_q%d" % (br, q4)
                t_ = M.alloc(nm, [128, 8, 256], BF16)
                c0 = 2048 + br * 1024 + q4 * 256
                wg_dmas.append(lambda t_=t_, c0=c0, nm=nm: dma("pool", t_[:], w_in_v[:, :, c0:c0 + 256], writes=[(nm,)]))
                wg_names.append(nm)
                for k_ in range(2):
                    w_g[(br, 2 * q4 + k_)] = (t_, nm, k_ * 128)
        tb = pre_B["tb"]
        a01 = M.alloc("a01", [128, 1, 512], F32)
        mg = pre_B["mg"]
        dma("sp", bgate[:], bgate_d, writes=[("bgate",)])
        ps_open()
        psTc = [ps_alloc("psTc0", [128, 8, 128], BF16), ps_alloc("psTc1", [128, 8, 128], BF16)]
        psG = [ps_alloc("psG0", [128, 512], F32), ps_alloc("psG1", [128, 512], F32)]
        psY = [ps_alloc("psY0", [128, 512], F32), ps_alloc("psY1", [128, 512], F32)]
        psX = ps_alloc("psX", [128, 2, 512], F32)
        c = {"G": 0, "Y": 0, "tg": 0, "tb": 0, "xs": 0, "T": 0}
        ysrc = [(w_upp, "w_upp", 2, ypT, lambda q0, n: [("ypT", t, (g_[0] - Q0) * 128) for t in range(2) for g_ in groups[1:]]),
                (w_upa, "w_upa", 4, OT, None), (w_upm, "w_upm", 2, OT, None)]
        qgroups = groups_C
        slots_of = {}

        def prologue_pairs(gi):
            tiles = qgroups[gi]
            gb = gi % 2
            slots_of[gi] = [None] * len(tiles)
            lists = []
            for j, i in enumerate(tiles):
                def slot_fn(j=j):
                    s = c["xs"] % XC; c["xs"] += 1
                    slots_of[gi][j] = s
                    return s

                def load_fn(s, i=i):
                    dma("sp", xsc[:, s, :], xw[i * 128:(i + 1) * 128, :], writes=[("xsc", s)])

                def ps_fn():
                    pt = c["T"] % 2; c["T"] += 1
                    return psTc[pt], ("psTc%d" % pt,)

                def dst(ps, pskey_, dkey, j=j):
                    evac_copy(hTc[:, gb, :, j * 128:(j + 1) * 128], ps[:], [pskey_], [dkey])
                stages, _ = norm_stages(load_fn, "g_a", g_a, (xsc, "xsc"), slot_fn, ps_fn, dst, ("hT", gb, j),
                                        reuse=(rstd_x[:, i:i + 1], ("rstd_x", i)))
                lists.append(stages)
            return wavefront(lists)

        def ntile(gi, nt):
            tiles = qgroups[gi]
            gb = gi % 2
            n = 128 * len(tiles)
            q0 = (tiles[0] - Q0) * 128
            hkeys = [("hT", gb, j) for j in range(len(tiles))]
            otkeys = [("OT", i - Q0) for i in tiles]
            terms = []
            for br in range(3):
                gs = c["G"] % 2; c["G"] += 1
                ys = c["Y"] % 2; c["Y"] += 1
                ti = c["tg"] % 2; c["tg"] += 1
                bi = c["tb"] % 4; c["tb"] += 1
                wgt, wgname, gc0 = w_g[(br, nt)]

                def mmG(e, gs=gs, wgt=wgt, gc0=gc0):
                    ins = None
                    for kc in range(8):
                        ins = e.matmul(psG[gs][:, 0:n], lhsT=wgt[:, kc, gc0:gc0 + 128], rhs=hTc[:, gb, kc, 0:n],
                                       start=(kc == 0), stop=(kc == 7))
                    return ins
                S.add("pe", mmG, reads=[(wgname,)] + hkeys, writes=[("psG%d" % gs,)])
                wt, wname, nk, src_, keyfn = ysrc[br]
                koff = 4 if br == 2 else 0

                def mmY(e, ys=ys, wt=wt, nk=nk, src_=src_, koff=koff):
                    ins = None
                    for kc in range(nk):
                        ins = e.matmul(psY[ys][:, 0:n], lhsT=wt[:, kc, nt * 128:(nt + 1) * 128],
                                       rhs=src_[:, koff + kc, q0:q0 + n], start=(kc == 0), stop=(kc == nk - 1))
                    return ins
                rk = keyfn(q0, n) if keyfn else otkeys
                S.add("pe", mmY, reads=[(wname,)] + rk, writes=[("psY%d" % ys,)])
                S.add("act", lambda e, gs=gs, ti=ti, br=br: e.activation(
                    out=tg[:, ti, 0:n], in_=psG[gs][:, 0:n], func=AF.Sigmoid, bias=bgate[:, br * 8 + nt:br * 8 + nt + 1]),
                    reads=[("psG%d" % gs,), ("bgate",)], writes=[("tg", ti)])
                S.add("dve", lambda e, ys=ys, ti=ti, bi=bi: e.tensor_tensor(
                    out=tb[:, bi, 0:n], in0=psY[ys][:, 0:n], in1=tg[:, ti, 0:n], op=ALU.mult),
                    reads=[("psY%d" % ys,), ("tg", ti)], writes=[("tb", bi)])
                terms.append(bi)
            S.add("pool", lambda e: e.tensor_tensor(out=a01[:, 0, 0:n], in0=tb[:, terms[0], 0:n], in1=tb[:, terms[1], 0:n],
                                                    op=ALU.add),
                  reads=[("tb", terms[0]), ("tb", terms[1])], writes=[("a01", 0)])
            S.add("pool", lambda e: e.tensor_tensor(out=mg[:, 0, nt, 0:n], in0=a01[:, 0, 0:n], in1=tb[:, terms[2], 0:n],
                                                    op=ALU.add),
                  reads=[("a01", 0), ("tb", terms[2])], writes=[("mg", 0, nt)])

        def out_stage(gi, hooks=None):
            tiles = qgroups[gi]
            mkeys = [("mg", 0, nt) for nt in range(8)]
            for j, i in enumerate(tiles):
                s = slots_of[gi][j]
                for fn_ in (hooks or {}).get(j, []):
                    fn_()

                for hf in range(2):
                    def mmX(e, j=j, hf=hf):
                        ins = None
                        for kc in range(8):
                            ins = e.matmul(psX[:, hf, :], lhsT=mg[:, 0, kc, j * 128:(j + 1) * 128],
                                           rhs=w_out[:, kc, hf * 512:(hf + 1) * 512], start=(kc == 0), stop=(kc == 7))
                        return ins
                    S.add("pe", mmX, reads=mkeys + [("w_out",)], writes=[("psX", hf)])
                    S.add("dve", lambda e, s=s, hf=hf: e.tensor_tensor(
                        out=xsc[:, s, hf * 512:(hf + 1) * 512], in0=psX[:, hf, :], in1=xsc[:, s, hf * 512:(hf + 1) * 512],
                        op=ALU.add),
                        reads=[("psX", hf), ("xsc", s)], writes=[("xsc", s)])
                if i >= 5:
                    r = i - 5
                    dma("sp", xmid_d[r * 128:(r + 1) * 128, :], xsc[:, s, :], reads=[("xsc", s)], writes=[("xmid", r)])
                else:
                    def dst(ps, pskey_, dkey):
                        S.add("act", lambda e: e.activation(out=h2halo[:], in_=ps[:, :, 126:128], func=AF.Copy,
                                                            scale=flag[:, 0:1]),
                              reads=[pskey_, ("flag",)], writes=[dkey])
                    ctx = norm_s1(None, "g_b", g_b, loaded=s, xbuf=(xsc, "xsc"))
                    pt = c["T"] % 2; c["T"] += 1
                    norm_s2(ctx, psTc[pt], ("psTc%d" % pt,), dst, ("h2halo",))

        pro0 = prologue_pairs(0)
        if wg_dmas:
            wg_dmas.pop(0)()
        for th in pro0[:11]:
            th()
        for fn_ in wg_dmas:
            fn_()
        for th in pro0[11:]:
            th()
        for gi in range(len(qgroups)):
            nxt = prologue_pairs(gi + 1) if gi + 1 < len(qgroups) else []
            ti_ = 0
            last = gi == len(qgroups) - 1
            for nt in range(8):
                ntile(gi, nt)
                want = min(len(nxt), (len(nxt) * (nt + 1)) // 5)
                while ti_ < want:
                    nxt[ti_](); ti_ += 1
                if last and nt == 3:
                    early = [nm for nm in wg_names if all(v[1] != nm or k[1] < 4 for k, v in w_g.items())]
                    for nm in early:
                        M.release(nm)
                        wg_names.remove(nm)
                    ffn_prefetch()
            if last:
                for nm in ["bgate", "hT", "tg", "tb", "a01", "w_upp", "w_upa", "w_upm", "OT", "ypT", "g_a"] + list(wg_names):
                    M.release(nm)
                def ps_fn_c():
                    pt = c["T"] % 2; c["T"] += 1
                    return psTc[pt], ("psTc%d" % pt,)
                lists0, halo0 = ffn_tile0_prologue(ps_fn_c)
                for sub in range(2):
                    lists0[sub][0]()
                    lists0[sub][1]()
                ffn_prefetch(limit=6)
                hooks = {2: [lists0[0][2], lists0[1][2]]}
                out_stage(gi, hooks)
                for fn_ in (lists0[0][3], lists0[1][3], halo0, ffn_prefetch):
                    fn_()
            else:
                out_stage(gi)
        ps_close()
        for nm in ("w_out", "mg", "xsc"):
            M.release(nm)

    if stop_after in ("C", "D"):
        phase_C()

    if debug and stop_after == "C":
        xdb = M.alloc("xdb", [128, 4, D], F32)
        for r in range(16):
            s = r % 4
            dma("sp", xdb[:, s, :], xmid_d[r * 128:(r + 1) * 128, :], reads=[("xmid", r)], writes=[("xdb", s)])
            dma("sp", dbg_d["xmid"][r * 128:(r + 1) * 128, :], xdb[:, s, :], reads=[("xdb", s)], tag="out")

    def phase_D():
        XD = 6
        xsd = ffn_state["xsd"]
        g_c = ffn_state["g_c"]
        h2T_pre = None
        h2T = ffn_state["h2T"]
        NR = 3
        cg = M.alloc("cg", [128, NR, 256], F32)
        cv = M.alloc("cv", [128, NR, 256], F32)
        gl = M.alloc("gl", [128, 2, 256], F32)
        mb = M.alloc("mb", [128, 4, 256], BF16)
        ob = M.alloc("ob", [128, 2, D], F32)
        late_dmas = []
        ffn_prefetch(strict=True, defer=late_dmas)
        ps_open()
        NPA = 4
        psA = [ps_alloc("psA%d" % i, [128, 512], F32) for i in range(NPA)]
        psAT = [t_[:].bitcast(BF16).rearrange("p (k c) -> p k c", k=8) for t_ in psA]
        psXD = [ps_alloc("psXD%d" % i, [128, 2, 512], F32) for i in range(2)]
        c = {"r": 0, "gl": 0, "mb": 0, "ob": 0, "xs": 2, "pa": 0}

        sched_at = {}

        def at(it, fn):
            sched_at.setdefault(it, []).append(fn)

        slots = {}

        def plan_prologue(tt, it0):
            tbuf = tt % 2
            slots[tt] = [None, None]
            per_sub = []
            for sub in range(2):
                r = tt * 2 + sub

                def slot_fn(sub=sub):
                    s = c["xs"] % XD; c["xs"] += 1
                    slots[tt][sub] = s
                    return s

                def load_fn(s, r=r):
                    dma("sp", xsd[:, s, :], xmid_d[r * 128:(r + 1) * 128, :], reads=[("xmid", r)], writes=[("xsd", s)])

                def ps_fn():
                    pa = c["pa"] % NPA; c["pa"] += 1
                    return psAT[pa], ("psA%d" % pa,)

                def dst(ps, pskey_, dkey, sub=sub):
                    evac_copy(h2T[:, tbuf, :, 2 + sub * 128:2 + (sub + 1) * 128], ps[:], [pskey_], [dkey], eng="act")
                stages, _ = norm_stages(load_fn, "g_b", g_b, (xsd, "xsd"), slot_fn, ps_fn, dst, ("h2T", tbuf, 1 + sub))
                per_sub.append(stages)

            def halo():
                if tt == 0:
                    S.add("pool", lambda e: e.tensor_copy(out=h2T[:, tbuf, :, 0:2], in_=h2halo[:]),
                          reads=[("h2halo",)], writes=[("h2T", tbuf, 0)])
                else:
                    S.add("pool", lambda e: e.tensor_copy(out=h2T[:, tbuf, :, 0:2], in_=h2T[:, 1 - tbuf, :, 256:258]),
                          reads=[("h2T", 1 - tbuf, 2)], writes=[("h2T", tbuf, 0)])
            offs = (0, 9, 12, 15)
            for k_ in range(4):
                for sub in range(2):
                    at(it0 + offs[k_] + (sub if k_ else 0), per_sub[sub][k_])
            at(it0 + 17, halo)

        def produce(tt, j):
            tbuf = tt % 2
            hk = [("h2T", tbuf, q) for q in range(3)]
            ri = c["r"] % NR; c["r"] += 1
            for gv in range(2):
                pa = c["pa"] % NPA; c["pa"] += 1
                col0 = gv * DFF + j * 128
                jj = gv * 22 + j
                cbuf, cname = (cg, "cg") if gv == 0 else (cv, "cv")

                wch = ffw[("g" if gv == 0 else "v", (j * 128) // 512)]
                wc0 = (j * 128) % 512
                wnm = "ff%s%d" % ("g" if gv == 0 else "v", (j * 128) // 512)

                def mmA(e, pa=pa, wch=wch, wc0=wc0):
                    ins = None
                    for kc in range(8):
                        ins = e.matmul(psA[pa][:, 0:258], lhsT=wch[:, kc, wc0:wc0 + 128], rhs=h2T[:, tbuf, kc, :],
                                       start=(kc == 0), stop=(kc == 7))
                    return ins
                S.add("pe", mmA, reads=[(wnm,)] + hk, writes=[("psA%d" % pa,)])
                S.add("act", lambda e, pa=pa, cbuf=cbuf, jj=jj: e.activation(
                    out=cbuf[:, ri, :], in_=psA[pa][:, 2:258], func=AF.Identity, bias=convb[:, jj:jj + 1],
                    scale=convw[:, jj, 2:3]),
                    reads=[("psA%d" % pa,), ("convw",), ("convb",)], writes=[(cname, ri)])
                for tap in (1, 0):
                    S.add("dve", lambda e, pa=pa, cbuf=cbuf, jj=jj, tap=tap: e.scalar_tensor_tensor(
                        out=cbuf[:, ri, :], in0=psA[pa][:, tap:tap + 256], scalar=convw[:, jj, tap:tap + 1],
                        in1=cbuf[:, ri, :], op0=ALU.mult, op1=ALU.add),
                        reads=[("psA%d" % pa,), ("convw",), (cname, ri)], writes=[(cname, ri)])
            return (tt, j, ri)

        def mid(ctx):
            tt, j, ri = ctx
            gi_ = c["gl"] % 2; c["gl"] += 1
            S.add("act", lambda e: e.activation(out=gl[:, gi_, :], in_=cg[:, ri, :], func=AF.Gelu),
                  reads=[("cg", ri)], writes=[("gl", gi_)])
            mi = c["mb"] % 4; c["mb"] += 1
            S.add("pool", lambda e: e.tensor_tensor(out=mb[:, mi, :], in0=gl[:, gi_, :], in1=cv[:, ri, :], op=ALU.mult),
                  reads=[("gl", gi_), ("cv", ri)], writes=[("mb", mi)])
            return (tt, j, mi)

        def consume(ctx, it):
            tt, j, mi = ctx
            wd, wdname, wdi = ffd[j]

            def mmD(e):
                ins = None
                for sub in range(2):
                    for hf in range(2):
                        ins = e.matmul(psXD[sub][:, hf, :], lhsT=mb[:, mi, sub * 128:(sub + 1) * 128],
                                       rhs=wd[:, wdi, hf * 512:(hf + 1) * 512], start=(j == 0), stop=(j == 21))
                return ins
            S.add("pe", mmD, reads=[("mb", mi), (wdname,)], writes=[("psXD0",), ("psXD1",)])
            if j == 21:
                plan_epilogue(tt, it)

        def plan_epilogue(tt, it):
            sl = slots[tt]
            ks = [None, None]
            ois = [None, None]

            def e1(sub):
                s = sl[sub]
                S.add("dve", lambda e: e.tensor_tensor(
                    out=xsd[:, s, :], in0=psXD[sub][:].rearrange("p a b -> p (a b)"), in1=xsd[:, s, :], op=ALU.add),
                    reads=[("psXD%d" % sub,), ("xsd", s)], writes=[("xsd", s)])

            def e1b(sub):
                s = sl[sub]
                k = ctr["ss"] % 8; ctr["ss"] += 1
                ks[sub] = k
                S.add("act", lambda e: e.activation(out=junk[:], in_=xsd[:, s, :], func=AF.Square,
                                                    scale=1.0 / 32.0, accum_out=ss[:, k:k + 1]),
                      reads=[("xsd", s)], writes=[("junk",), ("ss", k)])
                S.add("pool", lambda e: e.tensor_scalar(out=ss[:, k:k + 1], in0=ss[:, k:k + 1], scalar1=EPS,
                                                        scalar2=None, op0=ALU.add),
                      reads=[("ss", k)], writes=[("ss", k)])
                S.add("pool", lambda e: e.tensor_tensor(out=ss[:, k:k + 1], in0=ss[:, k:k + 1], in1=mhalf[:, 0:1],
                                                        op=ALU.pow),
                      reads=[("ss", k), ("mhalf",)], writes=[("ss", k)])

            def e2(sub):
                s, k = sl[sub], ks[sub]
                oi = c["ob"] % 2; c["ob"] += 1
                ois[sub] = oi
                S.add("act", lambda e: e.activation(out=ob[:, oi, :], in_=xsd[:, s, :], func=AF.Copy, scale=ss[:, k:k + 1]),
                      reads=[("xsd", s), ("ss", k)], writes=[("ob", oi)])

            def e3(sub):
                r = tt * 2 + sub
                oi = ois[sub]
                S.add("pool", lambda e: e.tensor_tensor(out=ob[:, oi, :], in0=ob[:, oi, :], in1=g_c[:], op=ALU.mult),
                      reads=[("ob", oi), ("g_c",)], writes=[("ob", oi)])
                dma("sp", out_d[r * 128:(r + 1) * 128, :], ob[:, oi, :], reads=[("ob", oi)], tag="out")
            e1(0)
            at(it + 1, lambda: e1(1))
            at(it + 1, lambda: e1b(0))
            at(it + 2, lambda: e1b(1))
            at(it + 3, lambda: e2(0))
            at(it + 4, lambda: e2(1))
            at(it + 5, lambda: e3(0))
            at(it + 6, lambda: e3(1))

        slots[0] = [0, 1]
        for k_, fn_ in enumerate(late_dmas):
            at(k_ // 2, fn_)
        q1, q2 = [], []
        it = 0
        for tt in range(8):
            if tt + 1 < 8:
                plan_prologue(tt + 1, it)
            for j in range(22):
                for fn in sched_at.pop(it, []):
                    fn()
                q1.append(produce(tt, j))
                if len(q1) > 1:
                    q2.append(mid(q1.pop(0)))
                if len(q2) > 1:
                    consume(q2.pop(0), it)
                it += 1
        while q1 or q2:
            for fn in sched_at.pop(it, []):
                fn()
            if q1:
                q2.append(mid(q1.pop(0)))
            if q2:
                consume(q2.pop(0), it)
            it += 1
        for it_ in sorted(sched_at):
            for fn in sched_at.pop(it_):
                fn()
        ps_close()

    if stop_after == "D":
        phase_D()

    outs = [o for o in S.dma_list if o.tag == "out"]
    S.add("sp", lambda e: e.nop(), deps=outs)
    S.emit()
    return nc


def _bias_layout(rel_bias):
    k = np.arange(128)[:, None, None]
    b = np.arange(5)[None, :, None]
    qi = np.arange(128)[None, None, :]
    rel = (4 - b) * 128 + qi - k
    idx = np.clip(rel, -128, 128) + 128
    g = rel_bias[:, idx]
    g = g.transpose(1, 0, 2, 3)
    near = g[:, :, [0, 3, 4], :].reshape(128, 8, 384)
    far = g[:, :, 1, 0:1]
    return np.ascontiguousarray(np.concatenate([near, far], axis=2).reshape(128, 8 * 385)).astype(np.float32)


def make_in_maps(inputs):
    f = lambda a: np.ascontiguousarray(np.asarray(a, dtype=np.float32))
    x = f(inputs["x"])[0]
    shared = {
        "mem": f(inputs["mem"])[0],
        "w_in": f(inputs["w_in"])[0],
        "bgate": f(f(inputs["b_gate"])[0].reshape(24, 128).T),
        "w_pool": f(inputs["w_pool"])[0],
        "pscale": f(f(inputs["pool_scale"])[0].reshape(2, 128).T),
        "biasT": _bias_layout(f(inputs["rel_bias"])[0]),
        "w_mem_kv": f(inputs["w_mem_kv"])[0],
        "w_up_pool": f(inputs["w_up_pool"])[0],
        "w_up_attn": f(inputs["w_up_attn"])[0],
        "w_up_mem": f(inputs["w_up_mem"])[0],
        "w_out": f(inputs["w_out"])[0],
        "w_ffn_up": f(inputs["w_ffn_up"])[0],
        "convw": f(f(inputs["conv_w"])[0].reshape(3, 44, 128).transpose(2, 1, 0).reshape(128, 132)),
        "convb": f(f(inputs["conv_b"])[0].reshape(44, 128).T),
        "w_ffn_down": f(inputs["w_ffn_down"])[0],
        "g_mix_b": f(np.broadcast_to(f(inputs["norm_mix_g"])[0], (128, D))),
        "g_mem_b": f(np.broadcast_to(f(inputs["norm_mem_g"])[0], (128, D))),
        "g_ffn_b": f(np.broadcast_to(f(inputs["norm_ffn_g"])[0], (128, D))),
        "g_fin_b": f(np.broadcast_to(f(inputs["norm_final_g"]), (128, D))),
        "ident": np.eye(128, dtype=np.float32),
    }
    maps = []
    wins = (2.0, 4.0, 8.0, 16.0)
    for c in range(NCORES):
        t0 = c * OWN
        xw = np.zeros((WIN, D), np.float32)
        lo = t0 - HALO
        src_lo = max(lo, 0)
        xw[src_lo - lo:] = x[src_lo:t0 + OWN]
        pos = lo + np.arange(WIN)
        valid = (pos >= 0).astype(np.float32).reshape(NT, 128).T
        inv = np.zeros((128, 2, 16), np.float32)
        for p in range(128):
            for t in range(2):
                w = wins[2 * t + (1 if p >= 64 else 0)]
                inv[p, t, :] = 1.0 / np.minimum(t0 + np.arange(16) + 1.0, w)
        m = dict(shared)
        m["xw"] = xw
        m["valid"] = f(valid)
        m["flag"] = np.full((128, 1), 0.0 if c == 0 else 1.0, np.float32)
        m["invcnt"] = f(inv.reshape(128, 32))
        maps.append(m)
    return maps


_NC_CACHE = {}


def kernel(**inputs):
    if "nc" not in _NC_CACHE:
        _NC_CACHE["nc"] = build()
    nc = _NC_CACHE["nc"]
    maps = make_in_maps(inputs)
    res = bu.run_bass_kernel_spmd(nc, maps, core_ids=list(range(NCORES)))
    out = np.concatenate([np.asarray(r["out"], dtype=np.float32) for r in res.results], axis=0)
    return out.reshape(1, SEQ, D)
```

```python
import numpy as np
import concourse.bass as bass
import concourse.mybir as mybir
import concourse.bass_utils as bu

F32 = mybir.dt.float32
BF16 = mybir.dt.bfloat16
AF = mybir.ActivationFunctionType
ALU = mybir.AluOpType

NCORES = 8
SEQ = 16384
D = 1024
OWN = SEQ // NCORES
HALO = 640
WIN = OWN + HALO
NT = WIN // 128
Q0 = 4
NQ = NT - Q0
NQTOK = NQ * 128
DFF = 2816
EPS = 1e-6
NEG = -30000.0
NDMA_SEM = {"sp": 36, "pool": 20}


class Op:
    __slots__ = ("eng", "fn", "deps", "idx", "dma", "signal", "sig", "sem", "tag")


class Sched:
    ENG = ("pe", "act", "dve", "pool", "sp")

    def __init__(self, nc):
        self.nc = nc
        self.q = {e: [] for e in self.ENG}
        self.lw = {}
        self.rd = {}
        self.base = {}
        self.names = {}
        self.dma_list = []
        self.dma_q = {}

    def add(self, eng, fn, reads=(), writes=(), deps=(), dma=False, tag=None):
        o = Op()
        o.eng = eng; o.fn = fn; o.dma = dma; o.signal = False; o.sig = 0; o.sem = None; o.tag = tag
        o.idx = len(self.q[eng])
        d = {}
        for x in deps:
            d[x] = True
        for k in reads:
            self.names.setdefault(k[0], set()).add(k)
            w = self.lw.get(k)
            if w is not None:
                d[w] = True
            for x in self.base.get(k[0], ()):
                d.setdefault(x, False)
        for k in writes:
            self.names.setdefault(k[0], set()).add(k)
            w = self.lw.get(k)
            if w is not None:
                d.setdefault(w, False)
            for r in self.rd.get(k, {}).values():
                d.setdefault(r, False)
            for x in self.base.get(k[0], ()):
                d.setdefault(x, False)
        for k in reads:
            slot = self.rd.setdefault(k, {})
            slot[("dma", len(self.dma_list)) if dma else eng] = o
        for k in writes:
            self.lw[k] = o
            self.rd[k] = {}
        if dma:
            lst = self.dma_q.setdefault(eng, [])
            n = len(lst)
            R = NDMA_SEM[eng]
            if n >= R:
                d[lst[n - R]] = True
            o.sem = (eng, n % R)
            o.sig = 16 * (n // R + 1)
            lst.append(o)
            self.dma_list.append(o)
        d.pop(o, None)
        o.deps = d
        self.q[eng].append(o)
        return o

    def frontier(self, name):
        best = {}
        out = []
        for k in self.names.get(name, ()):
            cands = []
            w = self.lw.get(k)
            if w is not None:
                cands.append(w)
            cands.extend(self.rd.get(k, {}).values())
            for o in cands:
                if o.dma:
                    out.append(o)
                else:
                    b = best.get(o.eng)
                    if b is None or o.idx > b.idx:
                        best[o.eng] = o
        out.extend(best.values())
        out.extend(self.base.get(name, ()))
        return out

    STRICT = True

    @staticmethod
    def _needs_wait(o, d, raw):
        if d.dma or o.dma:
            return True
        if d.eng != o.eng:
            return True
        if o.eng == "pe":
            return False
        return raw or Sched.STRICT

    def emit(self):
        nc = self.nc
        for e in self.ENG:
            for o in self.q[e]:
                for d, raw in o.deps.items():
                    if self._needs_wait(o, d, raw):
                        d.signal = True
        for e in self.ENG:
            c = 0
            for o in self.q[e]:
                if not o.dma and o.signal:
                    c += 1
                    o.sig = c
        from contextlib import ExitStack
        with ExitStack() as st:
            esem = {e: st.enter_context(nc.semaphore("s_" + e)) for e in self.ENG}
            dsem = {(q_, i): st.enter_context(nc.semaphore("d%s%d" % (q_, i))) for q_, r_ in NDMA_SEM.items() for i in range(r_)}
            block = st.enter_context(nc.Block())

            def run(e, eng):
                waited = {}
                for o in self.q[e]:
                    for d, raw in o.deps.items():
                        if not self._needs_wait(o, d, raw):
                            continue
                        sem = dsem[d.sem] if d.dma else esem[d.eng]
                        key = ("d", d.sem) if d.dma else d.eng
                        if waited.get(key, 0) >= d.sig:
                            continue
                        eng.wait_ge(sem, d.sig)
                        waited[key] = d.sig
                    ins = o.fn(eng)
                    if o.dma:
                        ins.then_inc(dsem[o.sem], 16)
                    elif o.signal:
                        ins.then_inc(esem[e], 1)

            @block.tensor
            def _(eng):
                run("pe", eng)

            @block.scalar
            def _(eng):
                run("act", eng)

            @block.vector
            def _(eng):
                run("dve", eng)

            @block.gpsimd
            def _(eng):
                run("pool", eng)

            @block.sync
            def _(eng):
                run("sp", eng)


class Mem:
    def __init__(self, nc, sched, lo=16512, hi=229376 - 128):
        self.nc = nc; self.S = sched
        self.free = [[lo, hi, ()]]
        self.live = {}
        self.uid = 0

    def alloc(self, name, shape, dtype, top=False):
        nbytes = int(np.prod(shape[1:])) * mybir.dt.size(dtype)
        nbytes = (nbytes + 63) // 64 * 64
        order = range(len(self.free) - 1, -1, -1) if top else range(len(self.free))
        for i in order:
            s, e, fr = self.free[i]
            if e - s >= nbytes:
                if top:
                    a = e - nbytes
                    self.free[i] = [s, a, fr]
                else:
                    a = s
                    self.free[i] = [s + nbytes, e, fr]
                if self.free[i][0] == self.free[i][1]:
                    del self.free[i]
                self.uid += 1
                t = self.nc.alloc_sbuf_tensor_at("%s_%d" % (name, self.uid), list(shape), dtype, offset=a)
                self.live[name] = (a, a + nbytes)
                if fr:
                    self.S.base[name] = tuple(fr)
                return t
        raise RuntimeError("SBUF OOM for %s (%d B); free=%s" % (name, nbytes, [(s, e) for s, e, _ in self.free]))

    def release(self, name):
        a, b = self.live.pop(name)
        fr = tuple(self.S.frontier(name))
        self.free.append([a, b, fr])
        self.free.sort(key=lambda r: r[0])
        merged = []
        for r in self.free:
            if merged and merged[-1][1] == r[0]:
                merged[-1] = [merged[-1][0], r[1], tuple(set(merged[-1][2]) | set(r[2]))]
            else:
                merged.append(r)
        self.free = merged
        for k in self.S.names.pop(name, ()):
            self.S.lw.pop(k, None)
            self.S.rd.pop(k, None)
        self.S.base.pop(name, None)


def build(debug=None):
    nc = bass.Bass("TRN2", target_bir_lowering=False)
    S = Sched(nc)
    M = Mem(nc, S)

    def din(name, shape):
        return nc.dram_tensor(name, list(shape), F32, kind="ExternalInput").ap()

    xw = din("xw", [WIN, D])
    memd = din("mem", [256, D])
    w_in = din("w_in", [D, 5120])
    bgate_d = din("bgate", [128, 24])
    wpool_d = din("w_pool", [4, 64, 64])
    pscale_d = din("pscale", [128, 2])
    biasT_d = din("biasT", [128, 8 * 385])
    w_mkv = din("w_mem_kv", [D, 512])
    w_upp_d = din("w_up_pool", [256, D])
    w_upa_d = din("w_up_attn", [512, D])
    w_upm_d = din("w_up_mem", [256, D])
    w_out_d = din("w_out", [D, D])
    w_fu_d = din("w_ffn_up", [D, 2 * DFF])
    convw_d = din("convw", [128, 44 * 3])
    convb_d = din("convb", [128, 44])
    w_fd_d = din("w_ffn_down", [DFF, D])
    gmix_d = din("g_mix_b", [128, D])
    gmem_d = din("g_mem_b", [128, D])
    gffn_d = din("g_ffn_b", [128, D])
    gfin_d = din("g_fin_b", [128, D])
    valid_d = din("valid", [128, NT])
    flag_d = din("flag", [128, 1])
    invcnt_d = din("invcnt", [128, 32])
    ident_d = din("ident", [128, 128])
    out_d = nc.dram_tensor("out", [OWN, D], F32, kind="ExternalOutput").ap()
    xmid_d = nc.dram_tensor("xmid_scr", [OWN, D], F32, kind="Internal").ap()
    wfu_bf = nc.dram_tensor("wfu_bf16", [D, 2 * DFF], BF16, kind="Internal").ap()
    wfd_bf = nc.dram_tensor("wfd_bf16", [DFF, D], BF16, kind="Internal").ap()
    dbg_d = {}
    if debug:
        for nm, shp in debug.items():
            if nm.startswith("_"):
                continue
            dbg_d[nm] = nc.dram_tensor("dbg_" + nm, list(shp), F32, kind="ExternalOutput").ap()

    from contextlib import ExitStack
    ps_state = {"stack": None, "names": [], "frontier": ()}

    def ps_open():
        ps_state["stack"] = ExitStack()
        ps_state["names"] = []

    def ps_alloc(name, shape, dtype=F32):
        t = ps_state["stack"].enter_context(nc.psum_tensor(name, list(shape), dtype))
        ps_state["names"].append(name)
        if ps_state["frontier"]:
            S.base[name] = tuple(ps_state["frontier"])
        return t

    def ps_close():
        fr = []
        for nm in ps_state["names"]:
            fr.extend(S.frontier(nm))
        ps_state["frontier"] = tuple(set(fr))
        ps_state["stack"].close()

    def dma(eng, out_ap, in_ap, reads=(), writes=(), tag=None):
        return S.add(eng, lambda e: e.dma_start(out=out_ap, in_=in_ap), reads=reads, writes=writes, dma=True, tag=tag)

    rr = {"evac": 0}

    def evac_copy(out_ap, in_ap, reads, writes, eng=None):
        if eng is None:
            eng = ("act", "dve")[rr["evac"] % 2]
            rr["evac"] += 1
        if eng == "act":
            return S.add("act", lambda e: e.activation(out=out_ap, in_=in_ap, func=AF.Copy), reads=reads, writes=writes)
        return S.add(eng, lambda e: e.tensor_copy(out=out_ap, in_=in_ap), reads=reads, writes=writes)

    ident = M.alloc("ident", [128, 128], BF16)
    id32 = M.alloc("id32", [128, 128], F32)
    def _ident_load():
        dma("sp", id32[:], ident_d, writes=[("id32",)])
        S.add("dve", lambda e: e.tensor_copy(out=ident[:], in_=id32[:]), reads=[("id32",)], writes=[("ident",)])
    const_dmas = [_ident_load]
    g_a = M.alloc("g_a", [128, D], F32)
    g_b = M.alloc("g_b", [128, D], F32)
    const_dmas.insert(0, lambda: dma("sp", g_a[:], gmix_d, writes=[("g_a",)]))
    const_dmas.append(lambda: dma("sp", g_b[:], gmem_d, writes=[("g_b",)]))
    valid = M.alloc("valid", [128, NT], F32)
    const_dmas.append(lambda: dma("sp", valid[:], valid_d, writes=[("valid",)]))
    flag = M.alloc("flag", [128, 1], F32)
    const_dmas.append(lambda: dma("sp", flag[:], flag_d, writes=[("flag",)]))
    ss = M.alloc("ss", [128, 8], F32)
    rstd_x = M.alloc("rstd_x", [128, NT], F32)
    mhalf = M.alloc("mhalf", [128, 1], F32)
    S.add("pool", lambda e: e.memset(mhalf[:], -0.5), writes=[("mhalf",)])
    S.add("act", lambda e: e.activation(out=junk[:, 0:1], in_=mhalf[:, 0:1], func=AF.Square), reads=[("mhalf",)], writes=[("junk",)])
    junk = M.alloc("junk", [128, D], BF16)
    xs = M.alloc("xs", [128, 5, D], F32)
    xs_glob = xs
    HB_N = 3
    hb = M.alloc("hb", [128, HB_N, D], BF16)
    h2halo = M.alloc("h2halo", [128, 8, 2], BF16)
    convw = M.alloc("convw", [128, 44, 3], F32)
    convb = M.alloc("convb", [128, 44], F32)
    bias8 = M.alloc("bias8", [128, 8, 385], F32)
    XS_N = 5
    ctr = {"xs": 0, "hb": 0, "ss": 0, "psT": 0, "pj": 0}

    w_a = M.alloc("w_a", [128, 8, 2048], BF16)
    kT = M.alloc("kT", [128, 4, WIN], BF16, top=True)
    Vb = M.alloc("V", [128, NT, 8, 65], BF16, top=True)
    qT = M.alloc("qT", [128, 4, NQTOK], BF16, top=True)
    qmT = M.alloc("qmT", [128, 2, NQTOK], BF16, top=True)
    ypT = M.alloc("ypT", [128, 2, NQTOK], BF16, top=True)
    kmT = M.alloc("kmT", [128, 2, 256], BF16, top=True)
    vm = M.alloc("vm", [128, 2, 4, 65], BF16, top=True)
    hT = M.alloc("hT", [128, 2, 8, 512], BF16)
    w_mk = M.alloc("w_mk", [128, 8, 512], BF16)
    wpbd = M.alloc("wpbd", [128, 2, 128], BF16)
    pscale = M.alloc("pscale", [128, 2], F32)
    invcnt = M.alloc("invcnt", [128, 2, 16], F32)
    ub = M.alloc("ub", [128, 2, 16 + 512], F32)
    sA = M.alloc("sA", [128, 2, 16 + 512], F32)
    sB = M.alloc("sB", [128, 2, 16 + 512], F32)
    pT = M.alloc("pT", [128, 2, 512], BF16)
    ptmp = M.alloc("ptmp", [128, 16], F32)

    w_in_v = w_in.rearrange("(kc p) c -> p kc c", p=128)
    for c0 in (0, 512, 1024, 1536):
        dma("pool", w_a[:, :, c0:c0 + 512], w_in_v[:, :, c0:c0 + 512],
            writes=[("w_a", c0 // 256), ("w_a", c0 // 256 + 1)])
    wp32 = M.alloc("wp32", [128, 2, 128], F32)
    wpbd_keys = [("wpbd",)]
    S.add("pool", lambda e: e.memset(wp32[:], 0.0), writes=[("wp32",)])

    def late_setup():
        dma("sp", bias8[:], biasT_d.rearrange("p (h c) -> p h c", h=8), writes=[("bias8",)])
        for h in range(8):
            S.add("dve", lambda e, h=h: e.tensor_scalar(
                out=bias8[:, h, 0:384], in0=bias8[:, h, 0:384], scalar1=bias8[:, h, 384:385], scalar2=8.0,
                op0=ALU.subtract, op1=ALU.mult),
                reads=[("bias8",)], writes=[("bias8",)])
        S.add("pool", lambda e: e.memset(bias8[0:64, :, 64:128], 8.0 * NEG), reads=[("bias8",)], writes=[("bias8",)])
        S.add("pool", lambda e: e.memset(bias8[64:128, :, 256:320], 8.0 * NEG), reads=[("bias8",)], writes=[("bias8",)])
        dma("sp", convw[:], convw_d.rearrange("p (j t) -> p j t", t=3), writes=[("convw",)])
        dma("sp", convb[:], convb_d, writes=[("convb",)])
        dma("pool", w_mk[:], w_mkv.rearrange("(kc p) c -> p kc c", p=128), writes=[("w_mk",)])
        dma("sp", pscale[:], pscale_d, writes=[("pscale",)])
        dma("sp", invcnt[:], invcnt_d.rearrange("p (t c) -> p t c", t=2), writes=[("invcnt",)])
        for g in range(4):
            t, hlf = g // 2, g % 2
            dma("sp", wp32[hlf * 64:(hlf + 1) * 64, t, hlf * 64:(hlf + 1) * 64], wpool_d[g], reads=[("wp32",)],
                writes=[("wp32", g)])
        S.add("dve", lambda e: e.tensor_copy(out=wpbd[:], in_=wp32[:]),
              reads=[("wp32", g) for g in range(4)], writes=[("wpbd",)])
        for h in range(8):
            S.add("pool", lambda e, h=h: e.tensor_copy(out=Vb[:, :, h, 64:65], in_=valid[:, :].unsqueeze(2)),
                  reads=[("valid",)], writes=[("Vones", h)])
        S.add("pool", lambda e: e.memset(vm[:, :, :, 64:65], 1.0), writes=[("vmones",)])
        S.add("pool", lambda e: e.memset(ub[:], 0.0), writes=[("ub",)])

    ps_open()
    psT = [ps_alloc("psT0", [128, 8, 128], BF16), ps_alloc("psT1", [128, 8, 128], BF16)]
    pj = [ps_alloc("pj%d" % i, [128, 512], F32) for i in range(4)]

    def pskey(kind, i):
        return ("%s%d" % (kind, i),)

    def norm_stages(load_fn, gname, gtile, xbuf, slot_fn, ps_fn, dst_fn, dst_key, save=None, reuse=None):
        xs, xname = xbuf
        st = {}

        def n0():
            st["s"] = slot_fn()
            if load_fn is not None:
                load_fn(st["s"])

        def n1():
            if reuse is not None:
                return
            s = st["s"]
            c = ctr["ss"] % 8; ctr["ss"] += 1
            st["c"] = c
            S.add("act", lambda e: e.activation(out=junk[:], in_=xs[:, s, :], func=AF.Square, scale=1.0 / 32.0,
                                                accum_out=ss[:, c:c + 1]),
                  reads=[(xname, s)], writes=[("junk",), ("ss", c)])
            S.add("pool", lambda e: e.tensor_scalar(out=ss[:, c:c + 1], in0=ss[:, c:c + 1], scalar1=EPS, scalar2=None,
                                                    op0=ALU.add),
                  reads=[("ss", c)], writes=[("ss", c)])
            if save is not None:
                sv_ap, sv_key = save
                S.add("pool", lambda e: e.tensor_tensor(out=sv_ap, in0=ss[:, c:c + 1], in1=mhalf[:, 0:1], op=ALU.pow),
                      reads=[("ss", c), ("mhalf",)], writes=[sv_key])
            else:
                S.add("pool", lambda e: e.tensor_tensor(out=ss[:, c:c + 1], in0=ss[:, c:c + 1], in1=mhalf[:, 0:1], op=ALU.pow),
                      reads=[("ss", c), ("mhalf",)], writes=[("ss", c)])

        def n2():
            s = st["s"]
            if reuse is not None:
                sc_ap, sc_key = reuse
            elif save is not None:
                sc_ap, sc_key = save
            else:
                c = st["c"]
                sc_ap, sc_key = ss[:, c:c + 1], ("ss", c)
            hs = ctr["hb"] % HB_N; ctr["hb"] += 1
            st["hs"] = hs
            S.add("dve", lambda e: e.scalar_tensor_tensor(out=hb[:, hs, :], in0=xs[:, s, :], scalar=sc_ap,
                                                          in1=gtile[:], op0=ALU.mult, op1=ALU.mult),
                  reads=[(xname, s), sc_key, (gname,)], writes=[("hb", hs)])

        def n3():
            hs = st["hs"]
            ps_ap, ps_key = ps_fn()

            def tr(e):
                ins = None
                for kc in range(8):
                    ins = e.transpose(ps_ap[:, kc, :], hb[:, hs, kc * 128:(kc + 1) * 128], ident[:])
                return ins
            S.add("pe", tr, reads=[("hb", hs), ("ident",)], writes=[ps_key])
            dst_fn(ps_ap, ps_key, dst_key)
        return [n0, n1, n2, n3], st

    def wavefront(stage_lists):
        out = []
        nt_ = len(stage_lists)
        ns_ = max(len(s_) for s_ in stage_lists) if stage_lists else 0
        for d_ in range(nt_ + ns_ - 1):
            for k_ in range(ns_):
                t_ = d_ - k_
                if 0 <= t_ < nt_ and k_ < len(stage_lists[t_]):
                    out.append(stage_lists[t_][k_])
        return out

    def norm_s1(src_rows_ap, gname, gtile, loaded=None, xbuf=None):
        xs, xname = xbuf if xbuf is not None else (xs_glob, "xs")
        if loaded is None:
            s = ctr["xs"] % XS_N; ctr["xs"] += 1
            dma("sp", xs[:, s, :], src_rows_ap, writes=[(xname, s)])
        else:
            s = loaded
        c = ctr["ss"] % 8; ctr["ss"] += 1
        S.add("act", lambda e: e.activation(out=junk[:], in_=xs[:, s, :], func=AF.Square, scale=1.0 / 32.0,
                                            accum_out=ss[:, c:c + 1]),
              reads=[(xname, s)], writes=[("junk",), ("ss", c)])
        S.add("pool", lambda e: e.tensor_scalar(out=ss[:, c:c + 1], in0=ss[:, c:c + 1], scalar1=EPS, scalar2=None,
                                                op0=ALU.add),
              reads=[("ss", c)], writes=[("ss", c)])
        S.add("pool", lambda e: e.tensor_tensor(out=ss[:, c:c + 1], in0=ss[:, c:c + 1], in1=mhalf[:, 0:1], op=ALU.pow),
              reads=[("ss", c), ("mhalf",)], writes=[("ss", c)])
        hs = ctr["hb"] % HB_N; ctr["hb"] += 1
        S.add("dve", lambda e: e.scalar_tensor_tensor(out=hb[:, hs, :], in0=xs[:, s, :], scalar=ss[:, c:c + 1],
                                                      in1=gtile[:], op0=ALU.mult, op1=ALU.mult),
              reads=[(xname, s), ("ss", c), (gname,)], writes=[("hb", hs)])
        return (s, hs)

    def norm_s2(ctx, ps_ap, ps_key, dst_fn, dst_key):
        s, hs = ctx

        def tr(e):
            ins = None
            for kc in range(8):
                ins = e.transpose(ps_ap[:, kc, :], hb[:, hs, kc * 128:(kc + 1) * 128], ident[:])
            return ins
        S.add("pe", tr, reads=[("hb", hs), ("ident",)], writes=[ps_key])
        dst_fn(ps_ap, ps_key, dst_key)

    def norm_tile(src_rows_ap, gname, gtile, psT_, dst_fn, dst_key, src_is_loaded=None, pname="psT", xbuf=None):
        ctx = norm_s1(src_rows_ap, gname, gtile, loaded=src_is_loaded, xbuf=xbuf)
        pt = ctr["psT"] % 2; ctr["psT"] += 1
        norm_s2(ctx, psT_[pt], pskey(pname, pt), dst_fn, dst_key)
        return ctx[0]

    def mem_path():
        memT = M.alloc("memT", [128, 8, 256], BF16)
        for mt in range(2):
            def dst(ps, pskey_, dkey, mt=mt):
                evac_copy(memT[:, :, mt * 128:(mt + 1) * 128], ps[:], [pskey_], [dkey])
            norm_tile(memd[mt * 128:(mt + 1) * 128, :], "g_b", g_b, psT, dst, ("memT", mt))
        for pr in range(2):
            pi = ctr["pj"] % 4; ctr["pj"] += 1

            def mm(e, pr=pr, pi=pi):
                ins = None
                for kc in range(8):
                    ins = e.matmul(pj[pi][:, 0:256], lhsT=w_mk[:, kc, pr * 128:(pr + 1) * 128], rhs=memT[:, kc, :],
                                   start=(kc == 0), stop=(kc == 7))
                return ins
            S.add("pe", mm, reads=[("w_mk",), ("memT", 0), ("memT", 1)], writes=[pskey("pj", pi)])
            evac_copy(kmT[:, pr, :], pj[pi][:, 0:256], [pskey("pj", pi)], [("kmT", pr)])
        for mt in range(2):
            pi = ctr["pj"] % 4; ctr["pj"] += 1

            def mm(e, mt=mt, pi=pi):
                ins = None
                for kc in range(8):
                    ins = e.matmul(pj[pi][:, 0:256], lhsT=memT[:, kc, mt * 128:(mt + 1) * 128], rhs=w_mk[:, kc, 256:512],
                                   start=(kc == 0), stop=(kc == 7))
                return ins
            S.add("pe", mm, reads=[("w_mk",), ("memT", mt)], writes=[pskey("pj", pi)])
            evac_copy(vm[:, mt, :, 0:64], pj[pi][:, 0:256].rearrange("p (h d) -> p h d", d=64),
                      [pskey("pj", pi)], [("vm", mt)])
        dma("sp", g_b[:], gffn_d, reads=[], writes=[("g_b",)])


    groups = [[0, 1, 2, 3], [4, 5, 6, 7], [8, 9, 10, 11], [12, 13, 14, 15], [16, 17, 18, 19], [20]]
    groups_C = [[5, 6, 7, 8], [4], [9, 10, 11, 12], [13, 14, 15, 16], [17, 18, 19, 20]]

    def tile_T_stages(i, gb, j):
        def slot_fn():
            s = ctr["xs"] % XS_N; ctr["xs"] += 1
            return s

        def load_fn(s):
            dma("sp", xs[:, s, :], xw[i * 128:(i + 1) * 128, :], writes=[("xs", s)])

        def ps_fn():
            pt = ctr["psT"] % 2; ctr["psT"] += 1
            return psT[pt], pskey("psT", pt)

        def dst(ps, pskey_, dkey):
            evac_copy(hT[:, gb, :, j * 128:(j + 1) * 128], ps[:], [pskey_], [dkey], eng="dve")
        stages, _ = norm_stages(load_fn, "g_a", g_a, (xs, "xs"), slot_fn, ps_fn, dst, ("hT", gb, j),
                                save=(rstd_x[:, i:i + 1], ("rstd_x", i)))
        return stages

    def staged_order(stage_pairs, ahead=2):
        out = []
        n_ = len(stage_pairs)
        for k_ in range(n_ + ahead):
            if k_ < n_:
                out.append(stage_pairs[k_][0])
            if k_ - ahead >= 0:
                out.append(stage_pairs[k_ - ahead][1])
        return out

    def proj_fm(col0, n, gb, ntiles, wkey, evac):
        pi = ctr["pj"] % 4; ctr["pj"] += 1

        def mm(e):
            ins = None
            for kc in range(8):
                ins = e.matmul(pj[pi][:, 0:n], lhsT=w_a[:, kc, col0:col0 + 128], rhs=hT[:, gb, kc, 0:n],
                               start=(kc == 0), stop=(kc == 7))
            return ins
        S.add("pe", mm, reads=[wkey] + [("hT", gb, j) for j in range(ntiles)], writes=[pskey("pj", pi)])
        evac(pj[pi], pskey("pj", pi))

    deferred = []
    pre_B = {}

    def early_B():
        M.release("w_a")

    def group_items(gi, tiles):
        gb = gi % 2
        n = 128 * len(tiles)
        ntl = len(tiles)
        tok0 = tiles[0] * 128
        isq = gi >= 1
        q0 = (tiles[0] - Q0) * 128
        L = 16 + n
        items = []

        def u_item(pr):
            if isq:
                proj_fm(pr * 128, n, gb, ntl, ("w_a", 0),
                        lambda ps, pk: evac_copy(ub[:, pr, 16:16 + n], ps[:, 0:n], [pk], [("ub",)], eng="act"))
            else:
                proj_fm(pr * 128, n, gb, ntl, ("w_a", 0),
                        lambda ps, pk: evac_copy(ub[:, pr, 0:16], ps[:, n - 16:n], [pk], [("ub",)], eng="act"))

        def k_item(pr):
            c0 = 768 + pr * 128
            proj_fm(c0, n, gb, ntl, ("w_a", c0 // 256),
                    lambda ps, pk: evac_copy(kT[:, pr, tok0:tok0 + n], ps[:, 0:n], [pk], [("kT", i) for i in tiles], eng="act"))

        def v_item(j, i):
            pi = ctr["pj"] % 4; ctr["pj"] += 1

            def mm(e):
                ins = None
                for kc in range(8):
                    ins = e.matmul(pj[pi][:, 0:512], lhsT=hT[:, gb, kc, j * 128:(j + 1) * 128],
                                   rhs=w_a[:, kc, 1280:1792], start=(kc == 0), stop=(kc == 7))
                return ins
            S.add("pe", mm, reads=[("w_a", 5), ("w_a", 6), ("hT", gb, j)], writes=[pskey("pj", pi)])
            evac_copy(Vb[:, i, :, 0:64], pj[pi][:, 0:512].rearrange("p (h d) -> p h d", d=64),
                      [pskey("pj", pi)], [("V", i)], eng="act")

        def q_item(pr):
            c0 = 256 + pr * 128
            proj_fm(c0, n, gb, ntl, ("w_a", c0 // 256),
                    lambda ps, pk: evac_copy(qT[:, pr, q0:q0 + n], ps[:, 0:n], [pk], [("qT", i) for i in tiles], eng="act"))

        def qm_item(pr):
            c0 = 1792 + pr * 128
            proj_fm(c0, n, gb, ntl, ("w_a", 7),
                    lambda ps, pk: evac_copy(qmT[:, pr, q0:q0 + n], ps[:, 0:n], [pk], [("qmT", i) for i in tiles], eng="act"))

        def pool_mixer_ops():
            ops = []
            ops.append(lambda: S.add("dve", lambda e: e.tensor_tensor(out=sA[:, :, 1:L], in0=ub[:, :, 1:L], in1=ub[:, :, 0:L - 1], op=ALU.add),
                                     reads=[("ub",)], writes=[("sA",)]))
            ops.append(lambda: S.add("dve", lambda e: e.tensor_tensor(out=sB[:, :, 3:L], in0=sA[:, :, 3:L], in1=sA[:, :, 1:L - 2], op=ALU.add),
                                     reads=[("sA",)], writes=[("sB",)]))
            ops.append(lambda: S.add("dve", lambda e: e.tensor_tensor(out=sA[:, 1, 7:L], in0=sB[:, 1, 7:L], in1=sB[:, 1, 3:L - 4], op=ALU.add),
                                     reads=[("sB",)], writes=[("sA",)]))
            ops.append(lambda: S.add("dve", lambda e: e.tensor_tensor(out=sB[:, 1, 15:L], in0=sA[:, 1, 15:L], in1=sA[:, 1, 7:L - 8], op=ALU.add),
                                     reads=[("sA",)], writes=[("sB",)]))
            srcs = [(sA, 0, 0, 2.0), (sB, 0, 1, 4.0), (sA, 1, 0, 8.0), (sB, 1, 1, 16.0)]
            for (sbuf_, t, hlf, w) in srcs:
                p0, p1 = hlf * 64, hlf * 64 + 64
                kk = ("sA",) if sbuf_ is sA else ("sB",)
                ops.append(lambda sbuf_=sbuf_, t=t, p0=p0, p1=p1, w=w, kk=kk: S.add("dve", lambda e: e.scalar_tensor_tensor(
                    out=pT[p0:p1, t, 0:n], in0=sbuf_[p0:p1, t, 16:L], scalar=1.0 / w, in1=ub[p0:p1, t, 16:L],
                    op0=ALU.mult, op1=ALU.subtract),
                    reads=[kk, ("ub",)], writes=[("pT",)]))
            if 5 in tiles:
                fo = (5 - tiles[0]) * 128

                def fix():
                    for (sbuf_, t, hlf, w) in srcs:
                        p0, p1 = hlf * 64, hlf * 64 + 64
                        kk = ("sA",) if sbuf_ is sA else ("sB",)
                        S.add("dve", lambda e, sbuf_=sbuf_, t=t, p0=p0, p1=p1: e.tensor_tensor(
                            out=ptmp[p0:p1, :], in0=sbuf_[p0:p1, t, 16 + fo:32 + fo], in1=invcnt[p0:p1, t, :], op=ALU.mult),
                            reads=[kk, ("invcnt",)], writes=[("ptmp",)])
                        S.add("dve", lambda e, t=t, p0=p0, p1=p1: e.tensor_tensor(
                            out=pT[p0:p1, t, fo:fo + 16], in0=ptmp[p0:p1, :], in1=ub[p0:p1, t, 16 + fo:32 + fo], op=ALU.subtract),
                            reads=[("ptmp",), ("ub",)], writes=[("pT",)])
                ops.append(fix)
            ops.append(lambda: S.add("dve", lambda e: e.tensor_copy(out=ub[:, :, 0:16], in_=ub[:, :, n:n + 16]),
                                     reads=[("ub",)], writes=[("ub",)]))
            return ops

        def pool_mm(t):
            pi = ctr["pj"] % 4; ctr["pj"] += 1
            S.add("pe", lambda e: e.matmul(pj[pi][:, 0:n], lhsT=wpbd[:, t, :], rhs=pT[:, t, 0:n], start=True, stop=True),
                  reads=[("pT",)] + wpbd_keys, writes=[pskey("pj", pi)])
            S.add("act", lambda e: e.activation(out=ypT[:, t, q0:q0 + n], in_=pj[pi][:, 0:n], func=AF.Copy,
                                                scale=pscale[:, t:t + 1]),
                  reads=[pskey("pj", pi), ("pscale",)], writes=[("ypT", t, q0)])

        for pr in range(2):
            items.append(lambda pr=pr: u_item(pr))
        for pr in range(4):
            items.append(lambda pr=pr: k_item(pr))
        mix = pool_mixer_ops() if isq else []
        for j, i in enumerate(tiles):
            items.append(lambda j=j, i=i: v_item(j, i))
            if mix:
                items.append(mix.pop(0))
        if isq:
            for pr in range(4):
                items.append(lambda pr=pr: q_item(pr))
                if mix:
                    items.append(mix.pop(0))
            for pr in range(2):
                items.append(lambda pr=pr: qm_item(pr))
                if mix:
                    items.append(mix.pop(0))
            items.extend(mix)
            if gi == len(groups) - 1:
                items.append(early_B)
            for t in range(2):
                deferred.append(lambda t=t: pool_mm(t))
        return items

    T_items = [wavefront([tile_T_stages(i, gi % 2, j) for j, i in enumerate(tiles)])
               for gi, tiles in enumerate(groups)]
    T_items[0][0]()
    const_dmas.pop(0)()
    T_items[0][1]()
    for fn_ in const_dmas:
        fn_()
    for th in T_items[0][2:]:
        th()
    late_setup()
    for gi, tiles in enumerate(groups):
        if gi == 1:
            mem_path()
        carry = list(deferred)
        del deferred[:]
        P = group_items(gi, tiles)
        P[6:6] = carry
        T = T_items[gi + 1] if gi + 1 < len(groups) else []
        ti = 0
        for k_, p in enumerate(P):
            p()
            want = min(len(T), (len(T) * (k_ + 1) * 10) // (len(P) * 6))
            while ti < want:
                T[ti](); ti += 1
        while ti < len(T):
            T[ti](); ti += 1
    for th in deferred:
        th()

    ps_close()
    for nm in ("hT", "w_mk", "wpbd", "wp32", "id32", "pscale", "invcnt", "ub", "sA", "sB", "pT", "ptmp", "memT"):
        M.release(nm)

    stop_after = (debug or {}).get("_stop", "D")
    if debug and stop_after == "A":
        stg = M.alloc("dbgstg", [128, NT * 8 * 65], F32)
        if "kT" in debug:
            S.add("dve", lambda e: e.tensor_copy(out=stg[:, 0:4 * WIN], in_=kT[:].rearrange("p a b -> p (a b)")),
                  reads=[("kT", i) for i in range(NT)], writes=[("dbgstg",)])
            dma("sp", dbg_d["kT"], stg[:, 0:4 * WIN], reads=[("dbgstg",)], tag="out")
        if "ypT" in debug:
            S.add("dve", lambda e: e.tensor_copy(out=stg[:, 0:2 * NQTOK], in_=ypT[:].rearrange("p a b -> p (a b)")),
                  reads=[("ypT", t, (g[0] - Q0) * 128) for t in range(2) for g in groups[1:]], writes=[("dbgstg",)])
            dma("sp", dbg_d["ypT"], stg[:, 0:2 * NQTOK], reads=[("dbgstg",)], tag="out")
        if "V" in debug:
            S.add("dve", lambda e: e.tensor_copy(out=stg[:], in_=Vb[:].rearrange("p a b c -> p (a b c)")),
                  reads=[("V", i) for i in range(NT)] + [("Vones", h) for h in range(8)], writes=[("dbgstg",)])
            dma("sp", dbg_d["V"], stg[:], reads=[("dbgstg",)], tag="out")

    def phase_B():
        BPOS = {0: 0, 3: 1, 4: 2, 1: 3, 2: 4}
        OT = M.alloc("OT", [128, 6, NQTOK], BF16)
        PT = M.alloc("PT", [128, 4, 640], BF16)
        PmT = M.alloc("PmT", [128, 4, 256], BF16)
        On = M.alloc("On", [128, 2, 12, 64], BF16)
        den = M.alloc("den", [128, 2, 12, 1], F32)
        w_upp = M.alloc("w_upp", [128, 2, D], BF16)
        w_upa = M.alloc("w_upa", [128, 4, D], BF16)
        w_upm = M.alloc("w_upm", [128, 2, D], BF16)
        dma("pool", w_upp[:], w_upp_d.rearrange("(kc p) c -> p kc c", p=128), writes=[("w_upp",)])
        dma("pool", w_upa[:], w_upa_d.rearrange("(kc p) c -> p kc c", p=128), writes=[("w_upa",)])
        dma("pool", w_upm[:], w_upm_d.rearrange("(kc p) c -> p kc c", p=128), writes=[("w_upm",)])
        M.release("xs")
        w_out = M.alloc("w_out", [128, 8, D], BF16)
        dma("pool", w_out[:], w_out_d.rearrange("(kc p) c -> p kc c", p=128), writes=[("w_out",)])
        pre_B["mg"] = M.alloc("mg", [128, 1, 8, 512], BF16)
        pre_B["tb"] = M.alloc("tb", [128, 4, 512], F32)
        w_g = {}
        for br in range(2):
            try:
                t_ = M.alloc("w_g%d_0" % br, [128, 8, 512], BF16)
            except RuntimeError:
                break
            c0 = 2048 + br * 1024
            dma("pool", t_[:], w_in_v[:, :, c0:c0 + 512], writes=[("w_g%d_0" % br,)])
            for nt in range(4):
                w_g[(br, nt)] = (t_, "w_g%d_0" % br, nt * 128)

        for kc in range(8):
            dma("pool", wfu_bf[kc * 128:(kc + 1) * 128, :], w_fu_d[kc * 128:(kc + 1) * 128, :], writes=[("wfu_bf", kc)])
        for rb in range(8):
            dma("pool", wfd_bf[rb * 352:(rb + 1) * 352, :], w_fd_d[rb * 352:(rb + 1) * 352, :], writes=[("wfd_bf", rb)])

        ps_open()
        NS = 3
        psS = [ps_alloc("psS%d" % i, [128, 1024], F32) for i in range(NS)]
        psST = [t_[:, 0:512].bitcast(BF16).rearrange("p (k c) -> p k c", k=8) for t_ in psS]
        psO = ps_alloc("psO", [128, 2, 512], F32)
        c = {"S": 0, "tmp": 0, "PT": 0, "Pm": 0, "On": 0, "T": 0}

        def unit(un):
            return un // 6, (un % 6) * 65

        def band_head(uq, h):
            u = uq + Q0
            pr, base = h // 2, (h % 2) * 64
            sb = c["S"] % NS; c["S"] += 1
            pb = c["PT"] % 4; c["PT"] += 1
            bank, col = unit(h)

            def mmS(e):
                ins = None
                for b in range(5):
                    kt = u - 4 + b
                    pos = BPOS[b]
                    ins = e.matmul(psS[sb][:, pos * 128:(pos + 1) * 128], lhsT=kT[base:base + 64, pr, kt * 128:(kt + 1) * 128],
                                   rhs=qT[base:base + 64, pr, uq * 128:(uq + 1) * 128], start=True, stop=True)
                return ins
            S.add("pe", mmS, reads=[("kT", u - 4 + b) for b in range(5)] + [("qT", u)], writes=[("psS%d" % sb,)])
            S.add("dve", lambda e: e.tensor_tensor(out=psS[sb][:, 0:384], in0=psS[sb][:, 0:384], in1=bias8[:, h, 0:384], op=ALU.add),
                  reads=[("psS%d" % sb,), ("bias8",)], writes=[("psS%d" % sb,)])
            S.add("act", lambda e: e.activation(out=PT[:, pb, :], in_=psS[sb][:, 0:640], func=AF.Exp, scale=0.125),
                  reads=[("psS%d" % sb,)], writes=[("PT", pb)])

            def mmO(e):
                ins = None
                for b in range(5):
                    kt = u - 4 + b
                    pos = BPOS[b]
                    ins = e.matmul(psO[:, bank, col:col + 65], lhsT=PT[:, pb, pos * 128:(pos + 1) * 128], rhs=Vb[:, kt, h, :],
                                   start=(b == 0), stop=(b == 4))
                return ins
            return lambda: S.add("pe", mmO, reads=[("PT", pb), ("Vones", h)] + [("V", u - 4 + b) for b in range(5)],
                                 writes=[("psO", bank)])

        def mem_head(uq, hm):
            u = uq + Q0
            pr, base = hm // 2, (hm % 2) * 64
            sb = c["S"] % NS; c["S"] += 1
            pm = c["Pm"] % 4; c["Pm"] += 1
            bank, col = unit(8 + hm)

            def mmS(e):
                ins = None
                for mt in range(2):
                    ins = e.matmul(psS[sb][:, mt * 128:(mt + 1) * 128], lhsT=kmT[base:base + 64, pr, mt * 128:(mt + 1) * 128],
                                   rhs=qmT[base:base + 64, pr, uq * 128:(uq + 1) * 128], start=True, stop=True)
                return ins
            S.add("pe", mmS, reads=[("kmT", pr), ("qmT", u)], writes=[("psS%d" % sb,)])
            S.add("act", lambda e: e.activation(out=PmT[:, pm, :], in_=psS[sb][:, 0:256], func=AF.Exp, scale=0.125),
                  reads=[("psS%d" % sb,)], writes=[("PmT", pm)])

            def mmO(e):
                ins = None
                for mt in range(2):
                    ins = e.matmul(psO[:, bank, col:col + 65], lhsT=PmT[:, pm, mt * 128:(mt + 1) * 128], rhs=vm[:, mt, hm, :],
                                   start=(mt == 0), stop=(mt == 1))
                return ins
            return lambda: S.add("pe", mmO, reads=[("PmT", pm), ("vm", 0), ("vm", 1), ("vmones",)], writes=[("psO", bank)])

        obs = {}

        def finish_bank(uq, bank):
            if bank == 0:
                obs[uq] = c["On"] % 2; c["On"] += 1
            ob = obs[uq]
            v3 = psO[:, bank, 0:390].rearrange("p (u c) -> p u c", c=65)
            dsl = den[:, ob, bank * 6:(bank + 1) * 6, :]
            S.add("dve", lambda e: e.tensor_scalar_max(out=dsl, in0=v3[:, :, 64:65], scalar1=1e-30),
                  reads=[("psO", bank)], writes=[("den", ob, bank)])
            S.add("dve", lambda e: e.reciprocal(out=dsl, in_=dsl),
                  reads=[("den", ob, bank)], writes=[("den", ob, bank)])
            S.add("dve", lambda e: e.tensor_tensor(
                out=On[:, ob, bank * 6:(bank + 1) * 6, :], in0=v3[:, :, 0:64], in1=dsl.to_broadcast([128, 6, 64]),
                op=ALU.mult),
                reads=[("psO", bank), ("den", ob, bank)], writes=[("On", ob, bank)])

        def finish_tile_T(uq):
            ob = obs[uq]
            pt = c["S"] % NS; c["S"] += 1
            Onf = On[:, ob, :, :].rearrange("p u d -> p (u d)")

            def tr(e):
                ins = None
                for blk in range(6):
                    ins = e.transpose(psST[pt][:, blk, :], Onf[:, blk * 128:(blk + 1) * 128], ident[:])
                return ins
            S.add("pe", tr, reads=[("On", ob, 0), ("On", ob, 1), ("ident",)], writes=[("psS%d" % pt,)])
            evac_copy(OT[:, :, uq * 128:(uq + 1) * 128], psST[pt][:, 0:6, :], [("psS%d" % pt,)], [("OT", uq)], eng="act")

        ORDER = (0, 1, 2, 3, 4, 5, 8, 9, 10, 11, 6, 7)
        work = [(uq, un) for uq in range(NQ) for un in ORDER]
        pend = []
        later = []
        DEPTH = 2

        def step_later():
            for ent in list(later):
                ent[0] -= 1
                if ent[0] <= 0:
                    later.remove(ent)
                    ent[1]()

        def after_consume(uq0, un0):
            if un0 == 5:
                finish_bank(uq0, 0)
            if un0 == ORDER[-1]:
                finish_bank(uq0, 1)
                later.append([3, lambda uq0=uq0: finish_tile_T(uq0)])

        for (uq, un) in work:
            cons = band_head(uq, un) if un < 8 else mem_head(uq, un - 8)
            pend.append((cons, uq, un))
            if len(pend) > DEPTH:
                c0_, uq0, un0 = pend.pop(0)
                c0_()
                step_later()
                after_consume(uq0, un0)
        while pend:
            c0_, uq0, un0 = pend.pop(0)
            c0_()
            step_later()
            after_consume(uq0, un0)
        while later:
            step_later()
        ps_close()
        for nm in ("bias8", "PT", "PmT", "On", "den", "kT", "V", "qT", "qmT", "kmT", "vm"):
            M.release(nm)
        return OT, w_upp, w_upa, w_upm, w_out, w_g

    if stop_after in ("B", "C", "D"):
        OT, w_upp, w_upa, w_upm, w_out, w_g = phase_B()

    if debug and stop_after == "B":
        stg = M.alloc("dbgstg", [128, 6 * NQTOK], F32)
        S.add("dve", lambda e: e.tensor_copy(out=stg[:], in_=OT[:].rearrange("p a b -> p (a b)")),
              reads=[("OT", uq) for uq in range(NQ)], writes=[("dbgstg",)])
        dma("sp", dbg_d["OT"], stg[:], reads=[("dbgstg",)], tag="out")

    ffw = {}
    ffd = {}
    w_fu_v = wfu_bf.rearrange("(kc p) c -> p kc c", p=128)
    w_fd_v = wfd_bf.rearrange("(j p) c -> p j c", p=128)
    fu_keys = [("wfu_bf", kc) for kc in range(8)]
    fd_keys = [("wfd_bf", rb) for rb in range(8)]
    ffn_order = []
    for ch in range(6):
        ffn_order += [("g", ch), ("v", ch), ("d", ch)]

    def ffn_prefetch(limit=None, strict=False, defer=None):
        n_new = 0
        for key in ffn_order:
            if key in ffw:
                continue
            if limit is not None and n_new >= limit:
                return
            kind, ch = key
            c0 = ch * 512
            cw = min(512, DFF - c0)
            nm = "ff%s%d" % (kind, ch)
            try:
                if kind == "d":
                    j0, j1 = ch * 4, min(22, ch * 4 + 4)
                    t_ = M.alloc(nm, [128, j1 - j0, D], BF16)
                    for j_ in range(j0, j1):
                        ffd[j_] = (t_, nm, j_ - j0)
                else:
                    t_ = M.alloc(nm, [128, 8, cw], BF16)
            except RuntimeError:
                if not strict:
                    return
                if kind != "d":
                    raise
                ffw[key] = None
                for j_ in range(j0, j1):
                    nmj = "ffdj%d" % j_
                    tj = M.alloc(nmj, [128, 1, D], BF16)
                    ffd[j_] = (tj, nmj, 0)
                    issue = (lambda tj=tj, j_=j_, nmj=nmj: dma("sp", tj[:], w_fd_v[:, j_:j_ + 1, :], reads=fd_keys, writes=[(nmj,)]))
                    if defer is None:
                        issue()
                    else:
                        defer.append(issue)
                n_new += 1
                continue
            ffw[key] = t_
            if kind == "d":
                issue = (lambda t_=t_, j0=j0, j1=j1, nm=nm: dma("sp", t_[:], w_fd_v[:, j0:j1, :], reads=fd_keys, writes=[(nm,)]))
            else:
                off = c0 if kind == "g" else DFF + c0
                issue = (lambda t_=t_, off=off, cw=cw, nm=nm: dma("sp", t_[:], w_fu_v[:, :, off:off + cw], reads=fu_keys, writes=[(nm,)]))
            if defer is None:
                issue()
            else:
                defer.append(issue)
            n_new += 1

    ffn_state = {}

    def ffn_tile0_prologue(ps_fn):
        XD_ = 6
        g_c_ = M.alloc("g_c", [128, D], F32)
        ffn_state["g_c"] = g_c_
        dma("sp", g_c_[:], gfin_d, writes=[("g_c",)])
        xsd_ = M.alloc("xsd", [128, XD_, D], F32)
        h2T_ = M.alloc("h2T", [128, 2, 8, 258], BF16)
        ffn_state["xsd"] = xsd_; ffn_state["h2T"] = h2T_
        lists = []
        for sub in range(2):
            def slot_fn(sub=sub):
                return sub

            def load_fn(s, sub=sub):
                dma("sp", xsd_[:, s, :], xmid_d[sub * 128:(sub + 1) * 128, :], reads=[("xmid", sub)], writes=[("xsd", s)])

            def dst(ps, pskey_, dkey, sub=sub):
                evac_copy(h2T_[:, 0, :, 2 + sub * 128:2 + (sub + 1) * 128], ps[:], [pskey_], [dkey], eng="act")
            stages, _ = norm_stages(load_fn, "g_b", g_b, (xsd_, "xsd"), slot_fn, ps_fn, dst, ("h2T", 0, 1 + sub))
            lists.append(stages)
        def halo():
            S.add("pool", lambda e: e.tensor_copy(out=h2T_[:, 0, :, 0:2], in_=h2halo[:]),
                  reads=[("h2halo",)], writes=[("h2T", 0, 0)])
        return lists, halo

    def phase_C():
        XC = 8
        xsc = M.alloc("xsc", [128, XC, D], F32)
        hTc = M.alloc("hT", [128, 2, 8, 512], BF16)
        tg = M.alloc("tg", [128, 2, 512], F32)
        bgate = M.alloc("bgate", [128, 24], F32)
        wg_names = sorted(set(v[1] for v in w_g.values()))
        wg_dmas = []
        for q4 in range(4):
            for br in range(3):
                if (br, 2 * q4) in w_g:
                    continue
                nm = "w_g%d_q%d" % (br, q4)
                t_ = M.alloc(nm, [128, 8, 256], BF16)
                c0 = 2048 + br * 1024 + q4 * 256
                wg_dmas.append(lambda t_=t_, c0=c0, nm=nm: dma("pool", t_[:], w_in_v[:, :, c0:c0 + 256], writes=[(nm,)]))
                wg_names.append(nm)
                for k_ in range(2):
                    w_g[(br, 2 * q4 + k_)] = (t_, nm, k_ * 128)
        tb = pre_B["tb"]
        a01 = M.alloc("a01", [128, 1, 512], F32)
        mg = pre_B["mg"]
        dma("sp", bgate[:], bgate_d, writes=[("bgate",)])
        ps_open()
        psTc = [ps_alloc("psTc0", [128, 8, 128], BF16), ps_alloc("psTc1", [128, 8, 128], BF16)]
        psG = [ps_alloc("psG0", [128, 512], F32), ps_alloc("psG1", [128, 512], F32)]
        psY = [ps_alloc("psY0", [128, 512], F32), ps_alloc("psY1", [128, 512], F32)]
        psX = ps_alloc("psX", [128, 2, 512], F32)
        c = {"G": 0, "Y": 0, "tg": 0, "tb": 0, "xs": 0, "T": 0}
        ysrc = [(w_upp, "w_upp", 2, ypT, lambda q0, n: [("ypT", t, (g_[0] - Q0) * 128) for t in range(2) for g_ in groups[1:]]),
                (w_upa, "w_upa", 4, OT, None), (w_upm, "w_upm", 2, OT, None)]
        qgroups = groups_C
        slots_of = {}

        def prologue_pairs(gi):
            tiles = qgroups[gi]
            gb = gi % 2
            slots_of[gi] = [None] * len(tiles)
            lists = []
            for j, i in enumerate(tiles):
                def slot_fn(j=j):
                    s = c["xs"] % XC; c["xs"] += 1
                    slots_of[gi][j] = s
                    return s

                def load_fn(s, i=i):
                    dma("sp", xsc[:, s, :], xw[i * 128:(i + 1) * 128, :], writes=[("xsc", s)])

                def ps_fn():
                    pt = c["T"] % 2; c["T"] += 1
                    return psTc[pt], ("psTc%d" % pt,)

                def dst(ps, pskey_, dkey, j=j):
                    evac_copy(hTc[:, gb, :, j * 128:(j + 1) * 128], ps[:], [pskey_], [dkey], eng=("act" if gi == 0 else None))
                stages, _ = norm_stages(load_fn, "g_a", g_a, (xsc, "xsc"), slot_fn, ps_fn, dst, ("hT", gb, j),
                                        reuse=(rstd_x[:, i:i + 1], ("rstd_x", i)))
                lists.append(stages)
            return wavefront(lists)

        def ntile(gi, nt):
            tiles = qgroups[gi]
            gb = gi % 2
            n = 128 * len(tiles)
            q0 = (tiles[0] - Q0) * 128
            hkeys = [("hT", gb, j) for j in range(len(tiles))]
            otkeys = [("OT", i - Q0) for i in tiles]
            terms = []
            for br in range(3):
                gs = c["G"] % 2; c["G"] += 1
                ys = c["Y"] % 2; c["Y"] += 1
                ti = c["tg"] % 2; c["tg"] += 1
                bi = c["tb"] % 4; c["tb"] += 1
                wgt, wgname, gc0 = w_g[(br, nt)]

                def mmG(e, gs=gs, wgt=wgt, gc0=gc0):
                    ins = None
                    for kc in range(8):
                        ins = e.matmul(psG[gs][:, 0:n], lhsT=wgt[:, kc, gc0:gc0 + 128], rhs=hTc[:, gb, kc, 0:n],
                                       start=(kc == 0), stop=(kc == 7))
                    return ins
                S.add("pe", mmG, reads=[(wgname,)] + hkeys, writes=[("psG%d" % gs,)])
                wt, wname, nk, src_, keyfn = ysrc[br]
                koff = 4 if br == 2 else 0

                def mmY(e, ys=ys, wt=wt, nk=nk, src_=src_, koff=koff):
                    ins = None
                    for kc in range(nk):
                        ins = e.matmul(psY[ys][:, 0:n], lhsT=wt[:, kc, nt * 128:(nt + 1) * 128],
                                       rhs=src_[:, koff + kc, q0:q0 + n], start=(kc == 0), stop=(kc == nk - 1))
                    return ins
                rk = keyfn(q0, n) if keyfn else otkeys
                S.add("pe", mmY, reads=[(wname,)] + rk, writes=[("psY%d" % ys,)])
                S.add("act", lambda e, gs=gs, ti=ti, br=br: e.activation(
                    out=tg[:, ti, 0:n], in_=psG[gs][:, 0:n], func=AF.Sigmoid, bias=bgate[:, br * 8 + nt:br * 8 + nt + 1]),
                    reads=[("psG%d" % gs,), ("bgate",)], writes=[("tg", ti)])
                S.add("dve", lambda e, ys=ys, ti=ti, bi=bi: e.tensor_tensor(
                    out=tb[:, bi, 0:n], in0=psY[ys][:, 0:n], in1=tg[:, ti, 0:n], op=ALU.mult),
                    reads=[("psY%d" % ys,), ("tg", ti)], writes=[("tb", bi)])
                terms.append(bi)
            S.add("pool", lambda e: e.tensor_tensor(out=a01[:, 0, 0:n], in0=tb[:, terms[0], 0:n], in1=tb[:, terms[1], 0:n],
                                                    op=ALU.add),
                  reads=[("tb", terms[0]), ("tb", terms[1])], writes=[("a01", 0)])
            S.add("pool", lambda e: e.tensor_tensor(out=mg[:, 0, nt, 0:n], in0=a01[:, 0, 0:n], in1=tb[:, terms[2], 0:n],
                                                    op=ALU.add),
                  reads=[("a01", 0), ("tb", terms[2])], writes=[("mg", 0, nt)])

        def out_stage(gi, hooks=None):
            tiles = qgroups[gi]
            mkeys = [("mg", 0, nt) for nt in range(8)]
            for j, i in enumerate(tiles):
                s = slots_of[gi][j]
                for fn_ in (hooks or {}).get(j, []):
                    fn_()

                for hf in range(2):
                    def mmX(e, j=j, hf=hf):
                        ins = None
                        for kc in range(8):
                            ins = e.matmul(psX[:, hf, :], lhsT=mg[:, 0, kc, j * 128:(j + 1) * 128],
                                           rhs=w_out[:, kc, hf * 512:(hf + 1) * 512], start=(kc == 0), stop=(kc == 7))
                        return ins
                    S.add("pe", mmX, reads=mkeys + [("w_out",)], writes=[("psX", hf)])
                    S.add("dve", lambda e, s=s, hf=hf: e.tensor_tensor(
                        out=xsc[:, s, hf * 512:(hf + 1) * 512], in0=psX[:, hf, :], in1=xsc[:, s, hf * 512:(hf + 1) * 512],
                        op=ALU.add),
                        reads=[("psX", hf), ("xsc", s)], writes=[("xsc", s)])
                if i >= 5:
                    r = i - 5
                    dma("sp", xmid_d[r * 128:(r + 1) * 128, :], xsc[:, s, :], reads=[("xsc", s)], writes=[("xmid", r)])
                else:
                    def dst(ps, pskey_, dkey):
                        S.add("act", lambda e: e.activation(out=h2halo[:], in_=ps[:, :, 126:128], func=AF.Copy,
                                                            scale=flag[:, 0:1]),
                              reads=[pskey_, ("flag",)], writes=[dkey])
                    ctx = norm_s1(None, "g_b", g_b, loaded=s, xbuf=(xsc, "xsc"))
                    pt = c["T"] % 2; c["T"] += 1
                    norm_s2(ctx, psTc[pt], ("psTc%d" % pt,), dst, ("h2halo",))

        pro0 = prologue_pairs(0)
        if wg_dmas:
            wg_dmas.pop(0)()
        for th in pro0[:11]:
            th()
        for fn_ in wg_dmas:
            fn_()
        for th in pro0[11:]:
            th()
        for gi in range(len(qgroups)):
            nxt = prologue_pairs(gi + 1) if gi + 1 < len(qgroups) else []
            ti_ = 0
            last = gi == len(qgroups) - 1
            for nt in range(8):
                ntile(gi, nt)
                want = min(len(nxt), (len(nxt) * (nt + 1)) // 5)
                while ti_ < want:
                    nxt[ti_](); ti_ += 1
                if last and nt == 3:
                    early = [nm for nm in wg_names if all(v[1] != nm or k[1] < 4 for k, v in w_g.items())]
                    for nm in early:
                        M.release(nm)
                        wg_names.remove(nm)
                    ffn_prefetch()
            if last:
                for nm in ["bgate", "hT", "tg", "tb", "a01", "w_upp", "w_upa", "w_upm", "OT", "ypT", "g_a"] + list(wg_names):
                    M.release(nm)
                def ps_fn_c():
                    pt = c["T"] % 2; c["T"] += 1
                    return psTc[pt], ("psTc%d" % pt,)
                lists0, halo0 = ffn_tile0_prologue(ps_fn_c)
                for sub in range(2):
                    lists0[sub][0]()
                    lists0[sub][1]()
                ffn_prefetch(limit=6)
                hooks = {2: [lists0[0][2], lists0[1][2]]}
                out_stage(gi, hooks)
                for fn_ in (lists0[0][3], lists0[1][3], halo0, ffn_prefetch):
                    fn_()
            else:
                out_stage(gi)
        ps_close()
        for nm in ("w_out", "mg", "xsc"):
            M.release(nm)

    if stop_after in ("C", "D"):
        phase_C()

    if debug and stop_after == "C":
        xdb = M.alloc("xdb", [128, 4, D], F32)
        for r in range(16):
            s = r % 4
            dma("sp", xdb[:, s, :], xmid_d[r * 128:(r + 1) * 128, :], reads=[("xmid", r)], writes=[("xdb", s)])
            dma("sp", dbg_d["xmid"][r * 128:(r + 1) * 128, :], xdb[:, s, :], reads=[("xdb", s)], tag="out")

    def phase_D():
        XD = 6
        xsd = ffn_state["xsd"]
        g_c = ffn_state["g_c"]
        h2T_pre = None
        h2T = ffn_state["h2T"]
        NR = 3
        cg = M.alloc("cg", [128, NR, 256], F32)
        cv = M.alloc("cv", [128, NR, 256], F32)
        gl = M.alloc("gl", [128, 2, 256], F32)
        mb = M.alloc("mb", [128, 4, 256], BF16)
        ob = M.alloc("ob", [128, 2, D], F32)
        late_dmas = []
        ffn_prefetch(strict=True, defer=late_dmas)
        ps_open()
        NPA = 4
        psA = [ps_alloc("psA%d" % i, [128, 512], F32) for i in range(NPA)]
        psAT = [t_[:].bitcast(BF16).rearrange("p (k c) -> p k c", k=8) for t_ in psA]
        psXD = [ps_alloc("psXD%d" % i, [128, 2, 512], F32) for i in range(2)]
        c = {"r": 0, "gl": 0, "mb": 0, "ob": 0, "xs": 2, "pa": 0}

        sched_at = {}

        def at(it, fn):
            sched_at.setdefault(it, []).append(fn)

        slots = {}

        def plan_prologue(tt, it0):
            tbuf = tt % 2
            slots[tt] = [None, None]
            per_sub = []
            for sub in range(2):
                r = tt * 2 + sub

                def slot_fn(sub=sub):
                    s = c["xs"] % XD; c["xs"] += 1
                    slots[tt][sub] = s
                    return s

                def load_fn(s, r=r):
                    dma("sp", xsd[:, s, :], xmid_d[r * 128:(r + 1) * 128, :], reads=[("xmid", r)], writes=[("xsd", s)])

                def ps_fn():
                    pa = c["pa"] % NPA; c["pa"] += 1
                    return psAT[pa], ("psA%d" % pa,)

                def dst(ps, pskey_, dkey, sub=sub):
                    evac_copy(h2T[:, tbuf, :, 2 + sub * 128:2 + (sub + 1) * 128], ps[:], [pskey_], [dkey], eng="act")
                stages, _ = norm_stages(load_fn, "g_b", g_b, (xsd, "xsd"), slot_fn, ps_fn, dst, ("h2T", tbuf, 1 + sub))
                per_sub.append(stages)

            def halo():
                if tt == 0:
                    S.add("pool", lambda e: e.tensor_copy(out=h2T[:, tbuf, :, 0:2], in_=h2halo[:]),
                          reads=[("h2halo",)], writes=[("h2T", tbuf, 0)])
                else:
                    S.add("pool", lambda e: e.tensor_copy(out=h2T[:, tbuf, :, 0:2], in_=h2T[:, 1 - tbuf, :, 256:258]),
                          reads=[("h2T", 1 - tbuf, 2)], writes=[("h2T", tbuf, 0)])
            offs = (0, 9, 12, 15)
            for k_ in range(4):
                for sub in range(2):
                    at(it0 + offs[k_] + (sub if k_ else 0), per_sub[sub][k_])
            at(it0 + 17, halo)

        def produce(tt, j):
            tbuf = tt % 2
            hk = [("h2T", tbuf, q) for q in range(3)]
            ri = c["r"] % NR; c["r"] += 1
            for gv in range(2):
                pa = c["pa"] % NPA; c["pa"] += 1
                col0 = gv * DFF + j * 128
                jj = gv * 22 + j
                cbuf, cname = (cg, "cg") if gv == 0 else (cv, "cv")

                wch = ffw[("g" if gv == 0 else "v", (j * 128) // 512)]
                wc0 = (j * 128) % 512
                wnm = "ff%s%d" % ("g" if gv == 0 else "v", (j * 128) // 512)

                def mmA(e, pa=pa, wch=wch, wc0=wc0):
                    ins = None
                    for kc in range(8):
                        ins = e.matmul(psA[pa][:, 0:258], lhsT=wch[:, kc, wc0:wc0 + 128], rhs=h2T[:, tbuf, kc, :],
                                       start=(kc == 0), stop=(kc == 7))
                    return ins
                S.add("pe", mmA, reads=[(wnm,)] + hk, writes=[("psA%d" % pa,)])
                S.add("act", lambda e, pa=pa, cbuf=cbuf, jj=jj: e.activation(
                    out=cbuf[:, ri, :], in_=psA[pa][:, 2:258], func=AF.Identity, bias=convb[:, jj:jj + 1],
                    scale=convw[:, jj, 2:3]),
                    reads=[("psA%d" % pa,), ("convw",), ("convb",)], writes=[(cname, ri)])
                for tap in (1, 0):
                    S.add("dve", lambda e, pa=pa, cbuf=cbuf, jj=jj, tap=tap: e.scalar_tensor_tensor(
                        out=cbuf[:, ri, :], in0=psA[pa][:, tap:tap + 256], scalar=convw[:, jj, tap:tap + 1],
                        in1=cbuf[:, ri, :], op0=ALU.mult, op1=ALU.add),
                        reads=[("psA%d" % pa,), ("convw",), (cname, ri)], writes=[(cname, ri)])
            return (tt, j, ri)

        def mid(ctx):
            tt, j, ri = ctx
            gi_ = c["gl"] % 2; c["gl"] += 1
            S.add("act", lambda e: e.activation(out=gl[:, gi_, :], in_=cg[:, ri, :], func=AF.Gelu),
                  reads=[("cg", ri)], writes=[("gl", gi_)])
            mi = c["mb"] % 4; c["mb"] += 1
            S.add("pool", lambda e: e.tensor_tensor(out=mb[:, mi, :], in0=gl[:, gi_, :], in1=cv[:, ri, :], op=ALU.mult),
                  reads=[("gl", gi_), ("cv", ri)], writes=[("mb", mi)])
            return (tt, j, mi)

        def consume(ctx, it):
            tt, j, mi = ctx
            wd, wdname, wdi = ffd[j]

            def mmD(e):
                ins = None
                for sub in range(2):
                    for hf in range(2):
                        ins = e.matmul(psXD[sub][:, hf, :], lhsT=mb[:, mi, sub * 128:(sub + 1) * 128],
                                       rhs=wd[:, wdi, hf * 512:(hf + 1) * 512], start=(j == 0), stop=(j == 21))
                return ins
            S.add("pe", mmD, reads=[("mb", mi), (wdname,)], writes=[("psXD0",), ("psXD1",)])
            if j == 21:
                plan_epilogue(tt, it)

        def plan_epilogue(tt, it):
            sl = slots[tt]
            ks = [None, None]
            ois = [None, None]

            def e1(sub):
                s = sl[sub]
                S.add("dve", lambda e: e.tensor_tensor(
                    out=xsd[:, s, :], in0=psXD[sub][:].rearrange("p a b -> p (a b)"), in1=xsd[:, s, :], op=ALU.add),
                    reads=[("psXD%d" % sub,), ("xsd", s)], writes=[("xsd", s)])

            def e1b(sub):
                s = sl[sub]
                k = ctr["ss"] % 8; ctr["ss"] += 1
                ks[sub] = k
                S.add("act", lambda e: e.activation(out=junk[:], in_=xsd[:, s, :], func=AF.Square,
                                                    scale=1.0 / 32.0, accum_out=ss[:, k:k + 1]),
                      reads=[("xsd", s)], writes=[("junk",), ("ss", k)])
                S.add("pool", lambda e: e.tensor_scalar(out=ss[:, k:k + 1], in0=ss[:, k:k + 1], scalar1=EPS,
                                                        scalar2=None, op0=ALU.add),
                      reads=[("ss", k)], writes=[("ss", k)])
                S.add("pool", lambda e: e.tensor_tensor(out=ss[:, k:k + 1], in0=ss[:, k:k + 1], in1=mhalf[:, 0:1],
                                                        op=ALU.pow),
                      reads=[("ss", k), ("mhalf",)], writes=[("ss", k)])

            def e2(sub):
                s, k = sl[sub], ks[sub]
                oi = c["ob"] % 2; c["ob"] += 1
                ois[sub] = oi
                S.add("act", lambda e: e.activation(out=ob[:, oi, :], in_=xsd[:, s, :], func=AF.Copy, scale=ss[:, k:k + 1]),
                      reads=[("xsd", s), ("ss", k)], writes=[("ob", oi)])

            def e3(sub):
                r = tt * 2 + sub
                oi = ois[sub]
                S.add("pool", lambda e: e.tensor_tensor(out=ob[:, oi, :], in0=ob[:, oi, :], in1=g_c[:], op=ALU.mult),
                      reads=[("ob", oi), ("g_c",)], writes=[("ob", oi)])
                dma("sp", out_d[r * 128:(r + 1) * 128, :], ob[:, oi, :], reads=[("ob", oi)], tag="out")
            e1(0)
            at(it + 1, lambda: e1(1))
            at(it + 1, lambda: e1b(0))
            at(it + 2, lambda: e1b(1))
            at(it + 3, lambda: e2(0))
            at(it + 4, lambda: e2(1))
            at(it + 5, lambda: e3(0))
            at(it + 6, lambda: e3(1))

        slots[0] = [0, 1]
        for k_, fn_ in enumerate(late_dmas):
            at(k_, fn_)
        q1, q2 = [], []
        it = 0
        for tt in range(8):
            if tt + 1 < 8:
                plan_prologue(tt + 1, it)
            for j in range(22):
                for fn in sched_at.pop(it, []):
                    fn()
                q1.append(produce(tt, j))
                if len(q1) > 1:
                    q2.append(mid(q1.pop(0)))
                if len(q2) > 1:
                    consume(q2.pop(0), it)
                it += 1
        while q1 or q2:
            for fn in sched_at.pop(it, []):
                fn()
            if q1:
                q2.append(mid(q1.pop(0)))
            if q2:
                consume(q2.pop(0), it)
            it += 1
        for it_ in sorted(sched_at):
            for fn in sched_at.pop(it_):
                fn()
        ps_close()

    if stop_after == "D":
        phase_D()

    outs = [o for o in S.dma_list if o.tag == "out"]
    S.add("sp", lambda e: e.nop(), deps=outs)
    S.emit()
    return nc


def _bias_layout(rel_bias):
    k = np.arange(128)[:, None, None]
    b = np.arange(5)[None, :, None]
    qi = np.arange(128)[None, None, :]
    rel = (4 - b) * 128 + qi - k
    idx = np.clip(rel, -128, 128) + 128
    g = rel_bias[:, idx]
    g = g.transpose(1, 0, 2, 3)
    near = g[:, :, [0, 3, 4], :].reshape(128, 8, 384)
    far = g[:, :, 1, 0:1]
    return np.ascontiguousarray(np.concatenate([near, far], axis=2).reshape(128, 8 * 385)).astype(np.float32)


def make_in_maps(inputs):
    f = lambda a: np.ascontiguousarray(np.asarray(a, dtype=np.float32))
    x = f(inputs["x"])[0]
    shared = {
        "mem": f(inputs["mem"])[0],
        "w_in": f(inputs["w_in"])[0],
        "bgate": f(f(inputs["b_gate"])[0].reshape(24, 128).T),
        "w_pool": f(inputs["w_pool"])[0],
        "pscale": f(f(inputs["pool_scale"])[0].reshape(2, 128).T),
        "biasT": _bias_layout(f(inputs["rel_bias"])[0]),
        "w_mem_kv": f(inputs["w_mem_kv"])[0],
        "w_up_pool": f(inputs["w_up_pool"])[0],
        "w_up_attn": f(inputs["w_up_attn"])[0],
        "w_up_mem": f(inputs["w_up_mem"])[0],
        "w_out": f(inputs["w_out"])[0],
        "w_ffn_up": f(inputs["w_ffn_up"])[0],
        "convw": f(f(inputs["conv_w"])[0].reshape(3, 44, 128).transpose(2, 1, 0).reshape(128, 132)),
        "convb": f(f(inputs["conv_b"])[0].reshape(44, 128).T),
        "w_ffn_down": f(inputs["w_ffn_down"])[0],
        "g_mix_b": f(np.broadcast_to(f(inputs["norm_mix_g"])[0], (128, D))),
        "g_mem_b": f(np.broadcast_to(f(inputs["norm_mem_g"])[0], (128, D))),
        "g_ffn_b": f(np.broadcast_to(f(inputs["norm_ffn_g"])[0], (128, D))),
        "g_fin_b": f(np.broadcast_to(f(inputs["norm_final_g"]), (128, D))),
        "ident": np.eye(128, dtype=np.float32),
    }
    maps = []
    wins = (2.0, 4.0, 8.0, 16.0)
    for c in range(NCORES):
        t0 = c * OWN
        xw = np.zeros((WIN, D), np.float32)
        lo = t0 - HALO
        src_lo = max(lo, 0)
        xw[src_lo - lo:] = x[src_lo:t0 + OWN]
        pos = lo + np.arange(WIN)
        valid = (pos >= 0).astype(np.float32).reshape(NT, 128).T
        inv = np.zeros((128, 2, 16), np.float32)
        for p in range(128):
            for t in range(2):
                w = wins[2 * t + (1 if p >= 64 else 0)]
                inv[p, t, :] = 1.0 / np.minimum(t0 + np.arange(16) + 1.0, w)
        m = dict(shared)
        m["xw"] = xw
        m["valid"] = f(valid)
        m["flag"] = np.full((128, 1), 0.0 if c == 0 else 1.0, np.float32)
        m["invcnt"] = f(inv.reshape(128, 32))
        maps.append(m)
    return maps


_NC_CACHE = {}


def kernel(**inputs):
    if "nc" not in _NC_CACHE:
        _NC_CACHE["nc"] = build()
    nc = _NC_CACHE["nc"]
    maps = make_in_maps(inputs)
    res = bu.run_bass_kernel_spmd(nc, maps, core_ids=list(range(NCORES)))
    out = np.concatenate([np.asarray(r["out"], dtype=np.float32) for r in res.results], axis=0)
    return out.reshape(1, SEQ, D)
```
